# Optimizing a Trainium2 kernel written in Bass

```python
import math
import jax, jax.numpy as jnp
from jax import lax
import numpy as np


D_MODEL = 2048
BATCH = 4
SEQ = 8192
DEPTH = 2

META = 16
CHUNK = 128
PAD = CHUNK - META
QBLK = 128
NEG = -1e30
EPS = 1e-6
MIX_WIDTH = D_MODEL // 2

SSD_HEAD_DIM = 64
SSD_INNER = MIX_WIDTH
SSD_HEADS = SSD_INNER // SSD_HEAD_DIM
SSD_GROUPS = 2
SSD_STATE = 128
CONV_W = 4
SSD_CONV_CH = SSD_INNER + 2 * SSD_GROUPS * SSD_STATE

ML_HEADS = 4
ML_V = MIX_WIDTH // ML_HEADS
ML_QK = ML_V // 2
ML_WIDTH = ML_HEADS * ML_V

DA_HEADS = 8
DA_V = MIX_WIDTH // DA_HEADS
DA_QK = DA_V // 2
DA_WIDTH = DA_HEADS * DA_V
ROT_DIM = DA_QK // 4
ROPE_THETA = 500000.0

PEER_HEADS = 8
PEER_TOPK = 16
N_KEYS = 128
N_EXPERTS = N_KEYS * N_KEYS
PK_HALF = 128
PEER_BLOCK = 128

IN_SIZES = (SSD_INNER, SSD_CONV_CH, SSD_HEADS,
            ML_HEADS * ML_QK, ML_HEADS * ML_QK, ML_WIDTH, ML_HEADS, ML_HEADS, ML_WIDTH,
            DA_HEADS * 2 * DA_QK, DA_HEADS * 2 * DA_QK, DA_WIDTH,
            3 * D_MODEL)
N_IN = sum(IN_SIZES)

kernel_name = 'hybrid_ssd_mlstm_diffattn_peer'


def rms_norm(x, w):
    xf = x.astype(jnp.float32)
    y = xf * lax.rsqrt(jnp.mean(xf * xf, axis=-1, keepdims=True) + EPS)
    return (y * w.astype(jnp.float32)).astype(x.dtype)


def pad_front(t, value=0.0):
    widths = [(0, 0), (PAD, 0)] + [(0, 0)] * (t.ndim - 2)
    return jnp.pad(t, widths, constant_values=value)


def rope_tables(pos):
    inv_freq = ROPE_THETA ** (-jnp.arange(0, ROT_DIM, 2, dtype=jnp.float32) / ROT_DIM)
    ang = pos.astype(jnp.float32)[..., None] * inv_freq
    return jnp.cos(ang), jnp.sin(ang)


def apply_partial_rope(t, cos, sin):
    tf = t.astype(jnp.float32)
    c = cos[:, :, None, None, :]
    s = sin[:, :, None, None, :]
    t1 = tf[..., :ROT_DIM // 2]
    t2 = tf[..., ROT_DIM // 2:ROT_DIM]
    out = jnp.concatenate([t1 * c - t2 * s, t2 * c + t1 * s, tf[..., ROT_DIM:]], axis=-1)
    return out.astype(t.dtype)


def causal_depthwise_conv(x, w, bias):
    y = lax.conv_general_dilated(x, w[:, None, :].astype(x.dtype), window_strides=(1,),
                                 padding=[(CONV_W - 1, 0)],
                                 dimension_numbers=('NWC', 'WIO', 'NWC'),
                                 feature_group_count=x.shape[-1])
    return y + bias.astype(x.dtype)


def ssd_chunked(x, log_a, bmat, cmat):
    b, lp, nh, p = x.shape
    g, n = bmat.shape[2], bmat.shape[3]
    e = nh // g
    c = lp // CHUNK
    x = x.reshape(b, c, CHUNK, g, e, p)
    bmat = bmat.reshape(b, c, CHUNK, g, n)
    cmat = cmat.reshape(b, c, CHUNK, g, n)
    a_cs = jnp.cumsum(log_a.reshape(b, c, CHUNK, g, e).transpose(0, 3, 4, 1, 2), axis=-1)
    causal = jnp.tril(jnp.ones((CHUNK, CHUNK), dtype=bool))
    decay = jnp.exp(jnp.where(causal, a_cs[..., :, None] - a_cs[..., None, :], -jnp.inf))
    cb = jnp.einsum('bclgn,bcsgn->bgcls', cmat, bmat)
    y_diag = jnp.einsum('bgecls,bcsgep->bclgep', cb[:, :, None] * decay, x)
    to_end = jnp.exp(a_cs[..., -1:] - a_cs)
    states = jnp.einsum('bclgn,bgecl,bclgep->cbgepn', bmat, to_end, x)
    chunk_decay = jnp.exp(a_cs[..., -1]).transpose(3, 0, 1, 2)

    def step(carry, inp):
        st, dec = inp
        return carry * dec[..., None, None] + st, carry

    init = jnp.zeros(states.shape[1:], states.dtype)
    _, prev = lax.scan(step, init, (states, chunk_decay))
    y_off = jnp.einsum('bclgn,cbgepn,bgecl->bclgep', cmat, prev, jnp.exp(a_cs))
    return (y_diag + y_off).reshape(b, lp, nh, p)


def mlstm_chunked(q, k, v, i_pre, log_f):
    b, lp, h, dk = q.shape
    dv = v.shape[-1]
    c = lp // CHUNK
    q = q.reshape(b, c, CHUNK, h, dk)
    k = k.reshape(b, c, CHUNK, h, dk)
    v = v.reshape(b, c, CHUNK, h, dv)
    i_pre = i_pre.reshape(b, c, CHUNK, h).transpose(0, 3, 1, 2)
    log_f = log_f.reshape(b, c, CHUNK, h).transpose(0, 3, 1, 2)
    f_cs = jnp.cumsum(log_f, axis=-1)
    f_tot = f_cs[..., -1]
    causal = jnp.tril(jnp.ones((CHUNK, CHUNK), dtype=bool))
    log_d = jnp.where(causal, f_cs[..., :, None] - f_cs[..., None, :] + i_pre[..., None, :], -jnp.inf)
    log_w = f_tot[..., None] - f_cs + i_pre
    m_loc = jnp.max(log_w, axis=-1)
    w = jnp.exp(log_w - m_loc[..., None])
    c_loc = jnp.einsum('bhcs,bcshv,bcshk->cbhvk', w, v, k)
    n_loc = jnp.einsum('bhcs,bcshk->cbhk', w, k)

    def step(carry, inp):
        c_prev, n_prev, m_prev = carry
        c_l, n_l, m_l, f_t = inp
        m_new = jnp.maximum(f_t + m_prev, m_l)
        s_prev = jnp.exp(f_t + m_prev - m_new)
        s_loc = jnp.exp(m_l - m_new)
        c_new = s_prev[..., None, None] * c_prev + s_loc[..., None, None] * c_l
        n_new = s_prev[..., None] * n_prev + s_loc[..., None] * n_l
        return (c_new, n_new, m_new), (c_prev, n_prev, m_prev)

    init = (jnp.zeros(c_loc.shape[1:], c_loc.dtype), jnp.zeros(n_loc.shape[1:], n_loc.dtype),
            jnp.full((b, h), NEG, jnp.float32))
    _, (c_prev, n_prev, m_prev) = lax.scan(
        step, init, (c_loc, n_loc, m_loc.transpose(2, 0, 1), f_tot.transpose(2, 0, 1)))
    log_inter = f_cs + m_prev.transpose(1, 2, 0)[..., None]
    m_t = jnp.maximum(jnp.max(log_d, axis=-1), log_inter)
    s = jnp.einsum('bclhk,bcshk->bhcls', q, k) * jnp.exp(log_d - m_t[..., None])
    inter = jnp.exp(log_inter - m_t)
    num = (jnp.einsum('bhcls,bcshv->bclhv', s, v)
           + jnp.einsum('bclhk,cbhvk->bclhv', q, c_prev) * inter.transpose(0, 2, 3, 1)[..., None])
    den = jnp.sum(s, axis=-1) + jnp.einsum('bclhk,cbhk->bhcl', q, n_prev) * inter
    den = jnp.maximum(jnp.abs(den), jnp.exp(-m_t)).transpose(0, 2, 3, 1)
    return (num / den[..., None]).reshape(b, lp, h, dv)


def diff_attention(q, k, v, lam):
    b, lp, h, _, dk = q.shape
    nblk = lp // QBLK
    q_blocks = q.reshape(b, nblk, QBLK, h, 2, dk).swapaxes(0, 1)
    key_pos = jnp.arange(lp)
    scale = dk ** -0.5

    def one_block(args):
        q_blk, blk = args
        q_pos = blk * QBLK + jnp.arange(QBLK)
        s = jnp.einsum('bqhmd,bkhmd->bhmqk', q_blk, k).astype(jnp.float32) * scale
        allowed = (key_pos[None, :] <= q_pos[:, None]) & (key_pos[None, :] >= PAD)
        p = jax.nn.softmax(jnp.where(allowed, s, NEG), axis=-1)
        a = p[:, :, 0] - lam * p[:, :, 1]
        return jnp.einsum('bhqk,bkhd->bqhd', a.astype(v.dtype), v)

    o = lax.map(one_block, (q_blocks, jnp.arange(nblk)))
    return o.swapaxes(0, 1).reshape(b, lp, h, -1)


def hybrid_mixer(h, cos, sin, lam_init, w_in, conv_w, conv_b, dt_bias, a_log, d_skip, ssd_norm_w,
                 i_bias, f_bias, ml_norm_w, lq1, lk1, lq2, lk2, diff_norm_w,
                 w_bs, w_bm, w_bd, w_out):
    b, L, _ = h.shape
    f32 = jnp.float32
    proj = h @ w_in
    (z, xbc, dt_raw, mq, mk, mv, mi, mf, mo, aq, ak, av, gates) = jnp.split(
        proj, np.cumsum(IN_SIZES)[:-1].tolist(), axis=-1)

    xbc = jax.nn.silu(causal_depthwise_conv(xbc, conv_w, conv_b))
    xs, bm, cm = jnp.split(xbc, [SSD_INNER, SSD_INNER + SSD_GROUPS * SSD_STATE], axis=-1)
    xs = xs.reshape(b, L, SSD_HEADS, SSD_HEAD_DIM)
    bm = bm.reshape(b, L, SSD_GROUPS, SSD_STATE)
    cm = cm.reshape(b, L, SSD_GROUPS, SSD_STATE)
    dt = jax.nn.softplus(dt_raw.astype(f32) + dt_bias.astype(f32))
    log_a = dt * -jnp.exp(a_log.astype(f32))
    y = ssd_chunked(pad_front(xs * dt[..., None]), pad_front(log_a), pad_front(bm), pad_front(cm))[:, PAD:]
    y = y + d_skip.astype(f32)[:, None] * xs
    y_ssd = rms_norm(y.reshape(b, L, SSD_INNER) * jax.nn.silu(z.astype(f32)), ssd_norm_w).astype(h.dtype)

    q = mq.reshape(b, L, ML_HEADS, ML_QK)
    k = mk.reshape(b, L, ML_HEADS, ML_QK) * ML_QK ** -0.5
    v = mv.reshape(b, L, ML_HEADS, ML_V)
    i_pre = mi.astype(f32) + i_bias.astype(f32)
    log_f = jax.nn.log_sigmoid(mf.astype(f32) + f_bias.astype(f32))
    ht = mlstm_chunked(pad_front(q), pad_front(k), pad_front(v), pad_front(i_pre, NEG), pad_front(log_f))[:, PAD:]
    ht = rms_norm(ht, ml_norm_w.reshape(ML_HEADS, ML_V))
    y_ml = (jax.nn.sigmoid(mo.astype(f32)) * ht.reshape(b, L, ML_WIDTH)).astype(h.dtype)

    qa = apply_partial_rope(aq.reshape(b, L, DA_HEADS, 2, DA_QK), cos, sin)
    ka = apply_partial_rope(ak.reshape(b, L, DA_HEADS, 2, DA_QK), cos, sin)
    va = av.reshape(b, L, DA_HEADS, DA_V)
    lam = (jnp.exp(jnp.sum(lq1.astype(f32) * lk1.astype(f32)))
           - jnp.exp(jnp.sum(lq2.astype(f32) * lk2.astype(f32))) + lam_init)
    o = diff_attention(pad_front(qa), pad_front(ka), pad_front(va), lam)[:, PAD:]
    y_da = (rms_norm(o, diff_norm_w) * (1.0 - lam_init)).reshape(b, L, DA_WIDTH).astype(h.dtype)

    g_s, g_m, g_d = jnp.split(jax.nn.sigmoid(gates), 3, axis=-1)
    merged = g_s * (y_ssd @ w_bs) + g_m * (y_ml @ w_bm) + g_d * (y_da @ w_bd)
    return merged @ w_out


def peer_ffn(h, w_q, sub_keys, u_tab, v_tab):
    b, L, d = h.shape
    q = (h @ w_q).reshape(b, L, PEER_HEADS, 2, PK_HALF)
    s = jnp.einsum('blhmd,mhnd->blhmn', q, sub_keys).astype(jnp.float32)
    top_s, top_i = lax.top_k(s, PEER_TOPK)
    cand_s = (top_s[..., 0, :, None] + top_s[..., 1, None, :]).reshape(b, L, PEER_HEADS, PEER_TOPK * PEER_TOPK)
    cand_i = (top_i[..., 0, :, None] * N_KEYS + top_i[..., 1, None, :]).reshape(b, L, PEER_HEADS, PEER_TOPK * PEER_TOPK)
    best_s, best_pos = lax.top_k(cand_s, PEER_TOPK)
    idx = jnp.take_along_axis(cand_i, best_pos, axis=-1)
    gate = jax.nn.softmax(best_s, axis=-1).astype(h.dtype)
    n_tok = b * L
    n_pad = (-n_tok) % PEER_BLOCK
    nb = (n_tok + n_pad) // PEER_BLOCK
    hk = PEER_HEADS * PEER_TOPK
    h_f = jnp.pad(h.reshape(n_tok, d), ((0, n_pad), (0, 0))).reshape(nb, PEER_BLOCK, d)
    i_f = jnp.pad(idx.reshape(n_tok, hk), ((0, n_pad), (0, 0))).reshape(nb, PEER_BLOCK, hk)
    g_f = jnp.pad(gate.reshape(n_tok, hk), ((0, n_pad), (0, 0))).reshape(nb, PEER_BLOCK, hk)

    def one_block(args):
        hb, ib, gb = args
        act = jax.nn.gelu(jnp.einsum('td,tkd->tk', hb, u_tab[ib]), approximate=False)
        return jnp.einsum('tk,tkd->td', gb * act, v_tab[ib])

    y = lax.map(one_block, (h_f, i_f, g_f)).reshape(nb * PEER_BLOCK, d)[:n_tok]
    return y.reshape(b, L, d).astype(h.dtype)


def setup_inputs(seed: int = 0) -> dict:
    key = jax.random.key(seed)
    ks = iter(jax.random.split(key, 40))

    def nrm(shape, scale):
        return jax.random.normal(next(ks), shape, jnp.float32) * scale

    def gain(shape):
        return 1.0 + nrm(shape, 0.02)

    x = nrm((BATCH, SEQ, D_MODEL), 1.0)
    offsets = jax.random.randint(next(ks), (BATCH, 1), 0, 4096, dtype=jnp.int32)
    positions = offsets + jnp.arange(SEQ, dtype=jnp.int32)[None, :]
    meta_tokens = nrm((META, D_MODEL), 1.0)
    mix_norm_w = gain((DEPTH, D_MODEL))
    w_in = nrm((DEPTH, D_MODEL, N_IN), D_MODEL ** -0.5)
    ssd_conv_w = nrm((DEPTH, CONV_W, SSD_CONV_CH), CONV_W ** -0.5)
    ssd_conv_b = nrm((DEPTH, SSD_CONV_CH), 0.02)
    dt0 = jnp.exp(jax.random.uniform(next(ks), (DEPTH, SSD_HEADS), jnp.float32, math.log(1e-3), math.log(1e-1)))
    ssd_dt_bias = dt0 + jnp.log(-jnp.expm1(-dt0))
    ssd_a_log = jnp.log(jax.random.uniform(next(ks), (DEPTH, SSD_HEADS), jnp.float32, 1.0, 16.0))
    ssd_d = 1.0 + nrm((DEPTH, SSD_HEADS), 0.1)
    ssd_norm_w = gain((DEPTH, SSD_INNER))
    mlstm_i_bias = nrm((DEPTH, ML_HEADS), 0.1)
    mlstm_f_bias = 3.0 + jax.random.uniform(next(ks), (DEPTH, ML_HEADS), jnp.float32, 0.0, 3.0)
    mlstm_norm_w = gain((DEPTH, ML_WIDTH))
    diff_lambda_q1 = nrm((DEPTH, DA_QK), 0.1)
    diff_lambda_k1 = nrm((DEPTH, DA_QK), 0.1)
    diff_lambda_q2 = nrm((DEPTH, DA_QK), 0.1)
    diff_lambda_k2 = nrm((DEPTH, DA_QK), 0.1)
    diff_norm_w = gain((DEPTH, DA_V))
    w_branch_ssd = nrm((DEPTH, SSD_INNER, D_MODEL), SSD_INNER ** -0.5)
    w_branch_mlstm = nrm((DEPTH, ML_WIDTH, D_MODEL), ML_WIDTH ** -0.5)
    w_branch_diff = nrm((DEPTH, DA_WIDTH, D_MODEL), DA_WIDTH ** -0.5)
    w_out = nrm((DEPTH, D_MODEL, D_MODEL), D_MODEL ** -0.5)
    ffn_norm_w = gain((DEPTH, D_MODEL))
    peer_w_q = nrm((DEPTH, D_MODEL, PEER_HEADS * 2 * PK_HALF), D_MODEL ** -0.5)
    peer_sub_keys = nrm((DEPTH, 2, PEER_HEADS, N_KEYS, PK_HALF), PK_HALF ** -0.5)
    peer_u = nrm((DEPTH, N_EXPERTS, D_MODEL), D_MODEL ** -0.5)
    peer_v = nrm((DEPTH, N_EXPERTS, D_MODEL), (PEER_HEADS * PEER_TOPK) ** -0.5)
    final_norm_w = gain((D_MODEL,))
    return {'x': x, 'positions': positions, 'meta_tokens': meta_tokens, 'mix_norm_w': mix_norm_w,
            'w_in': w_in, 'ssd_conv_w': ssd_conv_w, 'ssd_conv_b': ssd_conv_b, 'ssd_dt_bias': ssd_dt_bias,
            'ssd_a_log': ssd_a_log, 'ssd_d': ssd_d, 'ssd_norm_w': ssd_norm_w,
            'mlstm_i_bias': mlstm_i_bias, 'mlstm_f_bias': mlstm_f_bias, 'mlstm_norm_w': mlstm_norm_w,
            'diff_lambda_q1': diff_lambda_q1, 'diff_lambda_k1': diff_lambda_k1,
            'diff_lambda_q2': diff_lambda_q2, 'diff_lambda_k2': diff_lambda_k2, 'diff_norm_w': diff_norm_w,
            'w_branch_ssd': w_branch_ssd, 'w_branch_mlstm': w_branch_mlstm, 'w_branch_diff': w_branch_diff,
            'w_out': w_out, 'ffn_norm_w': ffn_norm_w, 'peer_w_q': peer_w_q, 'peer_sub_keys': peer_sub_keys,
            'peer_u': peer_u, 'peer_v': peer_v, 'final_norm_w': final_norm_w}


def reference(x, positions, meta_tokens, mix_norm_w, w_in, ssd_conv_w, ssd_conv_b, ssd_dt_bias,
              ssd_a_log, ssd_d, ssd_norm_w, mlstm_i_bias, mlstm_f_bias, mlstm_norm_w,
              diff_lambda_q1, diff_lambda_k1, diff_lambda_q2, diff_lambda_k2, diff_norm_w,
              w_branch_ssd, w_branch_mlstm, w_branch_diff, w_out, ffn_norm_w, peer_w_q,
              peer_sub_keys, peer_u, peer_v, final_norm_w):
    b = x.shape[0]
    h = jnp.concatenate([jnp.broadcast_to(meta_tokens.astype(x.dtype)[None], (b, META, D_MODEL)), x], axis=1)
    pos = jnp.concatenate([jnp.broadcast_to(jnp.arange(META, dtype=jnp.int32)[None], (b, META)),
                           positions + META], axis=1)
    cos, sin = rope_tables(pos)
    for l in range(DEPTH):
        lam_init = 0.8 - 0.6 * math.exp(-0.3 * l)
        h = h + hybrid_mixer(rms_norm(h, mix_norm_w[l]), cos, sin, lam_init, w_in[l],
                             ssd_conv_w[l], ssd_conv_b[l], ssd_dt_bias[l], ssd_a_log[l], ssd_d[l],
                             ssd_norm_w[l], mlstm_i_bias[l], mlstm_f_bias[l], mlstm_norm_w[l],
                             diff_lambda_q1[l], diff_lambda_k1[l], diff_lambda_q2[l], diff_lambda_k2[l],
                             diff_norm_w[l], w_branch_ssd[l], w_branch_mlstm[l], w_branch_diff[l], w_out[l])
        h = h + peer_ffn(rms_norm(h, ffn_norm_w[l]), peer_w_q[l], peer_sub_keys[l], peer_u[l], peer_v[l])
    return rms_norm(h, final_norm_w)[:, META:]
```

```python
import math
from contextlib import ExitStack
import numpy as np
import ml_dtypes
import concourse.bass as bass
import concourse.mybir as mybir
from concourse.bass_utils import run_bass_kernel_spmd

F32 = mybir.dt.float32
BF16 = mybir.dt.bfloat16
I32 = mybir.dt.int32
U32 = mybir.dt.uint32
AF = mybir.ActivationFunctionType
ALU = mybir.AluOpType
AX = mybir.AxisListType

D = 2048
KC = 16
NIN = 14872
OZ, OXBC, ODT, OMQ, OMK, OMV, OMI, OMF, OMO, OAQ, OAK, OAV, OG = (
    0, 1024, 2560, 2576, 3088, 3600, 4624, 4628, 4632, 5656, 6680, 7704, 8728)
EPS = 1e-6
NEGM = -30000.0
META = 16
PADN = 112
NDS = 24


class Sched:
    def __init__(self, nc):
        self.nc = nc
        self.eng = {'pe': nc.tensor, 'act': nc.scalar, 'dve': nc.vector, 'pool': nc.gpsimd, 'sp': nc.sync}
        self.sem = {k: nc.alloc_semaphore('sem_' + k) for k in self.eng}
        self.cnt = {k: 0 for k in self.eng}
        self.seen = {k: {} for k in self.eng}
        self.dsem = [nc.alloc_semaphore('dsem%d' % i) for i in range(NDS)]
        self.dcnt = [0] * NDS
        self.dnext = 0
        self.bufs = {}
        self.nins = 0

    def _deps(self, reads, writes):
        deps = {}
        for k in reads:
            b = self.bufs.get(k)
            if b and b['w']:
                s, v = b['w']
                deps[s] = max(deps.get(s, 0), v)
        for k in writes:
            b = self.bufs.get(k)
            if b:
                if b['w']:
                    s, v = b['w']
                    deps[s] = max(deps.get(s, 0), v)
                for s, v in b['r'].items():
                    deps[s] = max(deps.get(s, 0), v)
        return deps

    def _wait(self, e, deps):
        for src, val in deps.items():
            if e == 'pe' and src == 'pe':
                continue
            if self.seen[e].get(src, 0) >= val:
                continue
            sem = self.sem[src] if isinstance(src, str) else self.dsem[src[1]]
            self.eng[e].wait_ge(sem, val)
            self.seen[e][src] = val

    def _mark(self, reads, writes, src, val):
        for k in writes:
            self.bufs[k] = {'w': (src, val), 'r': {}}
        for k in reads:
            b = self.bufs.setdefault(k, {'w': None, 'r': {}})
            b['r'][src] = max(b['r'].get(src, 0), val)

    def op(self, e, fn, reads=(), writes=()):
        self._wait(e, self._deps(reads, writes))
        ins = fn(self.eng[e])
        self.cnt[e] += 1
        ins.then_inc(self.sem[e], 1)
        self._mark(reads, writes, e, self.cnt[e])
        self.nins += 1

    def dma(self, q, fn, reads=(), writes=()):
        deps = self._deps(reads, writes)
        i = self.dnext
        self.dnext = (i + 1) % NDS
        if self.dcnt[i] > 0:
            deps[('d', i)] = max(deps.get(('d', i), 0), 16 * self.dcnt[i])
        self._wait(q, deps)
        ins = fn(self.eng[q])
        self.dcnt[i] += 1
        ins.then_inc(self.dsem[i], 16)
        self._mark(reads, writes, ('d', i), 16 * self.dcnt[i])
        self.nins += 1

    def barrier(self):
        for e in self.eng:
            deps = {}
            for k in self.eng:
                if k != e and self.cnt[k]:
                    deps[k] = self.cnt[k]
            for i in range(NDS):
                if self.dcnt[i]:
                    deps[('d', i)] = 16 * self.dcnt[i]
            if self.cnt[e]:
                deps[e] = self.cnt[e]
            self._wait(e, deps)

    def finish(self):
        sp = self.eng['sp']
        for i in range(NDS):
            if self.dcnt[i]:
                sp.wait_ge(self.dsem[i], 16 * self.dcnt[i])
        for k in self.eng:
            if self.cnt[k]:
                sp.wait_ge(self.sem[k], self.cnt[k])


def host_consts():
    c = {}
    c['ident_bf'] = np.eye(128, dtype=np.float32).astype(ml_dtypes.bfloat16)
    c['ident_f'] = np.eye(128, dtype=np.float32)
    r = np.arange(128)
    c['utri'] = (r[:, None] <= r[None, :]).astype(np.float32)
    c['ones_f'] = np.ones((128, 128), np.float32)
    c['ones_bf'] = np.ones((128, 128), np.float32).astype(ml_dtypes.bfloat16)
    nm = np.where(r[None, :] >= r[:, None], 0.0, NEGM).astype(np.float32)
    c['negmask'] = nm
    c['negmask8'] = np.tile(nm, (1, 8)).astype(np.float32)
    cm = np.zeros((4, 128, 512), np.float32)
    for v in range(4):
        q = np.arange(512)[None, :]
        k = (v * 128 + r)[:, None]
        cm[v] = (q >= k).astype(np.float32)
    c['cmask'] = cm.astype(ml_dtypes.bfloat16)
    inv = (500000.0 ** (-np.arange(0, 16, 2, dtype=np.float32) / 16)).astype(np.float32)
    c['invfreq'] = np.tile(inv[None, :], (128, 1)).astype(np.float32)
    c['iota16'] = np.tile(np.arange(16, dtype=np.float32)[None, :], (128, 1))
    return c


CONST_SPECS = {
    'ident_bf': ([128, 128], BF16), 'ident_f': ([128, 128], F32), 'utri': ([128, 128], F32),
    'ones_f': ([128, 128], F32), 'ones_bf': ([128, 128], BF16), 'negmask': ([128, 128], F32),
    'negmask8': ([128, 1024], F32), 'cmask': ([4, 128, 512], BF16), 'invfreq': ([128, 8], F32),
    'iota16': ([128, 16], F32),
}

W_SPECS = {
    'mix_norm_w': [D], 'w_in': [D, NIN], 'ssd_conv_w': [4, 1536], 'ssd_conv_b': [1536],
    'ssd_dt_bias': [16], 'ssd_a_log': [16], 'ssd_d': [16], 'ssd_norm_w': [1024],
    'mlstm_i_bias': [4], 'mlstm_f_bias': [4], 'mlstm_norm_w': [1024],
    'diff_lambda_q1': [64], 'diff_lambda_k1': [64], 'diff_lambda_q2': [64], 'diff_lambda_k2': [64],
    'diff_norm_w': [128], 'w_branch_ssd': [1024, D], 'w_branch_mlstm': [1024, D],
    'w_branch_diff': [1024, D], 'w_out': [D, D], 'ffn_norm_w': [D], 'peer_w_q': [D, D],
    'peer_sub_keys': [2, 8, 128, 128], 'peer_u': [16384, D], 'peer_v': [16384, D],
}


class K:
    pass


def build(layers, NCH, NCC, dbg=False, do_b=True, do_c=True, stages=None):
    nc = bass.Bass("TRN2", target_bir_lowering=False)
    S = Sched(nc)
    T = NCH * 128
    g = K()
    g.nc, g.S, g.T, g.NCH, g.dbg = nc, S, T, NCH, dbg

    def din(name, shape, dt=F32):
        return nc.dram_tensor(name, shape, dt, kind="ExternalInput").ap()

    xin = din('xpad', [T, D])
    g.pos = din('pos', [128, NCH], I32)
    rows_half = din('rows', [128, NCC], I32)
    rmask_half = din('rowmask', [128, NCC])
    rows_full = din('rows_full', [128, NCH], I32)
    rmask_full = din('rowmask_full', [128, NCH])
    wl = {}
    g.in_names = set()
    for l in layers:
        wl[l] = {}
        for n, s in W_SPECS.items():
            if do_c or not n.startswith('peer_'):
                wl[l][n] = din('%s_l%d' % (n, l), s)
                g.in_names.add((n, l))
    g.fnw = din('final_norm_w', [D])
    g.c = {n: din(n, s, dt) for n, (s, dt) in CONST_SPECS.items()}
    xout = nc.dram_tensor('xout', [NCC * 128, D], F32, kind="ExternalOutput").ap()
    sk = "ExternalOutput" if dbg else "Internal"

    def dscr(name, shape, dt, kind=None):
        return nc.dram_tensor(name, shape, dt, kind=kind or sk).ap()

    g.xbct = dscr('s_xbct', [1536, T], BF16)
    g.xstok = dscr('s_xstok', [T, 1280], BF16)
    g.zs = dscr('s_zs', [T, 1024], BF16)
    g.dtla = dscr('s_dtla', [T, 32], F32)
    g.qtml = dscr('s_qtml', [512, T], BF16)
    g.ktml = dscr('s_ktml', [512, T], BF16)
    g.ktok = dscr('s_ktok', [T, 512], BF16)
    g.vml = dscr('s_vml', [T, 1024], BF16)
    g.og = dscr('s_og', [T, 1024], BF16)
    g.gates = dscr('s_gates', [8, T], F32)
    g.qa = dscr('s_qa', [16, 66, T], BF16)
    g.ka = dscr('s_ka', [16, 66, T], BF16)
    g.vda = dscr('s_vda', [T, 1024], BF16)
    g.yall = dscr('s_yall', [T, 3072], F32)
    xmid = [dscr('s_xmid%d' % i, [T, D], F32, kind="Internal") for i in range(max(0, len(layers) - 1))]
    if dbg:
        g.dbg_hmix = dscr('s_hmix', [T, D], F32)

    g.ps = [nc.alloc_psum_tensor('ps%d' % i, [128, 512], F32) for i in range(6)]
    g.pb = [nc.alloc_psum_tensor('pb%d' % i, [128, 1024], BF16) for i in range(2)]

    g.cs = {}
    for n, (s, dt) in CONST_SPECS.items():
        if n == 'cmask':
            t = nc.alloc_sbuf_tensor('c_' + n, [128, 4, 512], dt)
            S.dma('sp', lambda e, t=t, n=n: e.dma_start(out=t[:], in_=g.c[n].rearrange("v p q -> p v q")), writes=['c_' + n])
        else:
            t = nc.alloc_sbuf_tensor('c_' + n, s, dt)
            S.dma('sp', lambda e, t=t, n=n: e.dma_start(out=t[:], in_=g.c[n]), writes=['c_' + n])
        g.cs[n] = t

    CASTW = ('w_in', 'w_branch_ssd', 'w_branch_mlstm', 'w_branch_diff', 'w_out', 'peer_w_q')
    wbl = {}
    for l in layers:
        wbl[l] = {}
        for n in CASTW:
            if n not in wl[l]:
                continue
            shp = W_SPECS[n]
            wb_ap = nc.dram_tensor('wb_%s_l%d' % (n, l), shp, BF16, kind="Internal").ap()
            wbl[l][n] = wb_ap
            for r0 in range(0, shp[0], 256):
                S.dma('pool', lambda e, wb_ap=wb_ap, n=n, l=l, r0=r0: e.dma_start(out=wb_ap[r0:r0 + 256, :], in_=wl[l][n][r0:r0 + 256, :]),
                      reads=['d_wb_%s_%d' % (n, l)] if r0 else [], writes=['d_wb_%s_%d' % (n, l)])

    uvl = {}
    if do_c:
        for l in layers:
            uv = nc.dram_tensor('uvb_l%d' % l, [16384, 2 * D], BF16, kind="Internal").ap()
            uvl[l] = uv
            first = True
            for half, n in enumerate(('peer_u', 'peer_v')):
                for r0 in range(0, 16384, 2048):
                    S.dma('pool', lambda e, uv=uv, n=n, l=l, r0=r0, half=half: e.dma_start(out=uv[r0:r0 + 2048, half * D:(half + 1) * D], in_=wl[l][n][r0:r0 + 2048, :]),
                          reads=[] if first else ['d_uvb_%d' % l], writes=['d_uvb_%d' % l])
                    first = False

    uid = [0]

    def run_stage(fn):
        with ExitStack() as es:
            uid[0] += 1
            g.A = lambda name, shape, dt, u=uid[0]: es.enter_context(nc.sbuf_tensor('%s_u%d' % (name, u), shape, dt))
            fn(g)
            S.barrier()

    for li, l in enumerate(layers):
        last = (li == len(layers) - 1)
        g.w = wl[l]
        g.wb = wbl[l]
        g.uvb = uvl.get(l)
        g.lid = l
        g.lam_init = 0.8 - 0.6 * math.exp(-0.3 * l)
        g.x = xin if li == 0 else xmid[li - 1]
        g.xkey = 'd_x%d' % li
        g.final = last and (l == 1)
        if last:
            g.NCC, g.rows, g.rowmask, g.out, g.outkey = NCC, rows_half, rmask_half, xout, 'd_out'
        else:
            g.NCC, g.rows, g.rowmask, g.out, g.outkey = NCH, rows_full, rmask_full, xmid[li], 'd_x%d' % (li + 1)
        if do_b:
            for fn in (stage_p, stage_ssd, stage_mlstm, stage_attn):
                if stages is None or fn.__name__ in stages:
                    run_stage(fn)
        if do_c:
            run_stage(phase_c)
    S.finish()
    return nc, g


def bc_load(g, name, src_ap, n, dt=F32, q='sp'):
    t = g.A(name, [128, n], dt)
    g.S.dma(q, lambda e: e.dma_start(out=t[:], in_=src_ap.partition_broadcast(128)), writes=[name])
    return t


def rmsnorm_T(g, pfx, x_tile, xkey, nw_bc, nwkey, ub, junk, st, uT, uTkey, col0, want_f32=None, jkey=None, stkey=None):
    S = g.S
    ss, rs = st[:, 0:1], st[:, 1:2]
    S.op('act', lambda e: e.activation(out=junk[:], in_=x_tile, func=AF.Square, accum_out=ss),
         reads=[xkey], writes=[jkey or (pfx + 'junk'), (stkey or pfx) + 'ss'])
    S.op('dve', lambda e: e.tensor_scalar(out=rs, in0=ss, scalar1=1.0 / D, scalar2=EPS, op0=ALU.mult, op1=ALU.add),
         reads=[(stkey or pfx) + 'ss'], writes=[(stkey or pfx) + 'rs'])
    S.op('act', lambda e: e.activation(out=rs, in_=rs, func=AF.Sqrt), reads=[(stkey or pfx) + 'rs'], writes=[(stkey or pfx) + 'rs'])
    S.op('dve', lambda e: e.reciprocal(out=rs, in_=rs), reads=[(stkey or pfx) + 'rs'], writes=[(stkey or pfx) + 'rs'])
    if want_f32 is not None:
        f_ap, fkey = want_f32
        S.op('dve', lambda e: e.scalar_tensor_tensor(out=f_ap, in0=x_tile, scalar=rs, in1=nw_bc[:], op0=ALU.mult, op1=ALU.mult),
             reads=[xkey, (stkey or pfx) + 'rs', nwkey], writes=[fkey])
        S.op('act', lambda e: e.copy(out=ub[:], in_=f_ap), reads=[fkey], writes=[pfx + 'ub'])
    else:
        S.op('dve', lambda e: e.scalar_tensor_tensor(out=ub[:], in0=x_tile, scalar=rs, in1=nw_bc[:], op0=ALU.mult, op1=ALU.mult),
             reads=[xkey, (stkey or pfx) + 'rs', nwkey], writes=[pfx + 'ub'])
    transpose_to(g, ub, pfx + 'ub', 16, uT, uTkey, 0, col0)


def transpose_to(g, src, srckey, ntile, dstT, dstkey, kc0, col0):
    S = g.S
    for h in range(0, ntile, 8):
        n = min(8, ntile - h)
        pb = g.pb[(h // 8) % 2]
        pbk = 'pb%d' % ((h // 8) % 2)
        for i in range(n):
            S.op('pe', lambda e, i=i: e.transpose(out=pb[:, i * 128:(i + 1) * 128], in_=src[:, (h + i) * 128:(h + i + 1) * 128],
                                                   identity=g.cs['ident_bf'][:]),
                 reads=[srckey, 'c_ident_bf'], writes=[pbk])
        eng = 'act' if (h // 8) % 2 == 0 else 'dve'
        o = dstT[:, kc0 + h:kc0 + h + n, col0:col0 + 128]
        i_ = pb[:, 0:n * 128].rearrange("p (a b) -> p a b", b=128)
        if eng == 'act':
            S.op('act', lambda e: e.copy(out=o, in_=i_), reads=[pbk], writes=[dstkey])
        else:
            S.op('dve', lambda e: e.tensor_copy(out=o, in_=i_), reads=[pbk], writes=[dstkey])


def load_w(g, wt, wkey, src, r0, nk, c0, ncols):
    ap = g.wb[src][r0:r0 + nk * 128, c0:c0 + ncols].rearrange("(kc p) c -> p kc c", p=128)
    g.S.dma('sp', lambda e: e.dma_start(out=wt[:, 0:nk, 0:ncols], in_=ap), reads=['d_wb_%s_%d' % (src, g.lid)], writes=[wkey])


def stage_p(g):
    nc, S, T, NCH = g.nc, g.S, g.T, g.NCH
    cs = g.cs
    A = g.A
    w_in = 'w_in'
    xst = A('p_xst', [128, 1280], BF16)
    TB = 512
    nw_bc = bc_load(g, 'p_nw', g.w['mix_norm_w'], D)
    xt = [A('p_xt%d' % i, [128, D], F32) for i in range(2)]
    ub = A('p_ub', [128, D], BF16)
    junk = A('p_junk', [128, D], BF16)
    st = A('p_st', [128, 2], F32)
    uT = A('p_uT', [128, KC, TB], BF16)
    wt = [A('p_wt%d' % i, [128, KC, 512], BF16) for i in range(2)]
    ev = [A('p_ev%d' % i, [128, 1024], F32) for i in range(2)]
    eb = [A('p_eb%d' % i, [128, 1024], BF16) for i in range(2)]
    cw = A('p_cw', [128, 4, 12], F32)
    cb = A('p_cb', [128, 12], F32)
    for j_ in range(4):
        S.dma('sp', lambda e, j_=j_: e.dma_start(out=cw[:, j_, :], in_=g.w['ssd_conv_w'][j_, :].rearrange("(t p) -> p t", p=128),
                                                 allow_slow_non_contiguous=True), reads=['p_cw'] if j_ else [], writes=['p_cw'])
    S.dma('sp', lambda e: e.dma_start(out=cb[:], in_=g.w['ssd_conv_b'].rearrange("(t p) -> p t", p=128),
                                      allow_slow_non_contiguous=True), writes=['p_cb'])
    xraw = A('p_xraw', [128, 12, 3 + TB], F32)
    S.op('pool', lambda e: e.memset(xraw[:], 0.0), writes=['p_xraw'])
    xact = A('p_xact', [128, 12, TB], BF16)
    cacc = A('p_cacc', [128, TB], F32)
    dtb = bc_load(g, 'p_dtb', g.w['ssd_dt_bias'], 16)
    alog = bc_load(g, 'p_alog', g.w['ssd_a_log'], 16)
    negA = A('p_negA', [128, 16], F32)
    S.op('act', lambda e: e.activation(out=negA[:], in_=alog[:], func=AF.Exp), reads=['p_alog'], writes=['p_negA'])
    S.op('dve', lambda e: e.tensor_scalar(out=negA[:], in0=negA[:], scalar1=-1.0, scalar2=None, op0=ALU.mult),
         reads=['p_negA'], writes=['p_negA'])
    dts = A('p_dts', [128, 4, 16], F32)
    ib = A('p_ib', [4, 1], F32)
    fb = A('p_fb', [4, 1], F32)
    S.dma('sp', lambda e: e.dma_start(out=ib[:], in_=g.w['mlstm_i_bias'].rearrange("(p o) -> p o", o=1)), writes=['p_ib'])
    S.dma('sp', lambda e: e.dma_start(out=fb[:], in_=g.w['mlstm_f_bias'].rearrange("(p o) -> p o", o=1)), writes=['p_fb'])
    S.op('dve', lambda e: e.tensor_scalar(out=fb[:], in0=fb[:], scalar1=-1.0, scalar2=None, op0=ALU.mult),
         reads=['p_fb'], writes=['p_fb'])
    gt = A('p_gt', [4, 2, TB], F32)
    posi = A('p_posi', [128, NCH], I32)
    S.dma('sp', lambda e: e.dma_start(out=posi[:], in_=g.pos), writes=['p_posi'])
    posf = A('p_posf', [128, NCH], F32)
    S.op('dve', lambda e: e.tensor_copy(out=posf[:], in_=posi[:]), reads=['p_posi'], writes=['p_posf'])
    S.op('dve', lambda e: e.tensor_scalar(out=posf[:], in0=posf[:], scalar1=float(META), scalar2=None, op0=ALU.add),
         reads=['p_posf'], writes=['p_posf'])
    ang = A('p_ang', [128, NCH, 8], F32)
    S.op('dve', lambda e: e.tensor_tensor(out=ang[:], in0=posf[:].unsqueeze(2).to_broadcast([128, NCH, 8]),
                                           in1=cs['invfreq'][:].unsqueeze(1).to_broadcast([128, NCH, 8]), op=ALU.mult),
         reads=['p_posf', 'c_invfreq'], writes=['p_ang'])
    cosT = A('p_cos', [128, NCH, 8], F32)
    sinT = A('p_sin', [128, NCH, 8], F32)
    rr = A('p_rr', [128, NCH, 8], F32)
    rk = A('p_rk', [128, NCH, 8], F32)
    rki = A('p_rki', [128, NCH, 8], I32)
    TWO_PI = 2.0 * math.pi
    C1 = 6.28125
    C2 = TWO_PI - C1
    for shift, dst, dk in ((0.0, sinT, 'p_sin'), (math.pi / 2, cosT, 'p_cos')):
        S.op('dve', lambda e, shift=shift: e.tensor_scalar(out=rr[:], in0=ang[:], scalar1=shift, scalar2=None, op0=ALU.add),
             reads=['p_ang'], writes=['p_rr'])
        S.op('dve', lambda e: e.tensor_scalar(out=rk[:], in0=rr[:], scalar1=1.0 / TWO_PI, scalar2=None, op0=ALU.mult),
             reads=['p_rr'], writes=['p_rk'])
        S.op('dve', lambda e: e.tensor_copy(out=rki[:], in_=rk[:]), reads=['p_rk'], writes=['p_rki'])
        S.op('dve', lambda e: e.tensor_copy(out=rk[:], in_=rki[:]), reads=['p_rki'], writes=['p_rk'])
        S.op('dve', lambda e: e.scalar_tensor_tensor(out=rr[:], in0=rk[:], scalar=-C1, in1=rr[:], op0=ALU.mult, op1=ALU.add),
             reads=['p_rk', 'p_rr'], writes=['p_rr'])
        S.op('dve', lambda e: e.scalar_tensor_tensor(out=rr[:], in0=rk[:], scalar=-C2, in1=rr[:], op0=ALU.mult, op1=ALU.add),
             reads=['p_rk', 'p_rr'], writes=['p_rr'])
        S.op('dve', lambda e: e.tensor_scalar(out=rk[:], in0=rr[:], scalar1=math.pi, scalar2=-TWO_PI, op0=ALU.is_gt, op1=ALU.mult),
             reads=['p_rr'], writes=['p_rk'])
        S.op('dve', lambda e: e.tensor_tensor(out=rr[:], in0=rr[:], in1=rk[:], op=ALU.add), reads=['p_rr', 'p_rk'], writes=['p_rr'])
        S.op('dve', lambda e: e.tensor_scalar(out=rk[:], in0=rr[:], scalar1=-math.pi, scalar2=TWO_PI, op0=ALU.is_lt, op1=ALU.mult),
             reads=['p_rr'], writes=['p_rk'])
        S.op('dve', lambda e: e.tensor_tensor(out=rr[:], in0=rr[:], in1=rk[:], op=ALU.add), reads=['p_rr', 'p_rk'], writes=['p_rr'])
        S.op('dve', lambda e: e.tensor_scalar(out=rr[:], in0=rr[:], scalar1=3.14159, scalar2=-3.14159, op0=ALU.min, op1=ALU.max),
             reads=['p_rr'], writes=['p_rr'])
        S.op('act', lambda e, dst=dst: e.activation(out=dst[:], in_=rr[:], func=AF.Sin), reads=['p_rr'], writes=[dk])
    qk = A('p_qk', [128, 16, 66], F32)
    qkb = A('p_qkb', [128, 16, 66], BF16)
    r1 = A('p_r1', [128, 16, 8], F32)
    r2 = A('p_r2', [128, 16, 8], F32)
    r3 = A('p_r3', [128, 16, 8], F32)
    kn2 = A('p_kn2', [128, 16], F32)
    kmaxc = A('p_kmaxc', [128, 16], F32)
    S.op('pool', lambda e: e.memset(kmaxc[:], 0.0), writes=['p_kmaxc'])
    padk = A('p_padk', [128, 1], F32)
    S.op('pool', lambda e: e.memset(padk[:], 0.0), writes=['p_padk'])
    S.op('pool', lambda e: e.memset(padk[0:96, :], NEGM), reads=['p_padk'], writes=['p_padk'])
    S.op('pool', lambda e: e.memset(padk[96:112, :], NEGM), reads=['p_padk'], writes=['p_padk'])
    qaT = A('p_qaT', [66, 16, TB], BF16)

    nblk = (NCH + 3) // 4
    for bi in range(nblk):
        c0 = bi * 4
        ncb = min(4, NCH - c0)
        tb = ncb * 128
        t0 = c0 * 128
        for j in range(ncb):
            xtile = xt[j % 2]
            xkey = 'p_xt%d' % (j % 2)
            S.dma('sp', lambda e, xtile=xtile, j=j: e.dma_start(out=xtile[:], in_=g.x[t0 + j * 128:t0 + (j + 1) * 128, :]), reads=[g.xkey], writes=[xkey])
            rmsnorm_T(g, 'p_', xtile[:], xkey, nw_bc, 'p_nw', ub, junk, st, uT, 'p_uT', j * 128)
        wi = [0]

        def next_w(c0_, ncols):
            i = wi[0] % 2
            wi[0] += 1
            load_w(g, wt[i], 'p_wt%d' % i, w_in, 0, KC, c0_, ncols)
            return wt[i], 'p_wt%d' % i

        pidx = [0]

        def next_ps():
            i = pidx[0] % 6
            pidx[0] += 1
            return g.ps[i], 'ps%d' % i

        def proj_F(w, wkey, wc0, ncol):
            ps, pk = next_ps()
            for kc in range(KC):
                S.op('pe', lambda e, kc=kc: e.matmul(ps[0:ncol, 0:tb], lhsT=w[:, kc, wc0:wc0 + ncol], rhs=uT[:, kc, 0:tb],
                                                     start=(kc == 0), stop=(kc == KC - 1)),
                     reads=[wkey, 'p_uT'], writes=[pk])
            return ps, pk

        def proj_T(w, wkey, j, ncol):
            ps, pk = next_ps()
            for kc in range(KC):
                S.op('pe', lambda e, kc=kc: e.matmul(ps[:, 0:ncol], lhsT=uT[:, kc, j * 128:(j + 1) * 128], rhs=w[:, kc, 0:ncol],
                                                     start=(kc == 0), stop=(kc == KC - 1)),
                     reads=[wkey, 'p_uT'], writes=[pk])
            return ps, pk

        for grp in range(3):
            w, wkey = next_w(OXBC + grp * 512, 512)
            for ti in range(4):
                tt = grp * 4 + ti
                ps, pk = proj_F(w, wkey, ti * 128, 128)
                S.op('act', lambda e, tt=tt, ps=ps: e.copy(out=xraw[:, tt, 3:3 + tb], in_=ps[:, 0:tb]), reads=[pk], writes=['p_xraw'])
                S.op('dve', lambda e, tt=tt: e.tensor_scalar(out=cacc[:, 0:tb], in0=xraw[:, tt, 3:3 + tb], scalar1=cw[:, 3, tt:tt + 1],
                                                              scalar2=cb[:, tt:tt + 1], op0=ALU.mult, op1=ALU.add),
                     reads=['p_xraw', 'p_cw', 'p_cb'], writes=['p_cacc'])
                for jj in range(3):
                    S.op('dve', lambda e, tt=tt, jj=jj: e.scalar_tensor_tensor(out=cacc[:, 0:tb], in0=xraw[:, tt, jj:jj + tb],
                                                                                scalar=cw[:, jj, tt:tt + 1], in1=cacc[:, 0:tb],
                                                                                op0=ALU.mult, op1=ALU.add),
                         reads=['p_xraw', 'p_cw', 'p_cacc'], writes=['p_cacc'])
                S.op('act', lambda e, tt=tt: e.activation(out=xact[:, tt, 0:tb], in_=cacc[:, 0:tb], func=AF.Silu),
                     reads=['p_cacc'], writes=['p_xact'])
                S.op('pool', lambda e, tt=tt: e.tensor_copy(out=xraw[:, tt, 0:3], in_=xraw[:, tt, tb:tb + 3]),
                     reads=['p_xraw'], writes=['p_xraw'])
        if bi == 0:
            S.op('pool', lambda e: e.memset(xact[:, :, 0:PADN], 0.0), reads=['p_xact'], writes=['p_xact'])
        S.dma('sp', lambda e: e.dma_start(out=g.xbct[:, t0:t0 + tb].rearrange("(t p) n -> p t n", p=128), in_=xact[:, :, 0:tb]),
              reads=['p_xact'], writes=['d_xbct'])
        for j in range(ncb):
            for h in range(0, 10, 8):
                n = min(8, 10 - h)
                pb = g.pb[(h // 8) % 2]
                pbk = 'pb%d' % ((h // 8) % 2)
                for i in range(n):
                    S.op('pe', lambda e, i=i, h=h, pb=pb, j=j: e.transpose(out=pb[:, i * 128:(i + 1) * 128], in_=xact[:, h + i, j * 128:(j + 1) * 128],
                                                                           identity=cs['ident_bf'][:]),
                         reads=['p_xact', 'c_ident_bf'], writes=[pbk])
                if h == 0:
                    S.op('act', lambda e, pb=pb, h=h, n=n: e.copy(out=xst[:, h * 128:(h + n) * 128], in_=pb[:, 0:n * 128]), reads=[pbk], writes=['p_xst'])
                else:
                    S.op('dve', lambda e, pb=pb, h=h, n=n: e.tensor_copy(out=xst[:, h * 128:(h + n) * 128], in_=pb[:, 0:n * 128]), reads=[pbk], writes=['p_xst'])
            S.dma('sp', lambda e, j=j: e.dma_start(out=g.xstok[t0 + j * 128:t0 + (j + 1) * 128, :], in_=xst[:]), reads=['p_xst'], writes=['d_xstok'])
        w, wkey = next_w(OMQ, 512)
        for h in range(4):
            ps, pk = proj_F(w, wkey, h * 128, 128)
            e_ = eb[h % 2]
            S.op('act', lambda e, ps=ps, e_=e_: e.copy(out=e_[:, 0:tb], in_=ps[:, 0:tb]), reads=[pk], writes=['p_eb%d' % (h % 2)])
            S.dma('sp', lambda e, h=h, e_=e_: e.dma_start(out=g.qtml[h * 128:(h + 1) * 128, t0:t0 + tb], in_=e_[:, 0:tb]),
                  reads=['p_eb%d' % (h % 2)], writes=['d_qtml'])
        w, wkey = next_w(OMK, 512)
        for h in range(4):
            ps, pk = proj_F(w, wkey, h * 128, 128)
            e_ = eb[h % 2]
            S.op('act', lambda e, ps=ps, e_=e_: e.activation(out=e_[:, 0:tb], in_=ps[:, 0:tb], func=AF.Copy, scale=128.0 ** -0.5),
                 reads=[pk], writes=['p_eb%d' % (h % 2)])
            S.dma('sp', lambda e, h=h, e_=e_: e.dma_start(out=g.ktml[h * 128:(h + 1) * 128, t0:t0 + tb], in_=e_[:, 0:tb]),
                  reads=['p_eb%d' % (h % 2)], writes=['d_ktml'])
        for j in range(ncb):
            ps, pk = proj_T(w, wkey, j, 512)
            e_ = eb[j % 2]
            S.op('act', lambda e, ps=ps, e_=e_: e.activation(out=e_[:, 0:512], in_=ps[:, 0:512], func=AF.Copy, scale=128.0 ** -0.5),
                 reads=[pk], writes=['p_eb%d' % (j % 2)])
            S.dma('sp', lambda e, j=j, e_=e_: e.dma_start(out=g.ktok[t0 + j * 128:t0 + (j + 1) * 128, :], in_=e_[:, 0:512]),
                  reads=['p_eb%d' % (j % 2)], writes=['d_ktok'])
        w, wkey = next_w(OMI, 8)
        ps, pk = proj_F(w, wkey, 0, 4)
        S.op('act', lambda e, ps=ps: e.activation(out=gt[:, 0, 0:tb], in_=ps[0:4, 0:tb], func=AF.Identity, bias=ib[:, 0:1]),
             reads=[pk, 'p_ib'], writes=['p_gt0'])
        ps, pk = proj_F(w, wkey, 4, 4)
        S.op('act', lambda e, ps=ps: e.activation(out=gt[:, 1, 0:tb], in_=ps[0:4, 0:tb], func=AF.Exp, bias=fb[:, 0:1], scale=-1.0),
             reads=[pk, 'p_fb'], writes=['p_gt1'])
        S.op('act', lambda e: e.activation(out=gt[:, 1, 0:tb], in_=gt[:, 1, 0:tb], func=AF.Ln, bias=1.0), reads=['p_gt1'], writes=['p_gt1'])
        S.op('dve', lambda e: e.tensor_scalar(out=gt[:, 1, 0:tb], in0=gt[:, 1, 0:tb], scalar1=-1.0, scalar2=None, op0=ALU.mult),
             reads=['p_gt1'], writes=['p_gt1'])
        if bi == 0:
            S.op('pool', lambda e: e.memset(gt[:, 0, 0:PADN], NEGM), reads=['p_gt0'], writes=['p_gt0'])
            S.op('pool', lambda e: e.memset(gt[:, 1, 0:PADN], 0.0), reads=['p_gt1'], writes=['p_gt1'])
        S.dma('sp', lambda e: e.dma_start(out=g.gates[0:4, t0:t0 + tb], in_=gt[:, 0, 0:tb]), reads=['p_gt0'], writes=['d_gates'])
        S.dma('sp', lambda e: e.dma_start(out=g.gates[4:8, t0:t0 + tb], in_=gt[:, 1, 0:tb]), reads=['p_gt1'], writes=['d_gates'])
        w, wkey = next_w(ODT, 16)
        for j in range(ncb):
            ps, pk = proj_T(w, wkey, j, 16)
            d_ = dts[:, j, :]
            dkey = 'p_dts%d' % j
            S.op('dve', lambda e, ps=ps, d_=d_: e.tensor_tensor(out=d_, in0=ps[:, 0:16], in1=dtb[:], op=ALU.add), reads=[pk, 'p_dtb'], writes=[dkey])
            tmp = ev[0][:, 0:16]
            S.op('dve', lambda e, d_=d_, tmp=tmp: e.scalar_tensor_tensor(out=tmp, in0=d_, scalar=-1.0, in1=d_, op0=ALU.mult, op1=ALU.max), reads=[dkey], writes=['p_ev0'])
            S.op('act', lambda e, tmp=tmp: e.activation(out=tmp, in_=tmp, func=AF.Exp, scale=-1.0), reads=['p_ev0'], writes=['p_ev0'])
            S.op('act', lambda e, tmp=tmp: e.activation(out=tmp, in_=tmp, func=AF.Ln, bias=1.0), reads=['p_ev0'], writes=['p_ev0'])
            S.op('dve', lambda e, d_=d_, tmp=tmp: e.scalar_tensor_tensor(out=ev[0][:, 32:48], in0=d_, scalar=0.0, in1=tmp, op0=ALU.max, op1=ALU.add),
                 reads=[dkey, 'p_ev0'], writes=['p_ev0b'])
            S.op('dve', lambda e: e.tensor_tensor(out=ev[0][:, 48:64], in0=ev[0][:, 32:48], in1=negA[:], op=ALU.mult),
                 reads=['p_ev0b', 'p_negA'], writes=['p_ev0b'])
            S.dma('sp', lambda e, j=j: e.dma_start(out=g.dtla[t0 + j * 128:t0 + (j + 1) * 128, :], in_=ev[0][:, 32:64]),
                  reads=['p_ev0b'], writes=['d_dtla'])
        for (off, dst, dkey, fn) in ((OZ, g.zs, 'd_zs', AF.Silu), (OMV, g.vml, 'd_vml', None), (OMO, g.og, 'd_og', AF.Sigmoid),
                                     (OAV, g.vda, 'd_vda', None)):
            for half in range(2):
                w, wkey = next_w(off + half * 512, 512)
                for j in range(ncb):
                    ps, pk = proj_T(w, wkey, j, 512)
                    e_ = eb[j % 2]
                    ek = 'p_eb%d' % (j % 2)
                    if fn is None:
                        S.op('dve', lambda e, ps=ps, e_=e_: e.tensor_copy(out=e_[:, 0:512], in_=ps[:, 0:512]), reads=[pk], writes=[ek])
                    else:
                        S.op('act', lambda e, ps=ps, e_=e_, fn=fn: e.activation(out=e_[:, 0:512], in_=ps[:, 0:512], func=fn), reads=[pk], writes=[ek])
                    S.dma('sp', lambda e, j=j, e_=e_, dst=dst, half=half: e.dma_start(
                        out=dst[t0 + j * 128:t0 + (j + 1) * 128, half * 512:(half + 1) * 512], in_=e_[:, 0:512]), reads=[ek], writes=[dkey])
        for isk, (off, dst, dkey) in enumerate(((OAQ, g.qa, 'd_qa'), (OAK, g.ka, 'd_ka'))):
            for j in range(ncb):
                c = c0 + j
                for half in range(2):
                    if j == 0 or True:
                        w, wkey = next_w(off + half * 512, 512)
                    ps, pk = proj_T(w, wkey, j, 512)
                    S.op('act', lambda e, ps=ps, half=half: e.activation(
                        out=qk[:, half * 8:(half + 1) * 8, 0:64], in_=ps[:, 0:512].rearrange("p (a d) -> p a d", d=64),
                        func=AF.Copy, scale=(1.0 if isk else 0.125)), reads=[pk], writes=['p_qk'])
                cb_ = cosT[:, c, :].unsqueeze(1).to_broadcast([128, 16, 8])
                sb_ = sinT[:, c, :].unsqueeze(1).to_broadcast([128, 16, 8])
                x1 = qk[:, :, 0:8]
                x2 = qk[:, :, 8:16]
                S.op('dve', lambda e: e.tensor_tensor(out=r1[:], in0=x1, in1=sb_, op=ALU.mult), reads=['p_qk', 'p_sin'], writes=['p_r1'])
                S.op('dve', lambda e: e.tensor_tensor(out=r2[:], in0=x2, in1=sb_, op=ALU.mult), reads=['p_qk', 'p_sin'], writes=['p_r2'])
                S.op('dve', lambda e: e.tensor_tensor(out=r3[:], in0=x1, in1=cb_, op=ALU.mult), reads=['p_qk', 'p_cos'], writes=['p_r3'])
                S.op('dve', lambda e: e.tensor_tensor(out=x2, in0=x2, in1=cb_, op=ALU.mult), reads=['p_qk', 'p_cos'], writes=['p_qk'])
                S.op('dve', lambda e: e.tensor_tensor(out=x1, in0=r3[:], in1=r2[:], op=ALU.subtract), reads=['p_r3', 'p_r2', 'p_qk'], writes=['p_qk'])
                S.op('dve', lambda e: e.tensor_tensor(out=x2, in0=x2, in1=r1[:], op=ALU.add), reads=['p_r1', 'p_qk'], writes=['p_qk'])
                S.op('dve', lambda e: e.tensor_tensor(out=ev[1][:, 0:1024].rearrange("p (a d) -> p a d", d=64), in0=qk[:, :, 0:64], in1=qk[:, :, 0:64], op=ALU.mult),
                     reads=['p_qk'], writes=['p_ev1'])
                S.op('dve', lambda e: e.tensor_reduce(out=kn2[:], in_=ev[1][:, 0:1024].rearrange("p (a d) -> p a d", d=64), axis=AX.X, op=ALU.add),
                     reads=['p_ev1'], writes=['p_kn2'])
                if isk == 0:
                    S.op('act', lambda e: e.activation(out=kn2[:], in_=kn2[:], func=AF.Sqrt), reads=['p_kn2'], writes=['p_kn2'])
                    S.op('dve', lambda e: e.tensor_scalar(out=qk[:, :, 64:65], in0=kn2[:].unsqueeze(2), scalar1=-1.0, scalar2=None, op0=ALU.mult),
                         reads=['p_kn2', 'p_qk'], writes=['p_qk'])
                    S.op('pool', lambda e: e.memset(qk[:, :, 65:66], 1.0), reads=['p_qk'], writes=['p_qk'])
                else:
                    S.op('dve', lambda e: e.tensor_tensor(out=kmaxc[:], in0=kmaxc[:], in1=kn2[:], op=ALU.max), reads=['p_kn2', 'p_kmaxc'], writes=['p_kmaxc'])
                    S.op('pool', lambda e: e.memset(qk[:, :, 64:65], 0.0), reads=['p_qk'], writes=['p_qk'])
                    if c == 0:
                        S.op('dve', lambda e: e.tensor_copy(out=qk[:, :, 65:66], in_=padk[:].unsqueeze(1).to_broadcast([128, 16, 1])),
                             reads=['p_padk', 'p_qk'], writes=['p_qk'])
                    else:
                        S.op('pool', lambda e: e.memset(qk[:, :, 65:66], 0.0), reads=['p_qk'], writes=['p_qk'])
                S.op('act', lambda e: e.copy(out=qkb[:], in_=qk[:]), reads=['p_qk'], writes=['p_qkb'])
                for hh in range(2):
                    pb = g.pb[hh]
                    pbk = 'pb%d' % hh
                    for i in range(8):
                        S.op('pe', lambda e, i=i, hh=hh, pb=pb: e.transpose(out=pb[0:66, i * 128:(i + 1) * 128], in_=qkb[:, hh * 8 + i, :],
                                                                            identity=cs['ident_bf'][:]),
                             reads=['p_qkb', 'c_ident_bf'], writes=[pbk])
                    S.op('act' if hh == 0 else 'dve',
                         (lambda e, pb=pb, hh=hh, j=j: e.copy(out=qaT[:, hh * 8:(hh + 1) * 8, j * 128:(j + 1) * 128],
                                                              in_=pb[0:66, :].rearrange("p (a b) -> p a b", b=128))) if hh == 0 else
                         (lambda e, pb=pb, hh=hh, j=j: e.tensor_copy(out=qaT[:, hh * 8:(hh + 1) * 8, j * 128:(j + 1) * 128],
                                                                     in_=pb[0:66, :].rearrange("p (a b) -> p a b", b=128))),
                         reads=[pbk], writes=['p_qaT'])
            S.dma('sp', lambda e, dst=dst: e.dma_start(out=dst[:, :, t0:t0 + tb].rearrange("a r t -> r a t"), in_=qaT[:, :, 0:tb]),
                  reads=['p_qaT'], writes=[dkey])
    kT = g.ps[4]
    S.op('pe', lambda e: e.transpose(out=kT[0:16, 0:128], in_=kmaxc[:], identity=cs['ident_f'][:]), reads=['p_kmaxc', 'c_ident_f'], writes=['ps4'])
    km = A('p_km', [16, 1], F32)
    S.op('dve', lambda e: e.tensor_reduce(out=km[:], in_=kT[0:16, 0:128], axis=AX.X, op=ALU.max), reads=['ps4'], writes=['p_km'])
    S.op('act', lambda e: e.activation(out=km[:], in_=km[:], func=AF.Sqrt), reads=['p_km'], writes=['p_km'])
    kmrow = A('p_kmrow', [16, T], BF16)
    S.op('dve', lambda e: e.tensor_scalar(out=kmrow[:], in0=km[:].to_broadcast([16, T]), scalar1=1.02, scalar2=None, op0=ALU.mult),
         reads=['p_km'], writes=['p_kmrow'])
    S.dma('sp', lambda e: e.dma_start(out=g.ka[:, 64, :], in_=kmrow[:]), reads=['p_kmrow', 'd_ka'], writes=['d_ka'])


def stage_ssd(g):
    nc, S, T, NCH = g.nc, g.S, g.T, g.NCH
    cs = g.cs
    A = g.A
    ps = g.ps
    dsk = bc_load(g, 's_dsk', g.w['ssd_d'], 16)
    bt = [A('s_bt%d' % i, [128, 128], BF16) for i in range(2)]
    ct = [A('s_ct%d' % i, [128, 128], BF16) for i in range(2)]
    xsb = [A('s_xsb%d' % i, [128, 640], BF16) for i in range(2)]
    dl = [A('s_dl%d' % i, [128, 2, 8], F32) for i in range(2)]
    zt = [A('s_zt%d' % i, [128, 512], BF16) for i in range(2)]
    sm = A('s_sm', [128, 8, 8], F32)
    x3 = A('s_x3', [128, 8, 128], F32)
    dtt = A('s_dtt', [128, 8, 128], F32)
    mt = A('s_mt', [128, 8, 128], BF16)
    xdt = A('s_xdt', [128, 8, 64], BF16)
    xw = A('s_xw', [128, 8, 64], BF16)
    t1 = A('s_t1', [128, 8, 64], F32)
    t2 = A('s_t2', [128, 8, 64], F32)
    prevS = [A('s_prev%d' % i, [128, 8, 64], F32) for i in range(2)]
    prevSb = [A('s_prevb%d' % i, [128, 8, 64], BF16) for i in range(2)]
    for i in range(2):
        S.op('pool', lambda e, i=i: e.memset(prevS[i][:], 0.0), writes=['s_prev%d' % i])
        S.op('pool', lambda e, i=i: e.memset(prevSb[i][:], 0.0), writes=['s_prevb%d' % i])
    it = 0
    for c in range(NCH):
        r0 = c * 128
        for gq in range(2):
            p = it % 2
            it += 1
            k = lambda n: '%s%d' % (n, p)
            S.dma('sp', lambda e: e.dma_start(out=bt[p][:], in_=g.xbct[1024 + gq * 128:1024 + (gq + 1) * 128, r0:r0 + 128]), reads=['d_xbct'], writes=[k('s_bt')])
            S.dma('sp', lambda e: e.dma_start(out=ct[p][:], in_=g.xbct[1280 + gq * 128:1280 + (gq + 1) * 128, r0:r0 + 128]), reads=['d_xbct'], writes=[k('s_ct')])
            S.dma('sp', lambda e: e.dma_start(out=xsb[p][:, 0:512], in_=g.xstok[r0:r0 + 128, gq * 512:(gq + 1) * 512]), reads=['d_xstok'], writes=[k('s_xsb')])
            S.dma('sp', lambda e: e.dma_start(out=xsb[p][:, 512:640], in_=g.xstok[r0:r0 + 128, 1024 + gq * 128:1024 + (gq + 1) * 128]), reads=['d_xstok', k('s_xsb')], writes=[k('s_xsb')])
            S.dma('sp', lambda e: e.dma_start(out=dl[p][:, 0, :], in_=g.dtla[r0:r0 + 128, gq * 8:(gq + 1) * 8]), reads=['d_dtla'], writes=[k('s_dl')])
            S.dma('sp', lambda e: e.dma_start(out=dl[p][:, 1, :], in_=g.dtla[r0:r0 + 128, 16 + gq * 8:16 + (gq + 1) * 8]), reads=['d_dtla', k('s_dl')], writes=[k('s_dl')])
            S.dma('sp', lambda e: e.dma_start(out=zt[p][:], in_=g.zs[r0:r0 + 128, gq * 512:(gq + 1) * 512]), reads=['d_zs'], writes=[k('s_zt')])
            dt_ = dl[p][:, 0, :]
            la = dl[p][:, 1, :]
            xs3 = xsb[p][:, 0:512].rearrange("p (e d) -> p e d", d=64)
            S.op('pe', lambda e: e.matmul(ps[0][:, 0:8], lhsT=cs['utri'][:], rhs=la, start=True, stop=True), reads=[k('s_dl'), 'c_utri'], writes=['ps0a'])
            S.op('pe', lambda e: e.matmul(ps[0][:, 8:16], lhsT=cs['ones_f'][:], rhs=la, start=True, stop=True), reads=[k('s_dl'), 'c_ones_f'], writes=['ps0b'])
            S.op('dve', lambda e: e.tensor_copy(out=sm[:, 0, :], in_=ps[0][:, 0:8]), reads=['ps0a'], writes=['s_sm0'])
            S.op('dve', lambda e: e.tensor_copy(out=sm[:, 1, :], in_=ps[0][:, 8:16]), reads=['ps0b'], writes=['s_sm1'])
            S.op('dve', lambda e: e.tensor_scalar(out=sm[:, 2, :], in0=sm[:, 0, :], scalar1=-1.0, scalar2=None, op0=ALU.mult), reads=['s_sm0'], writes=['s_sm2'])
            S.op('act', lambda e: e.activation(out=sm[:, 3, :], in_=sm[:, 0, :], func=AF.Exp), reads=['s_sm0'], writes=['s_sm3'])
            S.op('dve', lambda e: e.tensor_tensor(out=sm[:, 7, :], in0=sm[:, 1, :], in1=sm[:, 0, :], op=ALU.subtract), reads=['s_sm0', 's_sm1'], writes=['s_sm7'])
            S.op('act', lambda e: e.activation(out=sm[:, 4, :], in_=sm[:, 7, :], func=AF.Exp), reads=['s_sm7'], writes=['s_sm4'])
            S.op('act', lambda e: e.activation(out=sm[:, 5, :], in_=sm[:, 1, :], func=AF.Exp), reads=['s_sm1'], writes=['s_sm5'])
            S.op('dve', lambda e: e.tensor_tensor(out=sm[:, 6, :], in0=sm[:, 4, :], in1=dt_, op=ALU.mult), reads=['s_sm4', k('s_dl')], writes=['s_sm6'])
            S.op('dve', lambda e: e.tensor_tensor(out=x3[:], in0=cs['utri'][:].unsqueeze(1).to_broadcast([128, 8, 128]),
                                                   in1=la.unsqueeze(2).to_broadcast([128, 8, 128]), op=ALU.mult),
                 reads=['c_utri', k('s_dl')], writes=['s_x3'])
            for hf in range(2):
                pk = 'ps%d' % (1 + hf)
                S.op('pe', lambda e, hf=hf: e.matmul(ps[1 + hf][:, 0:512], lhsT=cs['ones_f'][:], rhs=x3[:, hf * 4:(hf + 1) * 4, :].rearrange("p a b -> p (a b)"),
                                                     start=True, stop=False), reads=['s_x3', 'c_ones_f'], writes=[pk])
                S.op('pe', lambda e, hf=hf: e.matmul(ps[1 + hf][:, 0:512], lhsT=cs['ident_f'][:], rhs=cs['negmask8'][:, hf * 512:(hf + 1) * 512],
                                                     start=False, stop=True), reads=['c_ident_f', 'c_negmask8'], writes=[pk])
            for e_ in range(8):
                pk = 'ps%d' % (1 + e_ // 4)
                S.op('act', lambda e, e_=e_: e.activation(out=dtt[:, e_, :], in_=ps[1 + e_ // 4][:, (e_ % 4) * 128:(e_ % 4 + 1) * 128], func=AF.Exp,
                                                          bias=sm[:, 2, e_:e_ + 1]), reads=[pk, 's_sm2'], writes=['s_dtt'])
            S.op('pe', lambda e: e.matmul(ps[0][:, 128:256], lhsT=bt[p][:], rhs=ct[p][:], start=True, stop=True), reads=[k('s_bt'), k('s_ct')], writes=['ps0c'])
            S.op('dve', lambda e: e.tensor_tensor(out=mt[:], in0=dtt[:], in1=ps[0][:, 128:256].unsqueeze(1).to_broadcast([128, 8, 128]), op=ALU.mult),
                 reads=['s_dtt', 'ps0c'], writes=['s_mt'])
            S.op('pool', lambda e: e.tensor_tensor(out=xdt[:], in0=xs3, in1=dt_.unsqueeze(2).to_broadcast([128, 8, 64]), op=ALU.mult),
                 reads=[k('s_xsb'), k('s_dl')], writes=['s_xdt'])
            S.op('pool', lambda e: e.tensor_tensor(out=xw[:], in0=xs3, in1=sm[:, 6, :].unsqueeze(2).to_broadcast([128, 8, 64]), op=ALU.mult),
                 reads=[k('s_xsb'), 's_sm6'], writes=['s_xw'])
            for e_ in range(8):
                S.op('pe', lambda e, e_=e_: e.matmul(ps[4][:, e_ * 64:(e_ + 1) * 64], lhsT=mt[:, e_, :], rhs=xdt[:, e_, :], start=True, stop=True),
                     reads=['s_mt', 's_xdt'], writes=['ps4'])
            S.op('pe', lambda e: e.matmul(ps[5][:, 0:512], lhsT=ct[p][:], rhs=prevSb[gq][:].rearrange("p a b -> p (a b)"), start=True, stop=True),
                 reads=[k('s_ct'), 's_prevb%d' % gq], writes=['ps5'])
            S.op('pe', lambda e: e.matmul(ps[3][:, 0:512], lhsT=xsb[p][:, 512:640], rhs=xw[:].rearrange("p a b -> p (a b)"), start=True, stop=True),
                 reads=[k('s_xsb'), 's_xw'], writes=['ps3'])
            S.op('dve', lambda e: e.tensor_tensor(out=t1[:], in0=ps[5][:, 0:512].rearrange("p (a b) -> p a b", b=64),
                                                   in1=sm[:, 3, :].unsqueeze(2).to_broadcast([128, 8, 64]), op=ALU.mult), reads=['ps5', 's_sm3'], writes=['s_t1'])
            S.op('dve', lambda e: e.tensor_tensor(out=t1[:], in0=t1[:], in1=ps[4][:, 0:512].rearrange("p (a b) -> p a b", b=64), op=ALU.add),
                 reads=['ps4', 's_t1'], writes=['s_t1'])
            S.op('pool', lambda e: e.tensor_tensor(out=t2[:], in0=xs3, in1=dsk[:, gq * 8:(gq + 1) * 8].unsqueeze(2).to_broadcast([128, 8, 64]), op=ALU.mult),
                 reads=[k('s_xsb'), 's_dsk'], writes=['s_t2'])
            S.op('dve', lambda e: e.tensor_tensor(out=t1[:], in0=t1[:], in1=t2[:], op=ALU.add), reads=['s_t1', 's_t2'], writes=['s_t1'])
            S.op('dve', lambda e: e.tensor_tensor(out=t1[:], in0=t1[:], in1=zt[p][:].rearrange("p (a b) -> p a b", b=64), op=ALU.mult),
                 reads=['s_t1', k('s_zt')], writes=['s_t1'])
            S.dma('sp', lambda e: e.dma_start(out=g.yall[r0:r0 + 128, gq * 512:(gq + 1) * 512], in_=t1[:].rearrange("p a b -> p (a b)")),
                  reads=['s_t1'], writes=['d_yall'])
            S.op('dve', lambda e: e.tensor_tensor(out=prevS[gq][:], in0=prevS[gq][:], in1=sm[:, 5, :].unsqueeze(2).to_broadcast([128, 8, 64]), op=ALU.mult),
                 reads=['s_prev%d' % gq, 's_sm5'], writes=['s_prev%d' % gq])
            S.op('dve', lambda e: e.tensor_tensor(out=prevS[gq][:], in0=prevS[gq][:], in1=ps[3][:, 0:512].rearrange("p (a b) -> p a b", b=64), op=ALU.add),
                 reads=['s_prev%d' % gq, 'ps3'], writes=['s_prev%d' % gq])
            S.op('act', lambda e: e.copy(out=prevSb[gq][:], in_=prevS[gq][:]), reads=['s_prev%d' % gq], writes=['s_prevb%d' % gq])


def stage_mlstm(g):
    nc, S, T, NCH = g.nc, g.S, g.T, g.NCH
    cs = g.cs
    A = g.A
    mlw = bc_load(g, 'm_mlw', g.w['mlstm_norm_w'], 1024)
    Cst = [A('m_C%d' % h, [128, 257], F32) for h in range(4)]
    Cstb = [A('m_Cb%d' % h, [128, 257], BF16) for h in range(4)]
    mprev = A('m_mprev', [128, 4], F32)
    S.op('pool', lambda e: e.memset(mprev[:], NEGM), writes=['m_mprev0', 'm_mprev1', 'm_mprev2', 'm_mprev3'])
    for h in range(4):
        S.op('pool', lambda e, h=h: e.memset(Cst[h][:], 0.0), writes=['m_C%d' % h])
        S.op('pool', lambda e, h=h: e.memset(Cstb[h][:], 0.0), writes=['m_Cb%d' % h])
    TS = []
    for s in range(2):
        t = {}
        for p in range(2):
            sfx = '_%d_%d' % (s, p)
            t['qT', p] = A('m_qT' + sfx, [128, 128], BF16)
            t['kT', p] = A('m_kT' + sfx, [128, 128], BF16)
            t['ktk', p] = A('m_ktk' + sfx, [128, 128], BF16)
            t['va', p] = A('m_va' + sfx, [128, 257], BF16)
            t['ip', p] = A('m_ip' + sfx, [128, 128], F32)
            t['lf', p] = A('m_lf' + sfx, [128, 128], F32)
            t['og', p] = A('m_og' + sfx, [128, 256], BF16)
            S.op('pool', lambda e, tt=t['va', p]: e.memset(tt[:, 256:257], 1.0), writes=['m_va' + sfx])
        for n in ('fcs', 'gr', 'pr', 'tmp', 'et'):
            t[n] = A('m_%s_%d' % (n, s), [128, 128], F32)
        for n in ('set', 'qts', 'kw'):
            t[n] = A('m_%s_%d' % (n, s), [128, 128], BF16)
        t['col'] = A('m_col_%d' % s, [128, 16], F32)
        for n in ('ht', 'yml', 'jk'):
            t[n] = A('m_%s_%d' % (n, s), [128, 256], F32)
        TS.append(t)

    def body(c, h, s, p):
        t = TS[s]
        ps = g.ps[3 * s:3 * s + 3]
        pk = ['ps%d' % (3 * s + i) for i in range(3)]
        r0 = c * 128
        sfx = '_%d_%d' % (s, p)
        k = lambda n: 'm_%s%s' % (n, sfx)
        ks = lambda n: 'm_%s_%d' % (n, s)
        qT, kT, ktk, va, ipr, lfr, ogt = (t[n, p] for n in ('qT', 'kT', 'ktk', 'va', 'ip', 'lf', 'og'))
        fcs, gr, pr, tmp, et, setb, qts, kw, col, ht, yml, jk = (t[n] for n in ('fcs', 'gr', 'pr', 'tmp', 'et', 'set', 'qts', 'kw', 'col', 'ht', 'yml', 'jk'))
        S.dma('sp', lambda e: e.dma_start(out=qT[:], in_=g.qtml[h * 128:(h + 1) * 128, r0:r0 + 128]), reads=['d_qtml'], writes=[k('qT')]); yield
        S.dma('sp', lambda e: e.dma_start(out=kT[:], in_=g.ktml[h * 128:(h + 1) * 128, r0:r0 + 128]), reads=['d_ktml'], writes=[k('kT')]); yield
        S.dma('sp', lambda e: e.dma_start(out=ktk[:], in_=g.ktok[r0:r0 + 128, h * 128:(h + 1) * 128]), reads=['d_ktok'], writes=[k('ktk')]); yield
        S.dma('sp', lambda e: e.dma_start(out=va[:, 0:256], in_=g.vml[r0:r0 + 128, h * 256:(h + 1) * 256]), reads=['d_vml', k('va')], writes=[k('va')]); yield
        S.dma('sp', lambda e: e.dma_start(out=ipr[:], in_=g.gates[h, r0:r0 + 128].partition_broadcast(128)), reads=['d_gates'], writes=[k('ip')]); yield
        S.dma('sp', lambda e: e.dma_start(out=lfr[:], in_=g.gates[4 + h, r0:r0 + 128].partition_broadcast(128)), reads=['d_gates'], writes=[k('lf')]); yield
        S.dma('sp', lambda e: e.dma_start(out=ogt[:], in_=g.og[r0:r0 + 128, h * 256:(h + 1) * 256]), reads=['d_og'], writes=[k('og')]); yield
        mp = mprev[:, h:h + 1]
        mpk = 'm_mprev%d' % h
        ck = lambda i: 'm_col%d_%d' % (i, s)
        S.op('dve', lambda e: e.tensor_tensor_scan(out=fcs[:], data0=cs['ones_f'][:], data1=lfr[:], initial=0.0, op0=ALU.mult, op1=ALU.add),
             reads=[k('lf'), 'c_ones_f'], writes=[ks('fcs')]); yield
        S.op('dve', lambda e: e.tensor_tensor(out=gr[:], in0=ipr[:], in1=fcs[:], op=ALU.subtract), reads=[k('ip'), ks('fcs')], writes=[ks('gr')]); yield
        S.op('dve', lambda e: e.tensor_tensor_scan(out=pr[:], data0=gr[:], data1=gr[:], initial=mp, op0=ALU.max, op1=ALU.max),
             reads=[ks('gr'), mpk], writes=[ks('pr')]); yield
        ftot = fcs[:, 127:128]
        S.op('dve', lambda e: e.tensor_scalar(out=tmp[:], in0=gr[:], scalar1=ftot, scalar2=None, op0=ALU.add), reads=[ks('gr'), ks('fcs')], writes=[ks('tmp')]); yield
        S.op('dve', lambda e: e.tensor_reduce(out=col[:, 1:2], in_=tmp[:], axis=AX.X, op=ALU.max), reads=[ks('tmp')], writes=[ck(1)]); yield
        for ci, (src_, sk_) in ((2, (gr, ks('gr'))), (3, (pr, ks('pr'))), (4, (fcs, ks('fcs')))):
            S.op('dve', lambda e, ci=ci, src_=src_: e.scalar_tensor_tensor(out=tmp[:], in0=src_[:], scalar=1.0, in1=cs['ident_f'][:],
                                                                           op0=ALU.mult, op1=ALU.mult, accum_out=col[:, ci:ci + 1]),
                 reads=[sk_, 'c_ident_f'], writes=[ks('tmp'), ck(ci)]); yield
        S.op('dve', lambda e: e.tensor_tensor(out=col[:, 7:8], in0=ftot, in1=col[:, 1:2], op=ALU.subtract), reads=[ks('fcs'), ck(1)], writes=[ck(7)]); yield
        S.op('act', lambda e: e.activation(out=col[:, 5:6], in_=col[:, 2:3], func=AF.Exp, bias=col[:, 7:8]), reads=[ck(2), ck(7)], writes=[ck(5)]); yield
        S.op('dve', lambda e: e.tensor_scalar(out=kw[:], in0=ktk[:], scalar1=col[:, 5:6], scalar2=None, op0=ALU.mult), reads=[k('ktk'), ck(5)], writes=[ks('kw')]); yield
        S.op('pe', lambda e: e.matmul(ps[0][:, 0:257], lhsT=kw[:], rhs=va[:], start=True, stop=True), reads=[ks('kw'), k('va')], writes=[pk[0]]); yield
        S.op('pe', lambda e: e.matmul(ps[1][:, 0:128], lhsT=kT[:], rhs=qT[:], start=True, stop=True), reads=[k('kT'), k('qT')], writes=[pk[1]]); yield
        S.op('dve', lambda e: e.tensor_tensor(out=et[:], in0=cs['negmask'][:], in1=pr[:], op=ALU.subtract), reads=['c_negmask', ks('pr')], writes=[ks('et')]); yield
        S.op('act', lambda e: e.activation(out=et[:], in_=et[:], func=AF.Exp, bias=col[:, 2:3]), reads=[ks('et'), ck(2)], writes=[ks('et')]); yield
        S.op('dve', lambda e: e.tensor_tensor(out=setb[:], in0=et[:], in1=ps[1][:, 0:128], op=ALU.mult), reads=[ks('et'), pk[1]], writes=[ks('set')]); yield
        S.op('act', lambda e: e.activation(out=tmp[:], in_=pr[:], func=AF.Exp, bias=mp, scale=-1.0), reads=[ks('pr'), mpk], writes=[ks('tmp')]); yield
        S.op('dve', lambda e: e.tensor_tensor(out=qts[:], in0=qT[:], in1=tmp[:], op=ALU.mult), reads=[k('qT'), ks('tmp')], writes=[ks('qts')]); yield
        S.op('pe', lambda e: e.matmul(ps[2][:, 0:257], lhsT=setb[:], rhs=va[:], start=True, stop=False), reads=[ks('set'), k('va')], writes=[pk[2]]); yield
        S.op('pe', lambda e: e.matmul(ps[2][:, 0:257], lhsT=qts[:], rhs=Cstb[h][:], start=False, stop=True), reads=[ks('qts'), 'm_Cb%d' % h], writes=[pk[2]]); yield
        S.op('dve', lambda e: e.tensor_tensor(out=col[:, 8:9], in0=col[:, 4:5], in1=col[:, 3:4], op=ALU.add), reads=[ck(4), ck(3)], writes=[ck(8)]); yield
        S.op('dve', lambda e: e.tensor_scalar(out=col[:, 8:9], in0=col[:, 8:9], scalar1=-1.0, scalar2=80.0, op0=ALU.mult, op1=ALU.min), reads=[ck(8)], writes=[ck(8)]); yield
        S.op('act', lambda e: e.activation(out=col[:, 8:9], in_=col[:, 8:9], func=AF.Exp), reads=[ck(8)], writes=[ck(8)]); yield
        S.op('dve', lambda e: e.tensor_copy(out=col[:, 15:16], in_=ps[2][:, 256:257]), reads=[pk[2]], writes=[ck(15)]); yield
        S.op('dve', lambda e: e.scalar_tensor_tensor(out=col[:, 9:10], in0=col[:, 15:16], scalar=-1.0, in1=col[:, 15:16], op0=ALU.mult, op1=ALU.max), reads=[ck(15)], writes=[ck(9)]); yield
        S.op('dve', lambda e: e.tensor_tensor(out=col[:, 9:10], in0=col[:, 9:10], in1=col[:, 8:9], op=ALU.max), reads=[ck(9), ck(8)], writes=[ck(9)]); yield
        S.op('dve', lambda e: e.reciprocal(out=col[:, 9:10], in_=col[:, 9:10]), reads=[ck(9)], writes=[ck(9)]); yield
        S.op('dve', lambda e: e.tensor_scalar(out=ht[:], in0=ps[2][:, 0:256], scalar1=col[:, 9:10], scalar2=None, op0=ALU.mult), reads=[pk[2], ck(9)], writes=[ks('ht')]); yield
        S.op('act', lambda e: e.activation(out=jk[:], in_=ht[:], func=AF.Square, accum_out=col[:, 10:11]), reads=[ks('ht')], writes=[ks('jk'), ck(10)]); yield
        S.op('dve', lambda e: e.tensor_scalar(out=col[:, 10:11], in0=col[:, 10:11], scalar1=1.0 / 256, scalar2=EPS, op0=ALU.mult, op1=ALU.add), reads=[ck(10)], writes=[ck(10)]); yield
        S.op('act', lambda e: e.activation(out=col[:, 10:11], in_=col[:, 10:11], func=AF.Sqrt), reads=[ck(10)], writes=[ck(10)]); yield
        S.op('dve', lambda e: e.reciprocal(out=col[:, 10:11], in_=col[:, 10:11]), reads=[ck(10)], writes=[ck(10)]); yield
        S.op('dve', lambda e: e.scalar_tensor_tensor(out=yml[:], in0=ht[:], scalar=col[:, 10:11], in1=mlw[:, h * 256:(h + 1) * 256], op0=ALU.mult, op1=ALU.mult),
             reads=[ks('ht'), ck(10), 'm_mlw'], writes=[ks('yml')]); yield
        S.op('dve', lambda e: e.tensor_tensor(out=yml[:], in0=yml[:], in1=ogt[:], op=ALU.mult), reads=[ks('yml'), k('og')], writes=[ks('yml')]); yield
        S.dma('sp', lambda e: e.dma_start(out=g.yall[r0:r0 + 128, 1024 + h * 256:1024 + (h + 1) * 256], in_=yml[:]), reads=[ks('yml')], writes=['d_yall']); yield
        S.op('dve', lambda e: e.tensor_tensor(out=col[:, 11:12], in0=ftot, in1=mp, op=ALU.add), reads=[ks('fcs'), mpk], writes=[ck(11)]); yield
        S.op('dve', lambda e: e.tensor_tensor(out=col[:, 12:13], in0=col[:, 11:12], in1=col[:, 1:2], op=ALU.max), reads=[ck(11), ck(1)], writes=[ck(12)]); yield
        S.op('dve', lambda e: e.tensor_tensor(out=col[:, 13:14], in0=col[:, 11:12], in1=col[:, 12:13], op=ALU.subtract), reads=[ck(11), ck(12)], writes=[ck(13)]); yield
        S.op('dve', lambda e: e.tensor_tensor(out=col[:, 14:15], in0=col[:, 1:2], in1=col[:, 12:13], op=ALU.subtract), reads=[ck(1), ck(12)], writes=[ck(14)]); yield
        S.op('act', lambda e: e.activation(out=col[:, 13:15], in_=col[:, 13:15], func=AF.Exp), reads=[ck(13), ck(14)], writes=[ck(13), ck(14)]); yield
        S.op('dve', lambda e: e.tensor_scalar(out=Cst[h][:], in0=Cst[h][:], scalar1=col[:, 13:14], scalar2=None, op0=ALU.mult), reads=['m_C%d' % h, ck(13)], writes=['m_C%d' % h]); yield
        S.op('dve', lambda e: e.scalar_tensor_tensor(out=Cst[h][:], in0=ps[0][:, 0:257], scalar=col[:, 14:15], in1=Cst[h][:], op0=ALU.mult, op1=ALU.add),
             reads=[pk[0], ck(14), 'm_C%d' % h], writes=['m_C%d' % h]); yield
        S.op('act', lambda e: e.copy(out=Cstb[h][:], in_=Cst[h][:]), reads=['m_C%d' % h], writes=['m_Cb%d' % h]); yield
        S.op('dve', lambda e: e.tensor_copy(out=mp, in_=col[:, 12:13]), reads=[ck(12), mpk], writes=[mpk]); yield

    it = 0
    for c in range(NCH):
        for hp in range(2):
            p = it % 2
            it += 1
            gens = [body(c, 2 * hp + s, s, p) for s in range(2)]
            live = list(gens)
            while live:
                nxt = []
                for gen in live:
                    try:
                        next(gen)
                        nxt.append(gen)
                    except StopIteration:
                        pass
                live = nxt


def stage_attn(g):
    nc, S, T, NCH = g.nc, g.S, g.T, g.NCH
    cs = g.cs
    A = g.A
    ps = g.ps
    lt = [bc_load(g, 'a_l%d' % i, g.w[n], 64) for i, n in enumerate(('diff_lambda_q1', 'diff_lambda_k1', 'diff_lambda_q2', 'diff_lambda_k2'))]
    lj = A('a_lj', [128, 64], F32)
    lam = A('a_lam', [128, 4], F32)
    for i in range(2):
        S.op('dve', lambda e, i=i: e.scalar_tensor_tensor(out=lj[:], in0=lt[2 * i][:], scalar=1.0, in1=lt[2 * i + 1][:], op0=ALU.mult, op1=ALU.mult,
                                                          accum_out=lam[:, i:i + 1]), reads=['a_l%d' % (2 * i), 'a_l%d' % (2 * i + 1)], writes=['a_lj', 'a_lam%d' % i])
    S.op('act', lambda e: e.activation(out=lam[:, 0:2], in_=lam[:, 0:2], func=AF.Exp), reads=['a_lam0', 'a_lam1'], writes=['a_lam0', 'a_lam1'])
    S.op('dve', lambda e: e.tensor_tensor(out=lam[:, 2:3], in0=lam[:, 1:2], in1=lam[:, 0:1], op=ALU.subtract), reads=['a_lam0', 'a_lam1'], writes=['a_lam2'])
    S.op('dve', lambda e: e.tensor_scalar(out=lam[:, 2:3], in0=lam[:, 2:3], scalar1=-g.lam_init, scalar2=None, op0=ALU.add), reads=['a_lam2'], writes=['a_lam2'])
    dnw = A('a_dnw', [128, 1], F32)
    S.dma('sp', lambda e: e.dma_start(out=dnw[:], in_=g.w['diff_norm_w'].rearrange("(p o) -> p o", o=1)), writes=['a_dnw'])
    S.op('dve', lambda e: e.tensor_scalar(out=dnw[:], in0=dnw[:], scalar1=1.0 - g.lam_init, scalar2=None, op0=ALU.mult), reads=['a_dnw'], writes=['a_dnw'])
    kah = [A('a_ka%d' % m, [66, T], BF16) for m in range(2)]
    vh = A('a_vh', [128, NCH, 128], BF16)
    qat = [[A('a_qa%d_%d' % (i, m), [66, 512], BF16) for m in range(2)] for i in range(2)]
    pt = [A('a_pt%d' % i, [128, 512], BF16) for i in range(3)]
    rz = A('a_rz', [128, 512], F32)
    rm = [A('a_rm%d' % m, [128, 512], F32) for m in range(2)]
    sq = A('a_sq', [128, 512], F32)
    yb = A('a_yb', [128, 512], BF16)
    yo = A('a_yo', [128, 4, 128], F32)
    nqt = (NCH + 3) // 4
    pi = 0
    for h in range(8):
        for m in range(2):
            S.dma('sp', lambda e, m=m: e.dma_start(out=kah[m][:], in_=g.ka[h * 2 + m, :, :]), reads=['d_ka'], writes=['a_ka%d' % m])
        S.dma('sp', lambda e: e.dma_start(out=vh[:], in_=g.vda[:, h * 128:(h + 1) * 128].rearrange("(c p) d -> p c d", p=128)), reads=['d_vda'], writes=['a_vh'])
        for qt in range(nqt):
            q0 = qt * 512
            nq = min(512, T - q0)
            nkb = (q0 + nq) // 128
            qb = qt % 2
            for m in range(2):
                S.dma('sp', lambda e, m=m: e.dma_start(out=qat[qb][m][:, 0:nq], in_=g.qa[h * 2 + m, :, q0:q0 + nq]), reads=['d_qa'], writes=['a_qa%d_%d' % (qb, m)])
            for m in range(2):
                def emit_qk(kb, m=m):
                    sb = kb % 2
                    S.op('pe', lambda e: e.matmul(ps[sb][:, 0:nq], lhsT=kah[m][:, kb * 128:(kb + 1) * 128], rhs=qat[qb][m][:, 0:nq], start=True, stop=True),
                         reads=['a_ka%d' % m, 'a_qa%d_%d' % (qb, m)], writes=['ps%d' % sb])

                def emit_exp(kb, pti):
                    sb = kb % 2
                    S.op('act', lambda e: e.activation(out=pt[pti][:, 0:nq], in_=ps[sb][:, 0:nq], func=AF.Exp), reads=['ps%d' % sb], writes=['a_pt%d' % pti])
                    if kb * 128 + 127 > q0:
                        v = kb - qt * 4
                        S.op('pool', lambda e: e.tensor_tensor(out=pt[pti][:, 0:nq], in0=pt[pti][:, 0:nq], in1=cs['cmask'][:, v, 0:nq], op=ALU.mult),
                             reads=['a_pt%d' % pti, 'c_cmask'], writes=['a_pt%d' % pti])

                def emit_pv(kb, pti, m=m):
                    S.op('pe', lambda e: e.matmul(ps[2 + m][:, 0:nq], lhsT=vh[:, kb, :], rhs=pt[pti][:, 0:nq], start=(kb == 0), stop=(kb == nkb - 1)),
                         reads=['a_vh', 'a_pt%d' % pti], writes=['ps%d' % (2 + m)])
                    S.op('pe', lambda e: e.matmul(ps[4 + m][:, 0:nq], lhsT=cs['ones_bf'][:], rhs=pt[pti][:, 0:nq], start=(kb == 0), stop=(kb == nkb - 1)),
                         reads=['c_ones_bf', 'a_pt%d' % pti], writes=['ps%d' % (4 + m)])

                emit_qk(0)
                ptis = {}
                for kb in range(nkb):
                    ptis[kb] = pi % 3
                    pi += 1
                    emit_exp(kb, ptis[kb])
                    if kb + 1 < nkb:
                        emit_qk(kb + 1)
                    emit_pv(kb, ptis[kb])
                S.op('dve', lambda e, m=m: e.tensor_scalar(out=rz[:, 0:nq], in0=ps[4 + m][:, 0:nq], scalar1=1e-30, scalar2=None, op0=ALU.max), reads=['ps%d' % (4 + m)], writes=['a_rz'])
                S.op('dve', lambda e: e.reciprocal(out=rz[:, 0:nq], in_=rz[:, 0:nq]), reads=['a_rz'], writes=['a_rz'])
                S.op('dve', lambda e, m=m: e.tensor_tensor(out=rm[m][:, 0:nq], in0=ps[2 + m][:, 0:nq], in1=rz[:, 0:nq], op=ALU.mult), reads=['ps%d' % (2 + m), 'a_rz'], writes=['a_rm%d' % m])
            S.op('dve', lambda e: e.scalar_tensor_tensor(out=rm[0][:, 0:nq], in0=rm[1][:, 0:nq], scalar=lam[:, 2:3], in1=rm[0][:, 0:nq], op0=ALU.mult, op1=ALU.add),
                 reads=['a_rm0', 'a_rm1', 'a_lam2'], writes=['a_rm0'])
            S.op('act', lambda e: e.activation(out=sq[:, 0:nq], in_=rm[0][:, 0:nq], func=AF.Square), reads=['a_rm0'], writes=['a_sq'])
            S.op('pe', lambda e: e.matmul(ps[0][:, 0:nq], lhsT=cs['ones_f'][:], rhs=sq[:, 0:nq], start=True, stop=True), reads=['c_ones_f', 'a_sq'], writes=['ps0'])
            S.op('dve', lambda e: e.tensor_scalar(out=sq[:, 0:nq], in0=ps[0][:, 0:nq], scalar1=1.0 / 128, scalar2=EPS, op0=ALU.mult, op1=ALU.add), reads=['ps0', 'a_sq'], writes=['a_sq'])
            S.op('act', lambda e: e.activation(out=sq[:, 0:nq], in_=sq[:, 0:nq], func=AF.Sqrt), reads=['a_sq'], writes=['a_sq'])
            S.op('dve', lambda e: e.reciprocal(out=sq[:, 0:nq], in_=sq[:, 0:nq]), reads=['a_sq'], writes=['a_sq'])
            S.op('dve', lambda e: e.scalar_tensor_tensor(out=yb[:, 0:nq], in0=rm[0][:, 0:nq], scalar=dnw[:, 0:1], in1=sq[:, 0:nq], op0=ALU.mult, op1=ALU.mult),
                 reads=['a_rm0', 'a_dnw', 'a_sq'], writes=['a_yb'])
            nsub = nq // 128
            for i in range(nsub):
                S.op('pe', lambda e, i=i: e.transpose(out=g.pb[0][:, i * 128:(i + 1) * 128], in_=yb[:, i * 128:(i + 1) * 128], identity=cs['ident_bf'][:]),
                     reads=['a_yb', 'c_ident_bf'], writes=['pb0'])
            S.op('act', lambda e: e.copy(out=yo[:, 0:nsub, :], in_=g.pb[0][:, 0:nsub * 128].rearrange("p (a b) -> p a b", b=128)), reads=['pb0'], writes=['a_yo'])
            S.dma('sp', lambda e: e.dma_start(out=g.yall[q0:q0 + nq, 2048 + h * 128:2048 + (h + 1) * 128].rearrange("(a p) d -> p a d", p=128), in_=yo[:, 0:nsub, :]),
                  reads=['a_yo'], writes=['d_yall'])


def phase_c(g):
    nc, S, NCC = g.nc, g.S, g.NCC
    cs = g.cs
    A = g.A
    ps = g.ps
    w_in = 'w_in'
    TB = 256
    rows = A('c_rows', [128, NCC], I32)
    S.dma('sp', lambda e: e.dma_start(out=rows[:], in_=g.rows), writes=['c_rows'])
    rmask = A('c_rmask', [128, NCC], F32)
    S.dma('sp', lambda e: e.dma_start(out=rmask[:], in_=g.rowmask), writes=['c_rmask'])
    ssdw = bc_load(g, 'c_ssdw', g.w['ssd_norm_w'], 1024)
    nwb = A('c_nwb', [128, D], F32)
    xres = A('c_xres', [128, 2, D], F32)
    h2b = A('c_h2b', [128, 2, D], BF16)
    uT = A('c_uT', [128, KC, TB], BF16)
    ub = A('c_ub', [128, D], BF16)
    junk = A('c_junk', [128, D], BF16)
    st = A('c_st', [128, 2], F32)
    yrow = A('c_yrow', [128, 3072], F32)
    ybf = A('c_ybf', [128, 3072], BF16)
    yT = A('c_yT', [128, 24, TB], BF16)
    mT = A('c_mT', [128, KC, TB], BF16)
    acc4 = A('c_acc4', [128, 4, TB], F32)
    sg = A('c_sg', [128, TB], F32)
    tmpm = A('c_tmpm', [128, TB], F32)
    wt = [A('c_wt%d' % i, [128, KC, 512], BF16) for i in range(2)]
    skT = A('c_skT', [128, 16, 128], BF16)
    skl = A('c_skl', [128, 128], F32)
    qTs = A('c_qTs', [128, TB], BF16)
    sc = A('c_sc', [128, 128], F32)
    sc2 = A('c_sc2', [128, 128], F32)
    top = A('c_top', [128, 2, 16, 16], F32)
    topi = A('c_topi', [128, 2, 16, 16], U32)
    topf = A('c_topf', [128, 16, 16], F32)
    cand = A('c_cand', [128, 16, 16], F32)
    cand2 = A('c_cand2', [128, 16, 16], F32)
    best = A('c_best', [128, 8, 16], F32)
    bpos = A('c_bpos', [128, 8, 16], U32)
    bq = A('c_bq', [128, 8, 16], U32)
    af = A('c_af', [128, 8, 16], F32)
    oh = A('c_oh', [128, 8, 16, 16], F32)
    isel = A('c_isel', [128, 2, 8, 16], F32)
    idxf = A('c_idxf', [128, 128], F32)
    idx = A('c_idx', [128, 128], I32)
    gate = A('c_gate', [128, 8, 16], F32)
    gsum = A('c_gsum', [128, 8], F32)
    pre = A('c_pre', [128, 128], F32)
    wgt = A('c_wgt', [128, 128], F32)
    ug = [A('c_ug%d' % i, [128, 2 * D], BF16) for i in range(5)]
    ugk = ['c_ug%d' % i for i in range(5)]
    ug += [yT[:, 0:16, :].rearrange("p a b -> p (a b)"), mT[:].rearrange("p a b -> p (a b)"), uT[:].rearrange("p a b -> p (a b)")]
    ugk += ['c_yT', 'c_mT', 'c_uT']
    NG = len(ug)
    dg = [A('c_dg%d' % i, [128, 128], BF16) for i in range(2)]
    g1 = A('c_g1', [128, 128], F32)

    for m in range(2):
        for h in range(8):
            S.dma('sp', lambda e, m=m, h=h: e.dma_start(out=skl[:], in_=g.w['peer_sub_keys'][m, h, :, :]), writes=['c_skl'])
            S.op('pe', lambda e: e.transpose(out=ps[0][:, 0:128], in_=skl[:], identity=cs['ident_f'][:]), reads=['c_skl', 'c_ident_f'], writes=['ps0'])
            S.op('act', lambda e, m=m, h=h: e.copy(out=skT[:, h * 2 + m, :], in_=ps[0][:, 0:128]), reads=['ps0'], writes=['c_skT'])

    wi = [0]

    def next_w(src, nk, c0_, ncols):
        i = wi[0] % 2
        wi[0] += 1
        load_w(g, wt[i], 'c_wt%d' % i, src, 0, nk, c0_, ncols)
        return wt[i], 'c_wt%d' % i

    pidx = [0]

    def next_ps():
        i = pidx[0] % 6
        pidx[0] += 1
        return ps[i], 'ps%d' % i

    nblk = (NCC + 1) // 2
    gi = 0
    for bi in range(nblk):
        c0 = bi * 2
        ncb = min(2, NCC - c0)
        tb = ncb * 128
        S.dma('sp', lambda e: e.dma_start(out=nwb[:], in_=g.w['mix_norm_w'].partition_broadcast(128)), writes=['c_nwb'])
        for j in range(ncb):
            S.dma('pool', lambda e, j=j: e.indirect_dma_start(out=xres[:, j, :], out_offset=None, in_=g.x,
                                                              in_offset=bass.IndirectOffsetOnAxis(ap=rows[:, c0 + j:c0 + j + 1], axis=0)),
                  reads=['c_rows', g.xkey], writes=['c_xres%d' % j])
            rmsnorm_T(g, 'c_', xres[:, j, :], 'c_xres%d' % j, nwb, 'c_nwb', ub, junk, st, uT, 'c_uT', j * 128)
        for j in range(ncb):
            S.dma('pool', lambda e, j=j: e.indirect_dma_start(out=yrow[:], out_offset=None, in_=g.yall,
                                                              in_offset=bass.IndirectOffsetOnAxis(ap=rows[:, c0 + j:c0 + j + 1], axis=0)),
                  reads=['c_rows', 'd_yall'], writes=['c_yrow'])
            S.op('act', lambda e: e.activation(out=junk[:, 0:1024], in_=yrow[:, 0:1024], func=AF.Square, accum_out=st[:, 0:1]),
                 reads=['c_yrow'], writes=['c_junk', 'c_ss'])
            S.op('dve', lambda e: e.tensor_scalar(out=st[:, 1:2], in0=st[:, 0:1], scalar1=1.0 / 1024, scalar2=EPS, op0=ALU.mult, op1=ALU.add),
                 reads=['c_ss'], writes=['c_rs'])
            S.op('act', lambda e: e.activation(out=st[:, 1:2], in_=st[:, 1:2], func=AF.Sqrt), reads=['c_rs'], writes=['c_rs'])
            S.op('dve', lambda e: e.reciprocal(out=st[:, 1:2], in_=st[:, 1:2]), reads=['c_rs'], writes=['c_rs'])
            S.op('dve', lambda e: e.scalar_tensor_tensor(out=ybf[:, 0:1024], in0=yrow[:, 0:1024], scalar=st[:, 1:2], in1=ssdw[:], op0=ALU.mult, op1=ALU.mult),
                 reads=['c_yrow', 'c_rs', 'c_ssdw'], writes=['c_ybf'])
            S.op('act', lambda e: e.copy(out=ybf[:, 1024:3072], in_=yrow[:, 1024:3072]), reads=['c_yrow', 'c_ybf'], writes=['c_ybf'])
            transpose_to(g, ybf, 'c_ybf', 24, yT, 'c_yT', 0, j * 128)
        for jg in range(4):
            for br, wbn in enumerate(('w_branch_ssd', 'w_branch_mlstm', 'w_branch_diff')):
                wg, wgk = next_w(w_in, KC, OG + br * 2048 + jg * 512, 512)
                wb, wbk = next_w(wbn, 8, jg * 512, 512)
                for ti in range(4):
                    pa, pak = next_ps()
                    for kc in range(KC):
                        S.op('pe', lambda e, kc=kc, pa=pa, wg=wg, ti=ti: e.matmul(pa[:, 0:tb], lhsT=wg[:, kc, ti * 128:(ti + 1) * 128], rhs=uT[:, kc, 0:tb],
                                                                                 start=(kc == 0), stop=(kc == KC - 1)), reads=[wgk, 'c_uT'], writes=[pak])
                    pb_, pbk = next_ps()
                    for kc in range(8):
                        S.op('pe', lambda e, kc=kc, pb_=pb_, wb=wb, ti=ti, br=br: e.matmul(pb_[:, 0:tb], lhsT=wb[:, kc, ti * 128:(ti + 1) * 128], rhs=yT[:, br * 8 + kc, 0:tb],
                                                                                         start=(kc == 0), stop=(kc == 7)), reads=[wbk, 'c_yT'], writes=[pbk])
                    S.op('act', lambda e, pa=pa: e.activation(out=sg[:, 0:tb], in_=pa[:, 0:tb], func=AF.Sigmoid), reads=[pak], writes=['c_sg'])
                    if br == 0:
                        S.op('dve', lambda e, pb_=pb_, ti=ti: e.tensor_tensor(out=acc4[:, ti, 0:tb], in0=sg[:, 0:tb], in1=pb_[:, 0:tb], op=ALU.mult),
                             reads=['c_sg', pbk], writes=['c_acc%d' % ti])
                    else:
                        S.op('dve', lambda e, pb_=pb_: e.tensor_tensor(out=tmpm[:, 0:tb], in0=sg[:, 0:tb], in1=pb_[:, 0:tb], op=ALU.mult),
                             reads=['c_sg', pbk], writes=['c_tmpm'])
                        S.op('dve', lambda e, ti=ti: e.tensor_tensor(out=acc4[:, ti, 0:tb], in0=acc4[:, ti, 0:tb], in1=tmpm[:, 0:tb], op=ALU.add),
                             reads=['c_tmpm', 'c_acc%d' % ti], writes=['c_acc%d' % ti])
            for ti in range(4):
                S.op('act', lambda e, ti=ti, jg=jg: e.copy(out=mT[:, jg * 4 + ti, 0:tb], in_=acc4[:, ti, 0:tb]), reads=['c_acc%d' % ti], writes=['c_mT'])
        for cg in range(4):
            wo, wok = next_w('w_out', KC, cg * 512, 512)
            for j in range(ncb):
                pa, pak = next_ps()
                for kc in range(KC):
                    S.op('pe', lambda e, kc=kc, pa=pa, wo=wo, j=j: e.matmul(pa[:, 0:512], lhsT=mT[:, kc, j * 128:(j + 1) * 128], rhs=wo[:, kc, :],
                                                                           start=(kc == 0), stop=(kc == KC - 1)), reads=[wok, 'c_mT'], writes=[pak])
                S.op('dve', lambda e, pa=pa, j=j, cg=cg: e.tensor_tensor(out=xres[:, j, cg * 512:(cg + 1) * 512], in0=xres[:, j, cg * 512:(cg + 1) * 512],
                                                                        in1=pa[:, 0:512], op=ALU.add), reads=[pak, 'c_xres%d' % j], writes=['c_xres%d' % j])
        if g.dbg:
            for j in range(ncb):
                S.dma('sp', lambda e, j=j: e.dma_start(out=g.dbg_hmix[(c0 + j) * 128:(c0 + j + 1) * 128, :], in_=xres[:, j, :]), reads=['c_xres%d' % j], writes=['d_hmix'])
        S.dma('sp', lambda e: e.dma_start(out=nwb[:], in_=g.w['ffn_norm_w'].partition_broadcast(128)), reads=['c_nwb'], writes=['c_nwb'])
        for j in range(ncb):
            rmsnorm_T(g, 'c_h%d' % j, xres[:, j, :], 'c_xres%d' % j, nwb, 'c_nwb', h2b[:, j, :], junk, st, uT, 'c_uT', j * 128, jkey='c_junk', stkey='c_')
        for cg in range(4):
            wq, wqk = next_w('peer_w_q', KC, cg * 512, 512)
            for ti in range(4):
                hm = cg * 4 + ti
                pa, pak = next_ps()
                for kc in range(KC):
                    S.op('pe', lambda e, kc=kc, pa=pa, wq=wq, ti=ti: e.matmul(pa[:, 0:tb], lhsT=wq[:, kc, ti * 128:(ti + 1) * 128], rhs=uT[:, kc, 0:tb],
                                                                             start=(kc == 0), stop=(kc == KC - 1)), reads=[wqk, 'c_uT'], writes=[pak])
                S.op('act', lambda e, pa=pa: e.copy(out=qTs[:, 0:tb], in_=pa[:, 0:tb]), reads=[pak], writes=['c_qTs'])
                for j in range(ncb):
                    p2, p2k = next_ps()
                    S.op('pe', lambda e, p2=p2, j=j, hm=hm: e.matmul(p2[:, 0:128], lhsT=qTs[:, j * 128:(j + 1) * 128], rhs=skT[:, hm, :], start=True, stop=True),
                         reads=['c_qTs', 'c_skT'], writes=[p2k])
                    S.op('act', lambda e, p2=p2: e.copy(out=sc[:], in_=p2[:, 0:128]), reads=[p2k], writes=['c_sc'])
                    tk = 'c_top%d' % j
                    S.op('dve', lambda e, j=j, hm=hm: e.max(out=top[:, j, hm, 0:8], in_=sc[:]), reads=['c_sc'], writes=[tk])
                    S.op('dve', lambda e, j=j, hm=hm: e.max_index(out=topi[:, j, hm, 0:8], in_max=top[:, j, hm, 0:8], in_values=sc[:]), reads=['c_sc', tk], writes=[tk + 'i'])
                    S.op('dve', lambda e, j=j, hm=hm: e.match_replace(out=sc2[:], in_to_replace=top[:, j, hm, 0:8], in_values=sc[:], imm_value=-1e30),
                         reads=['c_sc', tk], writes=['c_sc2'])
                    S.op('dve', lambda e, j=j, hm=hm: e.max(out=top[:, j, hm, 8:16], in_=sc2[:]), reads=['c_sc2', tk], writes=[tk])
                    S.op('dve', lambda e, j=j, hm=hm: e.max_index(out=topi[:, j, hm, 8:16], in_max=top[:, j, hm, 8:16], in_values=sc2[:]), reads=['c_sc2', tk, tk + 'i'], writes=[tk + 'i'])
        for j in range(ncb):
            tk = 'c_top%d' % j
            S.op('dve', lambda e, j=j: e.tensor_copy(out=topf[:], in_=topi[:, j, :, :]), reads=[tk + 'i'], writes=['c_topf'])
            tv = top[:, j, :, :].rearrange("p (h m) k -> p h m k", m=2)
            for h in range(8):
                S.op('dve', lambda e, h=h, tv=tv: e.tensor_tensor(out=cand[:], in0=tv[:, h, 0, :].unsqueeze(2).to_broadcast([128, 16, 16]),
                                                                 in1=tv[:, h, 1, :].unsqueeze(1).to_broadcast([128, 16, 16]), op=ALU.add), reads=[tk], writes=['c_cand'])
                cf = cand[:].rearrange("p a b -> p (a b)")
                cf2 = cand2[:].rearrange("p a b -> p (a b)")
                S.op('dve', lambda e, h=h: e.max(out=best[:, h, 0:8], in_=cf), reads=['c_cand'], writes=['c_best'])
                S.op('dve', lambda e, h=h: e.max_index(out=bpos[:, h, 0:8], in_max=best[:, h, 0:8], in_values=cf), reads=['c_cand', 'c_best'], writes=['c_bpos'])
                S.op('dve', lambda e, h=h: e.match_replace(out=cf2, in_to_replace=best[:, h, 0:8], in_values=cf, imm_value=-1e30), reads=['c_cand', 'c_best'], writes=['c_cand2'])
                S.op('dve', lambda e, h=h: e.max(out=best[:, h, 8:16], in_=cf2), reads=['c_cand2', 'c_best'], writes=['c_best'])
                S.op('dve', lambda e, h=h: e.max_index(out=bpos[:, h, 8:16], in_max=best[:, h, 8:16], in_values=cf2), reads=['c_cand2', 'c_best', 'c_bpos'], writes=['c_bpos'])
            tf = topf[:].rearrange("p (h m) k -> p h m k", m=2)
            for m_, (opn, sval) in enumerate(((ALU.logical_shift_right, 4), (ALU.bitwise_and, 15))):
                S.op('dve', lambda e, opn=opn, sval=sval: e.tensor_single_scalar(out=bq[:], in_=bpos[:], scalar=sval, op=opn), reads=['c_bpos'], writes=['c_bq'])
                S.op('dve', lambda e: e.tensor_copy(out=af[:], in_=bq[:]), reads=['c_bq'], writes=['c_af'])
                S.op('dve', lambda e: e.tensor_tensor(out=oh[:], in0=af[:].unsqueeze(3).to_broadcast([128, 8, 16, 16]),
                                                       in1=cs['iota16'][:].unsqueeze(1).unsqueeze(1).to_broadcast([128, 8, 16, 16]), op=ALU.is_equal),
                     reads=['c_af', 'c_iota16'], writes=['c_oh'])
                S.op('dve', lambda e, m_=m_: e.tensor_tensor(out=oh[:], in0=oh[:], in1=tf[:, :, m_, :].unsqueeze(2).to_broadcast([128, 8, 16, 16]), op=ALU.mult),
                     reads=['c_oh', 'c_topf'], writes=['c_oh'])
                S.op('dve', lambda e, m_=m_: e.tensor_reduce(out=isel[:, m_, :, :], in_=oh[:], axis=AX.X, op=ALU.add), reads=['c_oh'], writes=['c_isel%d' % m_])
            S.op('dve', lambda e: e.scalar_tensor_tensor(out=idxf[:], in0=isel[:, 0, :, :].rearrange("p a b -> p (a b)"), scalar=128.0,
                                                          in1=isel[:, 1, :, :].rearrange("p a b -> p (a b)"), op0=ALU.mult, op1=ALU.add),
                 reads=['c_isel0', 'c_isel1'], writes=['c_idxf'])
            S.op('dve', lambda e: e.tensor_copy(out=idx[:], in_=idxf[:]), reads=['c_idxf'], writes=['c_idx'])
            S.op('dve', lambda e: e.tensor_tensor(out=gate[:], in0=best[:], in1=best[:, :, 0:1].to_broadcast([128, 8, 16]), op=ALU.subtract), reads=['c_best'], writes=['c_gate'])
            S.op('act', lambda e: e.activation(out=gate[:], in_=gate[:], func=AF.Exp), reads=['c_gate'], writes=['c_gate'])
            S.op('dve', lambda e: e.tensor_reduce(out=gsum[:], in_=gate[:], axis=AX.X, op=ALU.add), reads=['c_gate'], writes=['c_gsum'])
            S.op('dve', lambda e: e.reciprocal(out=gsum[:], in_=gsum[:]), reads=['c_gsum'], writes=['c_gsum'])
            S.op('dve', lambda e: e.tensor_tensor(out=gate[:], in0=gate[:], in1=gsum[:].unsqueeze(2).to_broadcast([128, 8, 16]), op=ALU.mult), reads=['c_gate', 'c_gsum'], writes=['c_gate'])
            gk = 'c_gate'

            def emit_v(sl, b_):
                d_ = sl % 2
                S.op('dve', lambda e: e.tensor_scalar(out=dg[d_][:], in0=cs['ident_bf'][:], scalar1=g1[:, sl:sl + 1], scalar2=gate[:].rearrange("p a b -> p (a b)")[:, sl:sl + 1],
                                                       op0=ALU.mult, op1=ALU.mult), reads=['c_ident_bf', 'c_g1_%d' % (sl % 4), gk], writes=['c_dg%d' % d_])
                for q in range(4):
                    S.op('pe', lambda e, q=q: e.matmul(ps[q][:, 0:512], lhsT=dg[d_][:], rhs=ug[b_][:, D + q * 512:D + (q + 1) * 512], start=(sl == 0), stop=(sl == 127)),
                         reads=['c_dg%d' % d_, ugk[b_]], writes=['ps%d' % q])

            prev = None
            for sl in range(128):
                b_ = gi % NG
                gi += 1
                S.dma('pool', lambda e, b_=b_, sl=sl: e.indirect_dma_start(out=ug[b_][:, :], out_offset=None, in_=g.uvb,
                                                                           in_offset=bass.IndirectOffsetOnAxis(ap=idx[:, sl:sl + 1], axis=0)),
                      reads=['c_idx', 'd_uvb_%d' % g.lid], writes=[ugk[b_]])
                S.op('dve', lambda e, b_=b_, sl=sl, j=j: e.scalar_tensor_tensor(out=junk[:], in0=ug[b_][:, 0:D], scalar=1.0, in1=h2b[:, j, :], op0=ALU.mult, op1=ALU.mult,
                                                                                accum_out=pre[:, sl:sl + 1]),
                     reads=[ugk[b_], 'c_h%dub' % j], writes=['c_junk', 'c_pre_%d' % (sl % 4)])
                S.op('act', lambda e, sl=sl: e.activation(out=g1[:, sl:sl + 1], in_=pre[:, sl:sl + 1], func=AF.Gelu), reads=['c_pre_%d' % (sl % 4)], writes=['c_g1_%d' % (sl % 4)])
                if prev is not None:
                    emit_v(*prev)
                prev = (sl, b_)
            emit_v(*prev)
            for q in range(4):
                S.op('dve', lambda e, q=q, j=j: e.tensor_tensor(out=xres[:, j, q * 512:(q + 1) * 512], in0=xres[:, j, q * 512:(q + 1) * 512], in1=ps[q][:, 0:512], op=ALU.add),
                     reads=['ps%d' % q, 'c_xres%d' % j], writes=['c_xres%d' % j])
            if g.final:
                S.dma('sp', lambda e: e.dma_start(out=nwb[:], in_=g.fnw.partition_broadcast(128)), reads=['c_nwb'], writes=['c_nwb'])
                S.op('act', lambda e, j=j: e.activation(out=junk[:], in_=xres[:, j, :], func=AF.Square, accum_out=st[:, 0:1]), reads=['c_xres%d' % j], writes=['c_junk', 'c_ss'])
                S.op('dve', lambda e: e.tensor_scalar(out=st[:, 1:2], in0=st[:, 0:1], scalar1=1.0 / D, scalar2=EPS, op0=ALU.mult, op1=ALU.add), reads=['c_ss'], writes=['c_rs'])
                S.op('act', lambda e: e.activation(out=st[:, 1:2], in_=st[:, 1:2], func=AF.Sqrt), reads=['c_rs'], writes=['c_rs'])
                S.op('dve', lambda e: e.reciprocal(out=st[:, 1:2], in_=st[:, 1:2]), reads=['c_rs'], writes=['c_rs'])
                S.op('dve', lambda e, j=j: e.scalar_tensor_tensor(out=xres[:, j, :], in0=xres[:, j, :], scalar=st[:, 1:2], in1=nwb[:], op0=ALU.mult, op1=ALU.mult),
                     reads=['c_xres%d' % j, 'c_rs', 'c_nwb'], writes=['c_xres%d' % j])
            else:
                S.op('dve', lambda e, j=j: e.tensor_scalar(out=xres[:, j, :], in0=xres[:, j, :], scalar1=rmask[:, c0 + j:c0 + j + 1], scalar2=None, op0=ALU.mult),
                     reads=['c_xres%d' % j, 'c_rmask'], writes=['c_xres%d' % j])
            S.dma('sp', lambda e, j=j: e.dma_start(out=g.out[(c0 + j) * 128:(c0 + j + 1) * 128, :], in_=xres[:, j, :]), reads=['c_xres%d' % j], writes=[g.outkey])


def make_in_maps(inp, layers, NCH, NCC, names=None):
    T = NCH * 128
    consts = host_consts()
    maps = []
    for core in range(8):
        b, s = core // 2, core % 2
        m = {}
        xp = np.zeros((T, D), np.float32)
        xp[PADN:PADN + META] = inp['meta_tokens']
        xp[128:] = inp['x'][b, :T - 128]
        m['xpad'] = xp
        pp = np.zeros((T,), np.int32)
        pp[PADN:PADN + META] = np.arange(META, dtype=np.int32) - META
        pp[128:] = inp['positions'][b, :T - 128]
        m['pos'] = np.ascontiguousarray(pp.reshape(NCH, 128).T)
        c0 = 0 if s == 0 else NCH - NCC
        rows = ((c0 + np.arange(NCC))[None, :] * 128 + np.arange(128)[:, None]).astype(np.int32)
        m['rows'] = np.ascontiguousarray(rows)
        m['rowmask'] = np.ascontiguousarray((rows >= PADN).astype(np.float32))
        rows_f = (np.arange(NCH)[None, :] * 128 + np.arange(128)[:, None]).astype(np.int32)
        m['rows_full'] = np.ascontiguousarray(rows_f)
        m['rowmask_full'] = np.ascontiguousarray((rows_f >= PADN).astype(np.float32))
        for l in layers:
            for n in W_SPECS:
                if names is None or (n, l) in names:
                    m['%s_l%d' % (n, l)] = np.ascontiguousarray(inp[n][l])
        m['final_norm_w'] = np.ascontiguousarray(inp['final_norm_w'])
        m.update(consts)
        maps.append(m)
    return maps


def kernel(**inputs):
    inp = {k: np.asarray(v) for k, v in inputs.items()}
    B, SEQ = inp['x'].shape[0], inp['x'].shape[1]
    NCH = (SEQ + 128) // 128
    NCC = (NCH + 1) // 2
    T = NCH * 128
    layers = [0, 1]
    nc, g = build(layers, NCH, NCC)
    in_maps = make_in_maps(inp, layers, NCH, NCC, names=g.in_names)
    res = run_bass_kernel_spmd(nc, in_maps, core_ids=list(range(8)))
    full = np.zeros((B, T, D), np.float32)
    for core in range(8):
        b, s = core // 2, core % 2
        c0 = 0 if s == 0 else NCH - NCC
        full[b, c0 * 128:(c0 + NCC) * 128] = np.asarray(res.results[core]['xout'])
    return np.ascontiguousarray(full[:, 128:, :])
```

```python
import math
from contextlib import ExitStack
import numpy as np
import ml_dtypes
import concourse.bass as bass
import concourse.mybir as mybir
from concourse.bass_utils import run_bass_kernel_spmd

F32 = mybir.dt.float32
BF16 = mybir.dt.bfloat16
I32 = mybir.dt.int32
U32 = mybir.dt.uint32
AF = mybir.ActivationFunctionType
ALU = mybir.AluOpType
AX = mybir.AxisListType

D = 2048
KC = 16
NIN = 14872
OZ, OXBC, ODT, OMQ, OMK, OMV, OMI, OMF, OMO, OAQ, OAK, OAV, OG = (
    0, 1024, 2560, 2576, 3088, 3600, 4624, 4628, 4632, 5656, 6680, 7704, 8728)
EPS = 1e-6
NEGM = -30000.0
META = 16
PADN = 112
NDS = 24


class Sched:
    def __init__(self, nc):
        self.nc = nc
        self.eng = {'pe': nc.tensor, 'act': nc.scalar, 'dve': nc.vector, 'pool': nc.gpsimd, 'sp': nc.sync}
        self.sem = {k: nc.alloc_semaphore('sem_' + k) for k in self.eng}
        self.cnt = {k: 0 for k in self.eng}
        self.seen = {k: {} for k in self.eng}
        self.dsem = [nc.alloc_semaphore('dsem%d' % i) for i in range(NDS)]
        self.dcnt = [0] * NDS
        self.dnext = 0
        self.bufs = {}
        self.nins = 0

    def _deps(self, reads, writes):
        deps = {}
        for k in reads:
            b = self.bufs.get(k)
            if b and b['w']:
                s, v = b['w']
                deps[s] = max(deps.get(s, 0), v)
        for k in writes:
            b = self.bufs.get(k)
            if b:
                if b['w']:
                    s, v = b['w']
                    deps[s] = max(deps.get(s, 0), v)
                for s, v in b['r'].items():
                    deps[s] = max(deps.get(s, 0), v)
        return deps

    def _wait(self, e, deps):
        for src, val in deps.items():
            if e == 'pe' and src == 'pe':
                continue
            if self.seen[e].get(src, 0) >= val:
                continue
            sem = self.sem[src] if isinstance(src, str) else self.dsem[src[1]]
            self.eng[e].wait_ge(sem, val)
            self.seen[e][src] = val

    def _mark(self, reads, writes, src, val):
        for k in writes:
            self.bufs[k] = {'w': (src, val), 'r': {}}
        for k in reads:
            b = self.bufs.setdefault(k, {'w': None, 'r': {}})
            b['r'][src] = max(b['r'].get(src, 0), val)

    def op(self, e, fn, reads=(), writes=()):
        self._wait(e, self._deps(reads, writes))
        ins = fn(self.eng[e])
        self.cnt[e] += 1
        ins.then_inc(self.sem[e], 1)
        self._mark(reads, writes, e, self.cnt[e])
        self.nins += 1

    def dma(self, q, fn, reads=(), writes=()):
        if q == 'sp' and any(isinstance(k, str) and k.startswith('d_') for k in writes):
            q = 'pool'
        deps = self._deps(reads, writes)
        i = self.dnext
        self.dnext = (i + 1) % NDS
        if self.dcnt[i] > 0:
            deps[('d', i)] = max(deps.get(('d', i), 0), 16 * self.dcnt[i])
        self._wait(q, deps)
        ins = fn(self.eng[q])
        self.dcnt[i] += 1
        ins.then_inc(self.dsem[i], 16)
        self._mark(reads, writes, ('d', i), 16 * self.dcnt[i])
        self.nins += 1

    def barrier(self):
        for e in self.eng:
            deps = {}
            for k in self.eng:
                if k != e and self.cnt[k]:
                    deps[k] = self.cnt[k]
            for i in range(NDS):
                if self.dcnt[i]:
                    deps[('d', i)] = 16 * self.dcnt[i]
            if self.cnt[e]:
                deps[e] = self.cnt[e]
            self._wait(e, deps)

    def finish(self):
        sp = self.eng['sp']
        for i in range(NDS):
            if self.dcnt[i]:
                sp.wait_ge(self.dsem[i], 16 * self.dcnt[i])
        for k in self.eng:
            if self.cnt[k]:
                sp.wait_ge(self.sem[k], self.cnt[k])


def host_consts():
    c = {}
    c['ident_bf'] = np.eye(128, dtype=np.float32).astype(ml_dtypes.bfloat16)
    c['ident_f'] = np.eye(128, dtype=np.float32)
    r = np.arange(128)
    c['utri'] = (r[:, None] <= r[None, :]).astype(np.float32)
    c['ones_f'] = np.ones((128, 128), np.float32)
    c['ones_bf'] = np.ones((128, 128), np.float32).astype(ml_dtypes.bfloat16)
    nm = np.where(r[None, :] >= r[:, None], 0.0, NEGM).astype(np.float32)
    c['negmask'] = nm
    c['negmask8'] = np.tile(nm, (1, 8)).astype(np.float32)
    cm = np.zeros((4, 128, 512), np.float32)
    for v in range(4):
        q = np.arange(512)[None, :]
        k = (v * 128 + r)[:, None]
        cm[v] = (q >= k).astype(np.float32)
    c['cmask'] = cm.astype(ml_dtypes.bfloat16)
    inv = (500000.0 ** (-np.arange(0, 16, 2, dtype=np.float32) / 16)).astype(np.float32)
    c['invfreq'] = np.tile(inv[None, :], (128, 1)).astype(np.float32)
    c['iota16'] = np.tile(np.arange(16, dtype=np.float32)[None, :], (128, 1))
    return c


CONST_SPECS = {
    'ident_bf': ([128, 128], BF16), 'ident_f': ([128, 128], F32), 'utri': ([128, 128], F32),
    'ones_f': ([128, 128], F32), 'ones_bf': ([128, 128], BF16), 'negmask': ([128, 128], F32),
    'negmask8': ([128, 1024], F32), 'cmask': ([4, 128, 512], BF16), 'invfreq': ([128, 8], F32),
    'iota16': ([128, 16], F32),
}

W_SPECS = {
    'mix_norm_w': [D], 'w_in': [D, NIN], 'ssd_conv_w': [4, 1536], 'ssd_conv_b': [1536],
    'ssd_dt_bias': [16], 'ssd_a_log': [16], 'ssd_d': [16], 'ssd_norm_w': [1024],
    'mlstm_i_bias': [4], 'mlstm_f_bias': [4], 'mlstm_norm_w': [1024],
    'diff_lambda_q1': [64], 'diff_lambda_k1': [64], 'diff_lambda_q2': [64], 'diff_lambda_k2': [64],
    'diff_norm_w': [128], 'w_branch_ssd': [1024, D], 'w_branch_mlstm': [1024, D],
    'w_branch_diff': [1024, D], 'w_out': [D, D], 'ffn_norm_w': [D], 'peer_w_q': [D, D],
    'peer_sub_keys': [2, 8, 128, 128], 'peer_u': [16384, D], 'peer_v': [16384, D],
}


class K:
    pass


def build(layers, NCH, NCC, dbg=False, do_b=True, do_c=True, stages=None):
    nc = bass.Bass("TRN2", target_bir_lowering=False)
    S = Sched(nc)
    T = NCH * 128
    g = K()
    g.nc, g.S, g.T, g.NCH, g.dbg = nc, S, T, NCH, dbg

    def din(name, shape, dt=F32):
        return nc.dram_tensor(name, shape, dt, kind="ExternalInput").ap()

    xin = din('xpad', [T, D])
    g.pos = din('pos', [128, NCH], I32)
    rows_half = din('rows', [128, NCC], I32)
    rmask_half = din('rowmask', [128, NCC])
    rows_full = din('rows_full', [128, NCH], I32)
    rmask_full = din('rowmask_full', [128, NCH])
    wl = {}
    g.in_names = set()
    for l in layers:
        wl[l] = {}
        for n, s in W_SPECS.items():
            if do_c or not n.startswith('peer_'):
                wl[l][n] = din('%s_l%d' % (n, l), s)
                g.in_names.add((n, l))
    g.fnw = din('final_norm_w', [D])
    g.c = {n: din(n, s, dt) for n, (s, dt) in CONST_SPECS.items()}
    xout = nc.dram_tensor('xout', [NCC * 128, D], F32, kind="ExternalOutput").ap()
    sk = "ExternalOutput" if dbg else "Internal"

    def dscr(name, shape, dt, kind=None):
        return nc.dram_tensor(name, shape, dt, kind=kind or sk).ap()

    g.xbct = dscr('s_xbct', [1536, T], BF16)
    g.xstok = dscr('s_xstok', [T, 1280], BF16)
    g.zs = dscr('s_zs', [T, 1024], BF16)
    g.dtla = dscr('s_dtla', [T, 32], F32)
    g.qtml = dscr('s_qtml', [512, T], BF16)
    g.ktml = dscr('s_ktml', [512, T], BF16)
    g.ktok = dscr('s_ktok', [T, 512], BF16)
    g.vml = dscr('s_vml', [T, 1024], BF16)
    g.og = dscr('s_og', [T, 1024], BF16)
    g.gates = dscr('s_gates', [8, T], F32)
    g.qa = dscr('s_qa', [16, 66, T], BF16)
    g.ka = dscr('s_ka', [16, 66, T], BF16)
    g.vda = dscr('s_vda', [T, 1024], BF16)
    g.yall = dscr('s_yall', [T, 3072], F32)
    xmid = [dscr('s_xmid%d' % i, [T, D], F32, kind="Internal") for i in range(max(0, len(layers) - 1))]
    if dbg:
        g.dbg_hmix = dscr('s_hmix', [T, D], F32)

    g.ps = [nc.alloc_psum_tensor('ps%d' % i, [128, 512], F32) for i in range(6)]
    g.pb = [nc.alloc_psum_tensor('pb%d' % i, [128, 1024], BF16) for i in range(2)]

    g.cs = {}
    for n, (s, dt) in CONST_SPECS.items():
        if n == 'cmask':
            t = nc.alloc_sbuf_tensor('c_' + n, [128, 4, 512], dt)
            S.dma('sp', lambda e, t=t, n=n: e.dma_start(out=t[:], in_=g.c[n].rearrange("v p q -> p v q")), writes=['c_' + n])
        else:
            t = nc.alloc_sbuf_tensor('c_' + n, s, dt)
            S.dma('sp', lambda e, t=t, n=n: e.dma_start(out=t[:], in_=g.c[n]), writes=['c_' + n])
        g.cs[n] = t

    CASTW = ('w_in', 'w_branch_ssd', 'w_branch_mlstm', 'w_branch_diff', 'w_out', 'peer_w_q')
    wbl = {}
    for l in layers:
        wbl[l] = {}
        for n in CASTW:
            if n not in wl[l]:
                continue
            shp = W_SPECS[n]
            wb_ap = nc.dram_tensor('wb_%s_l%d' % (n, l), shp, BF16, kind="Internal").ap()
            wbl[l][n] = wb_ap
            for r0 in range(0, shp[0], 256):
                S.dma('pool', lambda e, wb_ap=wb_ap, n=n, l=l, r0=r0: e.dma_start(out=wb_ap[r0:r0 + 256, :], in_=wl[l][n][r0:r0 + 256, :]),
                      reads=['d_wb_%s_%d' % (n, l)] if r0 else [], writes=['d_wb_%s_%d' % (n, l)])

    uvl = {}
    if do_c:
        for l in layers:
            uv = nc.dram_tensor('uvb_l%d' % l, [16384, 2 * D], BF16, kind="Internal").ap()
            uvl[l] = uv
            first = True
            for half, n in enumerate(('peer_u', 'peer_v')):
                for r0 in range(0, 16384, 2048):
                    S.dma('pool', lambda e, uv=uv, n=n, l=l, r0=r0, half=half: e.dma_start(out=uv[r0:r0 + 2048, half * D:(half + 1) * D], in_=wl[l][n][r0:r0 + 2048, :]),
                          reads=[] if first else ['d_uvb_%d' % l], writes=['d_uvb_%d' % l])
                    first = False

    uid = [0]

    def run_stage(fn):
        with ExitStack() as es:
            uid[0] += 1
            g.A = lambda name, shape, dt, u=uid[0]: es.enter_context(nc.sbuf_tensor('%s_u%d' % (name, u), shape, dt))
            fn(g)
            S.barrier()

    for li, l in enumerate(layers):
        last = (li == len(layers) - 1)
        g.w = wl[l]
        g.wb = wbl[l]
        g.uvb = uvl.get(l)
        g.lid = l
        g.lam_init = 0.8 - 0.6 * math.exp(-0.3 * l)
        g.x = xin if li == 0 else xmid[li - 1]
        g.xkey = 'd_x%d' % li
        g.final = last and (l == 1)
        if last:
            g.NCC, g.rows, g.rowmask, g.out, g.outkey = NCC, rows_half, rmask_half, xout, 'd_out'
        else:
            g.NCC, g.rows, g.rowmask, g.out, g.outkey = NCH, rows_full, rmask_full, xmid[li], 'd_x%d' % (li + 1)
        if do_b:
            for fn in (stage_p, stage_ssd, stage_mlstm, stage_attn):
                if stages is None or fn.__name__ in stages:
                    run_stage(fn)
        if do_c:
            run_stage(phase_c)
    S.finish()
    return nc, g


def bc_load(g, name, src_ap, n, dt=F32, q='sp'):
    t = g.A(name, [128, n], dt)
    g.S.dma(q, lambda e: e.dma_start(out=t[:], in_=src_ap.partition_broadcast(128)), writes=[name])
    return t


def rmsnorm_T(g, pfx, x_tile, xkey, nw_bc, nwkey, ub, junk, st, uT, uTkey, col0, want_f32=None, jkey=None, stkey=None):
    S = g.S
    ss, rs = st[:, 0:1], st[:, 1:2]
    S.op('act', lambda e: e.activation(out=junk[:], in_=x_tile, func=AF.Square, accum_out=ss),
         reads=[xkey], writes=[jkey or (pfx + 'junk'), (stkey or pfx) + 'ss'])
    S.op('dve', lambda e: e.tensor_scalar(out=rs, in0=ss, scalar1=1.0 / D, scalar2=EPS, op0=ALU.mult, op1=ALU.add),
         reads=[(stkey or pfx) + 'ss'], writes=[(stkey or pfx) + 'rs'])
    S.op('act', lambda e: e.activation(out=rs, in_=rs, func=AF.Sqrt), reads=[(stkey or pfx) + 'rs'], writes=[(stkey or pfx) + 'rs'])
    S.op('dve', lambda e: e.reciprocal(out=rs, in_=rs), reads=[(stkey or pfx) + 'rs'], writes=[(stkey or pfx) + 'rs'])
    if want_f32 is not None:
        f_ap, fkey = want_f32
        S.op('dve', lambda e: e.scalar_tensor_tensor(out=f_ap, in0=x_tile, scalar=rs, in1=nw_bc[:], op0=ALU.mult, op1=ALU.mult),
             reads=[xkey, (stkey or pfx) + 'rs', nwkey], writes=[fkey])
        S.op('act', lambda e: e.copy(out=ub[:], in_=f_ap), reads=[fkey], writes=[pfx + 'ub'])
    else:
        S.op('dve', lambda e: e.scalar_tensor_tensor(out=ub[:], in0=x_tile, scalar=rs, in1=nw_bc[:], op0=ALU.mult, op1=ALU.mult),
             reads=[xkey, (stkey or pfx) + 'rs', nwkey], writes=[pfx + 'ub'])
    transpose_to(g, ub, pfx + 'ub', 16, uT, uTkey, 0, col0)


def transpose_to(g, src, srckey, ntile, dstT, dstkey, kc0, col0):
    S = g.S
    for h in range(0, ntile, 8):
        n = min(8, ntile - h)
        pb = g.pb[(h // 8) % 2]
        pbk = 'pb%d' % ((h // 8) % 2)
        for i in range(n):
            S.op('pe', lambda e, i=i: e.transpose(out=pb[:, i * 128:(i + 1) * 128], in_=src[:, (h + i) * 128:(h + i + 1) * 128],
                                                   identity=g.cs['ident_bf'][:]),
                 reads=[srckey, 'c_ident_bf'], writes=[pbk])
        eng = 'act' if (h // 8) % 2 == 0 else 'dve'
        o = dstT[:, kc0 + h:kc0 + h + n, col0:col0 + 128]
        i_ = pb[:, 0:n * 128].rearrange("p (a b) -> p a b", b=128)
        if eng == 'act':
            S.op('act', lambda e: e.copy(out=o, in_=i_), reads=[pbk], writes=[dstkey])
        else:
            S.op('dve', lambda e: e.tensor_copy(out=o, in_=i_), reads=[pbk], writes=[dstkey])


def load_w(g, wt, wkey, src, r0, nk, c0, ncols):
    ap = g.wb[src][r0:r0 + nk * 128, c0:c0 + ncols].rearrange("(kc p) c -> p kc c", p=128)
    g.S.dma('sp', lambda e: e.dma_start(out=wt[:, 0:nk, 0:ncols], in_=ap), reads=['d_wb_%s_%d' % (src, g.lid)], writes=[wkey])


def stage_p(g):
    nc, S, T, NCH = g.nc, g.S, g.T, g.NCH
    cs = g.cs
    A = g.A
    w_in = 'w_in'
    xst = A('p_xst', [128, 1280], BF16)
    TB = 512
    nw_bc = bc_load(g, 'p_nw', g.w['mix_norm_w'], D)
    xt = [A('p_xt%d' % i, [128, D], F32) for i in range(2)]
    ub = A('p_ub', [128, D], BF16)
    junk = A('p_junk', [128, D], BF16)
    st = A('p_st', [128, 2], F32)
    uT = A('p_uT', [128, KC, TB], BF16)
    wt = [A('p_wt%d' % i, [128, KC, 512], BF16) for i in range(2)]
    ev = [A('p_ev%d' % i, [128, 1024], F32) for i in range(2)]
    eb = [A('p_eb%d' % i, [128, 1024], BF16) for i in range(2)]
    cw = A('p_cw', [128, 4, 12], F32)
    cb = A('p_cb', [128, 12], F32)
    for j_ in range(4):
        S.dma('sp', lambda e, j_=j_: e.dma_start(out=cw[:, j_, :], in_=g.w['ssd_conv_w'][j_, :].rearrange("(t p) -> p t", p=128),
                                                 allow_slow_non_contiguous=True), reads=['p_cw'] if j_ else [], writes=['p_cw'])
    S.dma('sp', lambda e: e.dma_start(out=cb[:], in_=g.w['ssd_conv_b'].rearrange("(t p) -> p t", p=128),
                                      allow_slow_non_contiguous=True), writes=['p_cb'])
    xraw = A('p_xraw', [128, 12, 3 + TB], F32)
    S.op('pool', lambda e: e.memset(xraw[:], 0.0), writes=['p_xraw'])
    xact = A('p_xact', [128, 12, TB], BF16)
    cacc = A('p_cacc', [128, TB], F32)
    dtb = bc_load(g, 'p_dtb', g.w['ssd_dt_bias'], 16)
    alog = bc_load(g, 'p_alog', g.w['ssd_a_log'], 16)
    negA = A('p_negA', [128, 16], F32)
    S.op('act', lambda e: e.activation(out=negA[:], in_=alog[:], func=AF.Exp), reads=['p_alog'], writes=['p_negA'])
    S.op('dve', lambda e: e.tensor_scalar(out=negA[:], in0=negA[:], scalar1=-1.0, scalar2=None, op0=ALU.mult),
         reads=['p_negA'], writes=['p_negA'])
    dts = A('p_dts', [128, 4, 16], F32)
    ib = A('p_ib', [4, 1], F32)
    fb = A('p_fb', [4, 1], F32)
    S.dma('sp', lambda e: e.dma_start(out=ib[:], in_=g.w['mlstm_i_bias'].rearrange("(p o) -> p o", o=1)), writes=['p_ib'])
    S.dma('sp', lambda e: e.dma_start(out=fb[:], in_=g.w['mlstm_f_bias'].rearrange("(p o) -> p o", o=1)), writes=['p_fb'])
    S.op('dve', lambda e: e.tensor_scalar(out=fb[:], in0=fb[:], scalar1=-1.0, scalar2=None, op0=ALU.mult),
         reads=['p_fb'], writes=['p_fb'])
    gt = A('p_gt', [4, 2, TB], F32)
    posi = A('p_posi', [128, NCH], I32)
    S.dma('sp', lambda e: e.dma_start(out=posi[:], in_=g.pos), writes=['p_posi'])
    posf = A('p_posf', [128, NCH], F32)
    S.op('dve', lambda e: e.tensor_copy(out=posf[:], in_=posi[:]), reads=['p_posi'], writes=['p_posf'])
    S.op('dve', lambda e: e.tensor_scalar(out=posf[:], in0=posf[:], scalar1=float(META), scalar2=None, op0=ALU.add),
         reads=['p_posf'], writes=['p_posf'])
    ang = A('p_ang', [128, NCH, 8], F32)
    S.op('dve', lambda e: e.tensor_tensor(out=ang[:], in0=posf[:].unsqueeze(2).to_broadcast([128, NCH, 8]),
                                           in1=cs['invfreq'][:].unsqueeze(1).to_broadcast([128, NCH, 8]), op=ALU.mult),
         reads=['p_posf', 'c_invfreq'], writes=['p_ang'])
    cosT = A('p_cos', [128, NCH, 8], F32)
    sinT = A('p_sin', [128, NCH, 8], F32)
    rr = A('p_rr', [128, NCH, 8], F32)
    rk = A('p_rk', [128, NCH, 8], F32)
    rki = A('p_rki', [128, NCH, 8], I32)
    TWO_PI = 2.0 * math.pi
    C1 = 6.28125
    C2 = TWO_PI - C1
    for shift, dst, dk in ((0.0, sinT, 'p_sin'), (math.pi / 2, cosT, 'p_cos')):
        S.op('dve', lambda e, shift=shift: e.tensor_scalar(out=rr[:], in0=ang[:], scalar1=shift, scalar2=None, op0=ALU.add),
             reads=['p_ang'], writes=['p_rr'])
        S.op('dve', lambda e: e.tensor_scalar(out=rk[:], in0=rr[:], scalar1=1.0 / TWO_PI, scalar2=None, op0=ALU.mult),
             reads=['p_rr'], writes=['p_rk'])
        S.op('dve', lambda e: e.tensor_copy(out=rki[:], in_=rk[:]), reads=['p_rk'], writes=['p_rki'])
        S.op('dve', lambda e: e.tensor_copy(out=rk[:], in_=rki[:]), reads=['p_rki'], writes=['p_rk'])
        S.op('dve', lambda e: e.scalar_tensor_tensor(out=rr[:], in0=rk[:], scalar=-C1, in1=rr[:], op0=ALU.mult, op1=ALU.add),
             reads=['p_rk', 'p_rr'], writes=['p_rr'])
        S.op('dve', lambda e: e.scalar_tensor_tensor(out=rr[:], in0=rk[:], scalar=-C2, in1=rr[:], op0=ALU.mult, op1=ALU.add),
             reads=['p_rk', 'p_rr'], writes=['p_rr'])
        S.op('dve', lambda e: e.tensor_scalar(out=rk[:], in0=rr[:], scalar1=math.pi, scalar2=-TWO_PI, op0=ALU.is_gt, op1=ALU.mult),
             reads=['p_rr'], writes=['p_rk'])
        S.op('dve', lambda e: e.tensor_tensor(out=rr[:], in0=rr[:], in1=rk[:], op=ALU.add), reads=['p_rr', 'p_rk'], writes=['p_rr'])
        S.op('dve', lambda e: e.tensor_scalar(out=rk[:], in0=rr[:], scalar1=-math.pi, scalar2=TWO_PI, op0=ALU.is_lt, op1=ALU.mult),
             reads=['p_rr'], writes=['p_rk'])
        S.op('dve', lambda e: e.tensor_tensor(out=rr[:], in0=rr[:], in1=rk[:], op=ALU.add), reads=['p_rr', 'p_rk'], writes=['p_rr'])
        S.op('dve', lambda e: e.tensor_scalar(out=rr[:], in0=rr[:], scalar1=3.14159, scalar2=-3.14159, op0=ALU.min, op1=ALU.max),
             reads=['p_rr'], writes=['p_rr'])
        S.op('act', lambda e, dst=dst: e.activation(out=dst[:], in_=rr[:], func=AF.Sin), reads=['p_rr'], writes=[dk])
    qk = A('p_qk', [128, 16, 66], F32)
    qkb = A('p_qkb', [128, 16, 66], BF16)
    r1 = A('p_r1', [128, 16, 8], F32)
    r2 = A('p_r2', [128, 16, 8], F32)
    r3 = A('p_r3', [128, 16, 8], F32)
    kn2 = A('p_kn2', [128, 16], F32)
    kmaxc = A('p_kmaxc', [128, 16], F32)
    S.op('pool', lambda e: e.memset(kmaxc[:], 0.0), writes=['p_kmaxc'])
    padk = A('p_padk', [128, 1], F32)
    S.op('pool', lambda e: e.memset(padk[:], 0.0), writes=['p_padk'])
    S.op('pool', lambda e: e.memset(padk[0:96, :], NEGM), reads=['p_padk'], writes=['p_padk'])
    S.op('pool', lambda e: e.memset(padk[96:112, :], NEGM), reads=['p_padk'], writes=['p_padk'])
    qaT = A('p_qaT', [66, 16, TB], BF16)

    nblk = (NCH + 3) // 4
    for bi in range(nblk):
        c0 = bi * 4
        ncb = min(4, NCH - c0)
        tb = ncb * 128
        t0 = c0 * 128
        for j in range(ncb):
            xtile = xt[j % 2]
            xkey = 'p_xt%d' % (j % 2)
            S.dma('sp', lambda e, xtile=xtile, j=j: e.dma_start(out=xtile[:], in_=g.x[t0 + j * 128:t0 + (j + 1) * 128, :]), reads=[g.xkey], writes=[xkey])
            rmsnorm_T(g, 'p_', xtile[:], xkey, nw_bc, 'p_nw', ub, junk, st, uT, 'p_uT', j * 128)
        wi = [0]

        def next_w(c0_, ncols):
            i = wi[0] % 2
            wi[0] += 1
            load_w(g, wt[i], 'p_wt%d' % i, w_in, 0, KC, c0_, ncols)
            return wt[i], 'p_wt%d' % i

        pidx = [0]

        def next_ps():
            i = pidx[0] % 4
            pidx[0] += 1
            return g.ps[i], 'ps%d' % i

        def proj_F(w, wkey, wc0, ncol):
            ps, pk = next_ps()
            for kc in range(KC):
                S.op('pe', lambda e, kc=kc: e.matmul(ps[0:ncol, 0:tb], lhsT=w[:, kc, wc0:wc0 + ncol], rhs=uT[:, kc, 0:tb],
                                                     start=(kc == 0), stop=(kc == KC - 1)),
                     reads=[wkey, 'p_uT'], writes=[pk])
            return ps, pk

        def proj_T(w, wkey, j, ncol):
            ps, pk = next_ps()
            for kc in range(KC):
                S.op('pe', lambda e, kc=kc: e.matmul(ps[:, 0:ncol], lhsT=uT[:, kc, j * 128:(j + 1) * 128], rhs=w[:, kc, 0:ncol],
                                                     start=(kc == 0), stop=(kc == KC - 1)),
                     reads=[wkey, 'p_uT'], writes=[pk])
            return ps, pk

        for grp in range(3):
            w, wkey = next_w(OXBC + grp * 512, 512)
            for ti in range(4):
                tt = grp * 4 + ti
                ps, pk = proj_F(w, wkey, ti * 128, 128)
                S.op('act', lambda e, tt=tt, ps=ps: e.copy(out=xraw[:, tt, 3:3 + tb], in_=ps[:, 0:tb]), reads=[pk], writes=['p_xraw'])
                S.op('dve', lambda e, tt=tt: e.tensor_scalar(out=cacc[:, 0:tb], in0=xraw[:, tt, 3:3 + tb], scalar1=cw[:, 3, tt:tt + 1],
                                                              scalar2=cb[:, tt:tt + 1], op0=ALU.mult, op1=ALU.add),
                     reads=['p_xraw', 'p_cw', 'p_cb'], writes=['p_cacc'])
                for jj in range(3):
                    S.op('dve', lambda e, tt=tt, jj=jj: e.scalar_tensor_tensor(out=cacc[:, 0:tb], in0=xraw[:, tt, jj:jj + tb],
                                                                                scalar=cw[:, jj, tt:tt + 1], in1=cacc[:, 0:tb],
                                                                                op0=ALU.mult, op1=ALU.add),
                         reads=['p_xraw', 'p_cw', 'p_cacc'], writes=['p_cacc'])
                S.op('act', lambda e, tt=tt: e.activation(out=xact[:, tt, 0:tb], in_=cacc[:, 0:tb], func=AF.Silu),
                     reads=['p_cacc'], writes=['p_xact'])
                S.op('pool', lambda e, tt=tt: e.tensor_copy(out=xraw[:, tt, 0:3], in_=xraw[:, tt, tb:tb + 3]),
                     reads=['p_xraw'], writes=['p_xraw'])
        if bi == 0:
            S.op('pool', lambda e: e.memset(xact[:, :, 0:PADN], 0.0), reads=['p_xact'], writes=['p_xact'])
        S.dma('sp', lambda e: e.dma_start(out=g.xbct[:, t0:t0 + tb].rearrange("(t p) n -> p t n", p=128), in_=xact[:, :, 0:tb]),
              reads=['p_xact'], writes=['d_xbct'])
        for j in range(ncb):
            for h in range(0, 10, 8):
                n = min(8, 10 - h)
                pb = g.pb[(h // 8) % 2]
                pbk = 'pb%d' % ((h // 8) % 2)
                for i in range(n):
                    S.op('pe', lambda e, i=i, h=h, pb=pb, j=j: e.transpose(out=pb[:, i * 128:(i + 1) * 128], in_=xact[:, h + i, j * 128:(j + 1) * 128],
                                                                           identity=cs['ident_bf'][:]),
                         reads=['p_xact', 'c_ident_bf'], writes=[pbk])
                if h == 0:
                    S.op('act', lambda e, pb=pb, h=h, n=n: e.copy(out=xst[:, h * 128:(h + n) * 128], in_=pb[:, 0:n * 128]), reads=[pbk], writes=['p_xst'])
                else:
                    S.op('dve', lambda e, pb=pb, h=h, n=n: e.tensor_copy(out=xst[:, h * 128:(h + n) * 128], in_=pb[:, 0:n * 128]), reads=[pbk], writes=['p_xst'])
            S.dma('sp', lambda e, j=j: e.dma_start(out=g.xstok[t0 + j * 128:t0 + (j + 1) * 128, :], in_=xst[:]), reads=['p_xst'], writes=['d_xstok'])
        w, wkey = next_w(OMQ, 512)
        for h in range(4):
            ps, pk = proj_F(w, wkey, h * 128, 128)
            e_ = eb[h % 2]
            S.op('act', lambda e, ps=ps, e_=e_: e.copy(out=e_[:, 0:tb], in_=ps[:, 0:tb]), reads=[pk], writes=['p_eb%d' % (h % 2)])
            S.dma('sp', lambda e, h=h, e_=e_: e.dma_start(out=g.qtml[h * 128:(h + 1) * 128, t0:t0 + tb], in_=e_[:, 0:tb]),
                  reads=['p_eb%d' % (h % 2)], writes=['d_qtml'])
        w, wkey = next_w(OMK, 512)
        for h in range(4):
            ps, pk = proj_F(w, wkey, h * 128, 128)
            e_ = eb[h % 2]
            S.op('act', lambda e, ps=ps, e_=e_: e.activation(out=e_[:, 0:tb], in_=ps[:, 0:tb], func=AF.Copy, scale=128.0 ** -0.5),
                 reads=[pk], writes=['p_eb%d' % (h % 2)])
            S.dma('sp', lambda e, h=h, e_=e_: e.dma_start(out=g.ktml[h * 128:(h + 1) * 128, t0:t0 + tb], in_=e_[:, 0:tb]),
                  reads=['p_eb%d' % (h % 2)], writes=['d_ktml'])
        for j in range(ncb):
            ps, pk = proj_T(w, wkey, j, 512)
            e_ = eb[j % 2]
            S.op('act', lambda e, ps=ps, e_=e_: e.activation(out=e_[:, 0:512], in_=ps[:, 0:512], func=AF.Copy, scale=128.0 ** -0.5),
                 reads=[pk], writes=['p_eb%d' % (j % 2)])
            S.dma('sp', lambda e, j=j, e_=e_: e.dma_start(out=g.ktok[t0 + j * 128:t0 + (j + 1) * 128, :], in_=e_[:, 0:512]),
                  reads=['p_eb%d' % (j % 2)], writes=['d_ktok'])
        w, wkey = next_w(OMI, 8)
        ps, pk = proj_F(w, wkey, 0, 4)
        S.op('act', lambda e, ps=ps: e.activation(out=gt[:, 0, 0:tb], in_=ps[0:4, 0:tb], func=AF.Identity, bias=ib[:, 0:1]),
             reads=[pk, 'p_ib'], writes=['p_gt0'])
        ps, pk = proj_F(w, wkey, 4, 4)
        S.op('act', lambda e, ps=ps: e.activation(out=gt[:, 1, 0:tb], in_=ps[0:4, 0:tb], func=AF.Exp, bias=fb[:, 0:1], scale=-1.0),
             reads=[pk, 'p_fb'], writes=['p_gt1'])
        S.op('act', lambda e: e.activation(out=gt[:, 1, 0:tb], in_=gt[:, 1, 0:tb], func=AF.Ln, bias=1.0), reads=['p_gt1'], writes=['p_gt1'])
        S.op('dve', lambda e: e.tensor_scalar(out=gt[:, 1, 0:tb], in0=gt[:, 1, 0:tb], scalar1=-1.0, scalar2=None, op0=ALU.mult),
             reads=['p_gt1'], writes=['p_gt1'])
        if bi == 0:
            S.op('pool', lambda e: e.memset(gt[:, 0, 0:PADN], NEGM), reads=['p_gt0'], writes=['p_gt0'])
            S.op('pool', lambda e: e.memset(gt[:, 1, 0:PADN], 0.0), reads=['p_gt1'], writes=['p_gt1'])
        S.dma('sp', lambda e: e.dma_start(out=g.gates[0:4, t0:t0 + tb], in_=gt[:, 0, 0:tb]), reads=['p_gt0'], writes=['d_gates'])
        S.dma('sp', lambda e: e.dma_start(out=g.gates[4:8, t0:t0 + tb], in_=gt[:, 1, 0:tb]), reads=['p_gt1'], writes=['d_gates'])
        w, wkey = next_w(ODT, 16)
        for j in range(ncb):
            ps, pk = proj_T(w, wkey, j, 16)
            d_ = dts[:, j, :]
            dkey = 'p_dts%d' % j
            S.op('dve', lambda e, ps=ps, d_=d_: e.tensor_tensor(out=d_, in0=ps[:, 0:16], in1=dtb[:], op=ALU.add), reads=[pk, 'p_dtb'], writes=[dkey])
            tmp = ev[0][:, 0:16]
            S.op('dve', lambda e, d_=d_, tmp=tmp: e.scalar_tensor_tensor(out=tmp, in0=d_, scalar=-1.0, in1=d_, op0=ALU.mult, op1=ALU.max), reads=[dkey], writes=['p_ev0'])
            S.op('act', lambda e, tmp=tmp: e.activation(out=tmp, in_=tmp, func=AF.Exp, scale=-1.0), reads=['p_ev0'], writes=['p_ev0'])
            S.op('act', lambda e, tmp=tmp: e.activation(out=tmp, in_=tmp, func=AF.Ln, bias=1.0), reads=['p_ev0'], writes=['p_ev0'])
            S.op('dve', lambda e, d_=d_, tmp=tmp: e.scalar_tensor_tensor(out=ev[0][:, 32:48], in0=d_, scalar=0.0, in1=tmp, op0=ALU.max, op1=ALU.add),
                 reads=[dkey, 'p_ev0'], writes=['p_ev0b'])
            S.op('dve', lambda e: e.tensor_tensor(out=ev[0][:, 48:64], in0=ev[0][:, 32:48], in1=negA[:], op=ALU.mult),
                 reads=['p_ev0b', 'p_negA'], writes=['p_ev0b'])
            S.dma('sp', lambda e, j=j: e.dma_start(out=g.dtla[t0 + j * 128:t0 + (j + 1) * 128, :], in_=ev[0][:, 32:64]),
                  reads=['p_ev0b'], writes=['d_dtla'])
        for (off, dst, dkey, fn) in ((OZ, g.zs, 'd_zs', AF.Silu), (OMV, g.vml, 'd_vml', None), (OMO, g.og, 'd_og', AF.Sigmoid),
                                     (OAV, g.vda, 'd_vda', None)):
            for half in range(2):
                w, wkey = next_w(off + half * 512, 512)
                for j in range(ncb):
                    ps, pk = proj_T(w, wkey, j, 512)
                    e_ = eb[j % 2]
                    ek = 'p_eb%d' % (j % 2)
                    if fn is None:
                        S.op('dve', lambda e, ps=ps, e_=e_: e.tensor_copy(out=e_[:, 0:512], in_=ps[:, 0:512]), reads=[pk], writes=[ek])
                    else:
                        S.op('act', lambda e, ps=ps, e_=e_, fn=fn: e.activation(out=e_[:, 0:512], in_=ps[:, 0:512], func=fn), reads=[pk], writes=[ek])
                    S.dma('sp', lambda e, j=j, e_=e_, dst=dst, half=half: e.dma_start(
                        out=dst[t0 + j * 128:t0 + (j + 1) * 128, half * 512:(half + 1) * 512], in_=e_[:, 0:512]), reads=[ek], writes=[dkey])
        for isk, (off, dst, dkey) in enumerate(((OAQ, g.qa, 'd_qa'), (OAK, g.ka, 'd_ka'))):
            for j in range(ncb):
                c = c0 + j
                for half in range(2):
                    if j == 0 or True:
                        w, wkey = next_w(off + half * 512, 512)
                    ps, pk = proj_T(w, wkey, j, 512)
                    S.op('act', lambda e, ps=ps, half=half: e.activation(
                        out=qk[:, half * 8:(half + 1) * 8, 0:64], in_=ps[:, 0:512].rearrange("p (a d) -> p a d", d=64),
                        func=AF.Copy, scale=(1.0 if isk else 0.125)), reads=[pk], writes=['p_qk'])
                cb_ = cosT[:, c, :].unsqueeze(1).to_broadcast([128, 16, 8])
                sb_ = sinT[:, c, :].unsqueeze(1).to_broadcast([128, 16, 8])
                x1 = qk[:, :, 0:8]
                x2 = qk[:, :, 8:16]
                S.op('dve', lambda e: e.tensor_tensor(out=r1[:], in0=x1, in1=sb_, op=ALU.mult), reads=['p_qk', 'p_sin'], writes=['p_r1'])
                S.op('dve', lambda e: e.tensor_tensor(out=r2[:], in0=x2, in1=sb_, op=ALU.mult), reads=['p_qk', 'p_sin'], writes=['p_r2'])
                S.op('dve', lambda e: e.tensor_tensor(out=r3[:], in0=x1, in1=cb_, op=ALU.mult), reads=['p_qk', 'p_cos'], writes=['p_r3'])
                S.op('dve', lambda e: e.tensor_tensor(out=x2, in0=x2, in1=cb_, op=ALU.mult), reads=['p_qk', 'p_cos'], writes=['p_qk'])
                S.op('dve', lambda e: e.tensor_tensor(out=x1, in0=r3[:], in1=r2[:], op=ALU.subtract), reads=['p_r3', 'p_r2', 'p_qk'], writes=['p_qk'])
                S.op('dve', lambda e: e.tensor_tensor(out=x2, in0=x2, in1=r1[:], op=ALU.add), reads=['p_r1', 'p_qk'], writes=['p_qk'])
                S.op('dve', lambda e: e.tensor_tensor(out=ev[1][:, 0:1024].rearrange("p (a d) -> p a d", d=64), in0=qk[:, :, 0:64], in1=qk[:, :, 0:64], op=ALU.mult),
                     reads=['p_qk'], writes=['p_ev1'])
                S.op('dve', lambda e: e.tensor_reduce(out=kn2[:], in_=ev[1][:, 0:1024].rearrange("p (a d) -> p a d", d=64), axis=AX.X, op=ALU.add),
                     reads=['p_ev1'], writes=['p_kn2'])
                if isk == 0:
                    S.op('act', lambda e: e.activation(out=kn2[:], in_=kn2[:], func=AF.Sqrt), reads=['p_kn2'], writes=['p_kn2'])
                    S.op('dve', lambda e: e.tensor_scalar(out=qk[:, :, 64:65], in0=kn2[:].unsqueeze(2), scalar1=-1.0, scalar2=None, op0=ALU.mult),
                         reads=['p_kn2', 'p_qk'], writes=['p_qk'])
                    S.op('pool', lambda e: e.memset(qk[:, :, 65:66], 1.0), reads=['p_qk'], writes=['p_qk'])
                else:
                    S.op('dve', lambda e: e.tensor_tensor(out=kmaxc[:], in0=kmaxc[:], in1=kn2[:], op=ALU.max), reads=['p_kn2', 'p_kmaxc'], writes=['p_kmaxc'])
                    S.op('pool', lambda e: e.memset(qk[:, :, 64:65], 0.0), reads=['p_qk'], writes=['p_qk'])
                    if c == 0:
                        S.op('dve', lambda e: e.tensor_copy(out=qk[:, :, 65:66], in_=padk[:].unsqueeze(1).to_broadcast([128, 16, 1])),
                             reads=['p_padk', 'p_qk'], writes=['p_qk'])
                    else:
                        S.op('pool', lambda e: e.memset(qk[:, :, 65:66], 0.0), reads=['p_qk'], writes=['p_qk'])
                S.op('act', lambda e: e.copy(out=qkb[:], in_=qk[:]), reads=['p_qk'], writes=['p_qkb'])
                for hh in range(2):
                    pb = g.pb[hh]
                    pbk = 'pb%d' % hh
                    for i in range(8):
                        S.op('pe', lambda e, i=i, hh=hh, pb=pb: e.transpose(out=pb[0:66, i * 128:(i + 1) * 128], in_=qkb[:, hh * 8 + i, :],
                                                                            identity=cs['ident_bf'][:]),
                             reads=['p_qkb', 'c_ident_bf'], writes=[pbk])
                    S.op('act' if hh == 0 else 'dve',
                         (lambda e, pb=pb, hh=hh, j=j: e.copy(out=qaT[:, hh * 8:(hh + 1) * 8, j * 128:(j + 1) * 128],
                                                              in_=pb[0:66, :].rearrange("p (a b) -> p a b", b=128))) if hh == 0 else
                         (lambda e, pb=pb, hh=hh, j=j: e.tensor_copy(out=qaT[:, hh * 8:(hh + 1) * 8, j * 128:(j + 1) * 128],
                                                                     in_=pb[0:66, :].rearrange("p (a b) -> p a b", b=128))),
                         reads=[pbk], writes=['p_qaT'])
            S.dma('sp', lambda e, dst=dst: e.dma_start(out=dst[:, :, t0:t0 + tb].rearrange("a r t -> r a t"), in_=qaT[:, :, 0:tb]),
                  reads=['p_qaT'], writes=[dkey])
    kT = g.ps[4]
    S.op('pe', lambda e: e.transpose(out=kT[0:16, 0:128], in_=kmaxc[:], identity=cs['ident_f'][:]), reads=['p_kmaxc', 'c_ident_f'], writes=['ps4'])
    km = A('p_km', [16, 1], F32)
    S.op('dve', lambda e: e.tensor_reduce(out=km[:], in_=kT[0:16, 0:128], axis=AX.X, op=ALU.max), reads=['ps4'], writes=['p_km'])
    S.op('act', lambda e: e.activation(out=km[:], in_=km[:], func=AF.Sqrt), reads=['p_km'], writes=['p_km'])
    kmrow = A('p_kmrow', [16, T], BF16)
    S.op('dve', lambda e: e.tensor_scalar(out=kmrow[:], in0=km[:].to_broadcast([16, T]), scalar1=1.02, scalar2=None, op0=ALU.mult),
         reads=['p_km'], writes=['p_kmrow'])
    S.dma('sp', lambda e: e.dma_start(out=g.ka[:, 64, :], in_=kmrow[:]), reads=['p_kmrow', 'd_ka'], writes=['d_ka'])


def stage_ssd(g):
    nc, S, T, NCH = g.nc, g.S, g.T, g.NCH
    cs = g.cs
    A = g.A
    ps = g.ps
    dsk = bc_load(g, 's_dsk', g.w['ssd_d'], 16)
    bt = [A('s_bt%d' % i, [128, 128], BF16) for i in range(2)]
    ct = [A('s_ct%d' % i, [128, 128], BF16) for i in range(2)]
    xsb = [A('s_xsb%d' % i, [128, 640], BF16) for i in range(2)]
    dl = [A('s_dl%d' % i, [128, 2, 8], F32) for i in range(2)]
    zt = [A('s_zt%d' % i, [128, 512], BF16) for i in range(2)]
    sm = A('s_sm', [128, 8, 8], F32)
    x3 = A('s_x3', [128, 8, 128], F32)
    dtt = A('s_dtt', [128, 8, 128], F32)
    mt = A('s_mt', [128, 8, 128], BF16)
    xdt = A('s_xdt', [128, 8, 64], BF16)
    xw = A('s_xw', [128, 8, 64], BF16)
    t1 = A('s_t1', [128, 8, 64], F32)
    t2 = A('s_t2', [128, 8, 64], F32)
    prevS = [A('s_prev%d' % i, [128, 8, 64], F32) for i in range(2)]
    prevSb = [A('s_prevb%d' % i, [128, 8, 64], BF16) for i in range(2)]
    for i in range(2):
        S.op('pool', lambda e, i=i: e.memset(prevS[i][:], 0.0), writes=['s_prev%d' % i])
        S.op('pool', lambda e, i=i: e.memset(prevSb[i][:], 0.0), writes=['s_prevb%d' % i])
    it = 0
    for c in range(NCH):
        r0 = c * 128
        for gq in range(2):
            p = it % 2
            it += 1
            k = lambda n: '%s%d' % (n, p)
            S.dma('sp', lambda e: e.dma_start(out=bt[p][:], in_=g.xbct[1024 + gq * 128:1024 + (gq + 1) * 128, r0:r0 + 128]), reads=['d_xbct'], writes=[k('s_bt')])
            S.dma('sp', lambda e: e.dma_start(out=ct[p][:], in_=g.xbct[1280 + gq * 128:1280 + (gq + 1) * 128, r0:r0 + 128]), reads=['d_xbct'], writes=[k('s_ct')])
            S.dma('sp', lambda e: e.dma_start(out=xsb[p][:, 0:512], in_=g.xstok[r0:r0 + 128, gq * 512:(gq + 1) * 512]), reads=['d_xstok'], writes=[k('s_xsb')])
            S.dma('sp', lambda e: e.dma_start(out=xsb[p][:, 512:640], in_=g.xstok[r0:r0 + 128, 1024 + gq * 128:1024 + (gq + 1) * 128]), reads=['d_xstok', k('s_xsb')], writes=[k('s_xsb')])
            S.dma('sp', lambda e: e.dma_start(out=dl[p][:, 0, :], in_=g.dtla[r0:r0 + 128, gq * 8:(gq + 1) * 8]), reads=['d_dtla'], writes=[k('s_dl')])
            S.dma('sp', lambda e: e.dma_start(out=dl[p][:, 1, :], in_=g.dtla[r0:r0 + 128, 16 + gq * 8:16 + (gq + 1) * 8]), reads=['d_dtla', k('s_dl')], writes=[k('s_dl')])
            S.dma('sp', lambda e: e.dma_start(out=zt[p][:], in_=g.zs[r0:r0 + 128, gq * 512:(gq + 1) * 512]), reads=['d_zs'], writes=[k('s_zt')])
            dt_ = dl[p][:, 0, :]
            la = dl[p][:, 1, :]
            xs3 = xsb[p][:, 0:512].rearrange("p (e d) -> p e d", d=64)
            S.op('pe', lambda e: e.matmul(ps[0][:, 0:8], lhsT=cs['utri'][:], rhs=la, start=True, stop=True), reads=[k('s_dl'), 'c_utri'], writes=['ps0a'])
            S.op('pe', lambda e: e.matmul(ps[0][:, 8:16], lhsT=cs['ones_f'][:], rhs=la, start=True, stop=True), reads=[k('s_dl'), 'c_ones_f'], writes=['ps0b'])
            S.op('dve', lambda e: e.tensor_copy(out=sm[:, 0, :], in_=ps[0][:, 0:8]), reads=['ps0a'], writes=['s_sm0'])
            S.op('dve', lambda e: e.tensor_copy(out=sm[:, 1, :], in_=ps[0][:, 8:16]), reads=['ps0b'], writes=['s_sm1'])
            S.op('dve', lambda e: e.tensor_scalar(out=sm[:, 2, :], in0=sm[:, 0, :], scalar1=-1.0, scalar2=None, op0=ALU.mult), reads=['s_sm0'], writes=['s_sm2'])
            S.op('act', lambda e: e.activation(out=sm[:, 3, :], in_=sm[:, 0, :], func=AF.Exp), reads=['s_sm0'], writes=['s_sm3'])
            S.op('dve', lambda e: e.tensor_tensor(out=sm[:, 7, :], in0=sm[:, 1, :], in1=sm[:, 0, :], op=ALU.subtract), reads=['s_sm0', 's_sm1'], writes=['s_sm7'])
            S.op('act', lambda e: e.activation(out=sm[:, 4, :], in_=sm[:, 7, :], func=AF.Exp), reads=['s_sm7'], writes=['s_sm4'])
            S.op('act', lambda e: e.activation(out=sm[:, 5, :], in_=sm[:, 1, :], func=AF.Exp), reads=['s_sm1'], writes=['s_sm5'])
            S.op('dve', lambda e: e.tensor_tensor(out=sm[:, 6, :], in0=sm[:, 4, :], in1=dt_, op=ALU.mult), reads=['s_sm4', k('s_dl')], writes=['s_sm6'])
            S.op('dve', lambda e: e.tensor_tensor(out=x3[:], in0=cs['utri'][:].unsqueeze(1).to_broadcast([128, 8, 128]),
                                                   in1=la.unsqueeze(2).to_broadcast([128, 8, 128]), op=ALU.mult),
                 reads=['c_utri', k('s_dl')], writes=['s_x3'])
            for hf in range(2):
                pk = 'ps%d' % (1 + hf)
                S.op('pe', lambda e, hf=hf: e.matmul(ps[1 + hf][:, 0:512], lhsT=cs['ones_f'][:], rhs=x3[:, hf * 4:(hf + 1) * 4, :].rearrange("p a b -> p (a b)"),
                                                     start=True, stop=False), reads=['s_x3', 'c_ones_f'], writes=[pk])
                S.op('pe', lambda e, hf=hf: e.matmul(ps[1 + hf][:, 0:512], lhsT=cs['ident_f'][:], rhs=cs['negmask8'][:, hf * 512:(hf + 1) * 512],
                                                     start=False, stop=True), reads=['c_ident_f', 'c_negmask8'], writes=[pk])
            for e_ in range(8):
                pk = 'ps%d' % (1 + e_ // 4)
                S.op('act', lambda e, e_=e_: e.activation(out=dtt[:, e_, :], in_=ps[1 + e_ // 4][:, (e_ % 4) * 128:(e_ % 4 + 1) * 128], func=AF.Exp,
                                                          bias=sm[:, 2, e_:e_ + 1]), reads=[pk, 's_sm2'], writes=['s_dtt'])
            S.op('pe', lambda e: e.matmul(ps[0][:, 128:256], lhsT=bt[p][:], rhs=ct[p][:], start=True, stop=True), reads=[k('s_bt'), k('s_ct')], writes=['ps0c'])
            S.op('dve', lambda e: e.tensor_tensor(out=mt[:], in0=dtt[:], in1=ps[0][:, 128:256].unsqueeze(1).to_broadcast([128, 8, 128]), op=ALU.mult),
                 reads=['s_dtt', 'ps0c'], writes=['s_mt'])
            S.op('pool', lambda e: e.tensor_tensor(out=xdt[:], in0=xs3, in1=dt_.unsqueeze(2).to_broadcast([128, 8, 64]), op=ALU.mult),
                 reads=[k('s_xsb'), k('s_dl')], writes=['s_xdt'])
            S.op('pool', lambda e: e.tensor_tensor(out=xw[:], in0=xs3, in1=sm[:, 6, :].unsqueeze(2).to_broadcast([128, 8, 64]), op=ALU.mult),
                 reads=[k('s_xsb'), 's_sm6'], writes=['s_xw'])
            for e_ in range(8):
                S.op('pe', lambda e, e_=e_: e.matmul(ps[4][:, e_ * 64:(e_ + 1) * 64], lhsT=mt[:, e_, :], rhs=xdt[:, e_, :], start=True, stop=True),
                     reads=['s_mt', 's_xdt'], writes=['ps4'])
            S.op('pe', lambda e: e.matmul(ps[5][:, 0:512], lhsT=ct[p][:], rhs=prevSb[gq][:].rearrange("p a b -> p (a b)"), start=True, stop=True),
                 reads=[k('s_ct'), 's_prevb%d' % gq], writes=['ps5'])
            S.op('pe', lambda e: e.matmul(ps[3][:, 0:512], lhsT=xsb[p][:, 512:640], rhs=xw[:].rearrange("p a b -> p (a b)"), start=True, stop=True),
                 reads=[k('s_xsb'), 's_xw'], writes=['ps3'])
            S.op('dve', lambda e: e.tensor_tensor(out=t1[:], in0=ps[5][:, 0:512].rearrange("p (a b) -> p a b", b=64),
                                                   in1=sm[:, 3, :].unsqueeze(2).to_broadcast([128, 8, 64]), op=ALU.mult), reads=['ps5', 's_sm3'], writes=['s_t1'])
            S.op('dve', lambda e: e.tensor_tensor(out=t1[:], in0=t1[:], in1=ps[4][:, 0:512].rearrange("p (a b) -> p a b", b=64), op=ALU.add),
                 reads=['ps4', 's_t1'], writes=['s_t1'])
            S.op('pool', lambda e: e.tensor_tensor(out=t2[:], in0=xs3, in1=dsk[:, gq * 8:(gq + 1) * 8].unsqueeze(2).to_broadcast([128, 8, 64]), op=ALU.mult),
                 reads=[k('s_xsb'), 's_dsk'], writes=['s_t2'])
            S.op('dve', lambda e: e.tensor_tensor(out=t1[:], in0=t1[:], in1=t2[:], op=ALU.add), reads=['s_t1', 's_t2'], writes=['s_t1'])
            S.op('dve', lambda e: e.tensor_tensor(out=t1[:], in0=t1[:], in1=zt[p][:].rearrange("p (a b) -> p a b", b=64), op=ALU.mult),
                 reads=['s_t1', k('s_zt')], writes=['s_t1'])
            S.dma('sp', lambda e: e.dma_start(out=g.yall[r0:r0 + 128, gq * 512:(gq + 1) * 512], in_=t1[:].rearrange("p a b -> p (a b)")),
                  reads=['s_t1'], writes=['d_yall'])
            S.op('dve', lambda e: e.tensor_tensor(out=prevS[gq][:], in0=prevS[gq][:], in1=sm[:, 5, :].unsqueeze(2).to_broadcast([128, 8, 64]), op=ALU.mult),
                 reads=['s_prev%d' % gq, 's_sm5'], writes=['s_prev%d' % gq])
            S.op('dve', lambda e: e.tensor_tensor(out=prevS[gq][:], in0=prevS[gq][:], in1=ps[3][:, 0:512].rearrange("p (a b) -> p a b", b=64), op=ALU.add),
                 reads=['s_prev%d' % gq, 'ps3'], writes=['s_prev%d' % gq])
            S.op('act', lambda e: e.copy(out=prevSb[gq][:], in_=prevS[gq][:]), reads=['s_prev%d' % gq], writes=['s_prevb%d' % gq])


def stage_mlstm(g):
    nc, S, T, NCH = g.nc, g.S, g.T, g.NCH
    cs = g.cs
    A = g.A
    mlw = bc_load(g, 'm_mlw', g.w['mlstm_norm_w'], 1024)
    Cst = [A('m_C%d' % h, [128, 257], F32) for h in range(4)]
    Cstb = [A('m_Cb%d' % h, [128, 257], BF16) for h in range(4)]
    mprev = A('m_mprev', [128, 4], F32)
    S.op('pool', lambda e: e.memset(mprev[:], NEGM), writes=['m_mprev0', 'm_mprev1', 'm_mprev2', 'm_mprev3'])
    for h in range(4):
        S.op('pool', lambda e, h=h: e.memset(Cst[h][:], 0.0), writes=['m_C%d' % h])
        S.op('pool', lambda e, h=h: e.memset(Cstb[h][:], 0.0), writes=['m_Cb%d' % h])
    TS = []
    for s in range(2):
        t = {}
        for p in range(2):
            sfx = '_%d_%d' % (s, p)
            t['qT', p] = A('m_qT' + sfx, [128, 128], BF16)
            t['kT', p] = A('m_kT' + sfx, [128, 128], BF16)
            t['ktk', p] = A('m_ktk' + sfx, [128, 128], BF16)
            t['va', p] = A('m_va' + sfx, [128, 257], BF16)
            t['ip', p] = A('m_ip' + sfx, [128, 128], F32)
            t['lf', p] = A('m_lf' + sfx, [128, 128], F32)
            t['og', p] = A('m_og' + sfx, [128, 256], BF16)
            S.op('pool', lambda e, tt=t['va', p]: e.memset(tt[:, 256:257], 1.0), writes=['m_va' + sfx])
        for n in ('fcs', 'gr', 'pr', 'tmp', 'et'):
            t[n] = A('m_%s_%d' % (n, s), [128, 128], F32)
        for n in ('set', 'qts', 'kw'):
            t[n] = A('m_%s_%d' % (n, s), [128, 128], BF16)
        t['col'] = A('m_col_%d' % s, [128, 16], F32)
        for n in ('ht', 'yml', 'jk'):
            t[n] = A('m_%s_%d' % (n, s), [128, 256], F32)
        TS.append(t)

    def body(c, h, s, p):
        t = TS[s]
        ps = g.ps[3 * s:3 * s + 3]
        pk = ['ps%d' % (3 * s + i) for i in range(3)]
        r0 = c * 128
        sfx = '_%d_%d' % (s, p)
        k = lambda n: 'm_%s%s' % (n, sfx)
        ks = lambda n: 'm_%s_%d' % (n, s)
        qT, kT, ktk, va, ipr, lfr, ogt = (t[n, p] for n in ('qT', 'kT', 'ktk', 'va', 'ip', 'lf', 'og'))
        fcs, gr, pr, tmp, et, setb, qts, kw, col, ht, yml, jk = (t[n] for n in ('fcs', 'gr', 'pr', 'tmp', 'et', 'set', 'qts', 'kw', 'col', 'ht', 'yml', 'jk'))
        S.dma('sp', lambda e: e.dma_start(out=qT[:], in_=g.qtml[h * 128:(h + 1) * 128, r0:r0 + 128]), reads=['d_qtml'], writes=[k('qT')]); yield
        S.dma('sp', lambda e: e.dma_start(out=kT[:], in_=g.ktml[h * 128:(h + 1) * 128, r0:r0 + 128]), reads=['d_ktml'], writes=[k('kT')]); yield
        S.dma('sp', lambda e: e.dma_start(out=ktk[:], in_=g.ktok[r0:r0 + 128, h * 128:(h + 1) * 128]), reads=['d_ktok'], writes=[k('ktk')]); yield
        S.dma('sp', lambda e: e.dma_start(out=va[:, 0:256], in_=g.vml[r0:r0 + 128, h * 256:(h + 1) * 256]), reads=['d_vml', k('va')], writes=[k('va')]); yield
        S.dma('sp', lambda e: e.dma_start(out=ipr[:], in_=g.gates[h, r0:r0 + 128].partition_broadcast(128)), reads=['d_gates'], writes=[k('ip')]); yield
        S.dma('sp', lambda e: e.dma_start(out=lfr[:], in_=g.gates[4 + h, r0:r0 + 128].partition_broadcast(128)), reads=['d_gates'], writes=[k('lf')]); yield
        S.dma('sp', lambda e: e.dma_start(out=ogt[:], in_=g.og[r0:r0 + 128, h * 256:(h + 1) * 256]), reads=['d_og'], writes=[k('og')]); yield
        mp = mprev[:, h:h + 1]
        mpk = 'm_mprev%d' % h
        ck = lambda i: 'm_col%d_%d' % (i, s)
        S.op('dve', lambda e: e.tensor_tensor_scan(out=fcs[:], data0=cs['ones_f'][:], data1=lfr[:], initial=0.0, op0=ALU.mult, op1=ALU.add),
             reads=[k('lf'), 'c_ones_f'], writes=[ks('fcs')]); yield
        S.op('dve', lambda e: e.tensor_tensor(out=gr[:], in0=ipr[:], in1=fcs[:], op=ALU.subtract), reads=[k('ip'), ks('fcs')], writes=[ks('gr')]); yield
        S.op('dve', lambda e: e.tensor_tensor_scan(out=pr[:], data0=gr[:], data1=gr[:], initial=mp, op0=ALU.max, op1=ALU.max),
             reads=[ks('gr'), mpk], writes=[ks('pr')]); yield
        ftot = fcs[:, 127:128]
        S.op('dve', lambda e: e.tensor_scalar(out=tmp[:], in0=gr[:], scalar1=ftot, scalar2=None, op0=ALU.add), reads=[ks('gr'), ks('fcs')], writes=[ks('tmp')]); yield
        S.op('dve', lambda e: e.tensor_reduce(out=col[:, 1:2], in_=tmp[:], axis=AX.X, op=ALU.max), reads=[ks('tmp')], writes=[ck(1)]); yield
        for ci, (src_, sk_) in ((2, (gr, ks('gr'))), (3, (pr, ks('pr'))), (4, (fcs, ks('fcs')))):
            S.op('dve', lambda e, ci=ci, src_=src_: e.scalar_tensor_tensor(out=tmp[:], in0=src_[:], scalar=1.0, in1=cs['ident_f'][:],
                                                                           op0=ALU.mult, op1=ALU.mult, accum_out=col[:, ci:ci + 1]),
                 reads=[sk_, 'c_ident_f'], writes=[ks('tmp'), ck(ci)]); yield
        S.op('dve', lambda e: e.tensor_tensor(out=col[:, 7:8], in0=ftot, in1=col[:, 1:2], op=ALU.subtract), reads=[ks('fcs'), ck(1)], writes=[ck(7)]); yield
        S.op('act', lambda e: e.activation(out=col[:, 5:6], in_=col[:, 2:3], func=AF.Exp, bias=col[:, 7:8]), reads=[ck(2), ck(7)], writes=[ck(5)]); yield
        S.op('dve', lambda e: e.tensor_scalar(out=kw[:], in0=ktk[:], scalar1=col[:, 5:6], scalar2=None, op0=ALU.mult), reads=[k('ktk'), ck(5)], writes=[ks('kw')]); yield
        S.op('pe', lambda e: e.matmul(ps[0][:, 0:257], lhsT=kw[:], rhs=va[:], start=True, stop=True), reads=[ks('kw'), k('va')], writes=[pk[0]]); yield
        S.op('pe', lambda e: e.matmul(ps[1][:, 0:128], lhsT=kT[:], rhs=qT[:], start=True, stop=True), reads=[k('kT'), k('qT')], writes=[pk[1]]); yield
        S.op('dve', lambda e: e.tensor_tensor(out=et[:], in0=cs['negmask'][:], in1=pr[:], op=ALU.subtract), reads=['c_negmask', ks('pr')], writes=[ks('et')]); yield
        S.op('act', lambda e: e.activation(out=et[:], in_=et[:], func=AF.Exp, bias=col[:, 2:3]), reads=[ks('et'), ck(2)], writes=[ks('et')]); yield
        S.op('dve', lambda e: e.tensor_tensor(out=setb[:], in0=et[:], in1=ps[1][:, 0:128], op=ALU.mult), reads=[ks('et'), pk[1]], writes=[ks('set')]); yield
        S.op('act', lambda e: e.activation(out=tmp[:], in_=pr[:], func=AF.Exp, bias=mp, scale=-1.0), reads=[ks('pr'), mpk], writes=[ks('tmp')]); yield
        S.op('dve', lambda e: e.tensor_tensor(out=qts[:], in0=qT[:], in1=tmp[:], op=ALU.mult), reads=[k('qT'), ks('tmp')], writes=[ks('qts')]); yield
        S.op('pe', lambda e: e.matmul(ps[2][:, 0:257], lhsT=setb[:], rhs=va[:], start=True, stop=False), reads=[ks('set'), k('va')], writes=[pk[2]]); yield
        S.op('pe', lambda e: e.matmul(ps[2][:, 0:257], lhsT=qts[:], rhs=Cstb[h][:], start=False, stop=True), reads=[ks('qts'), 'm_Cb%d' % h], writes=[pk[2]]); yield
        S.op('dve', lambda e: e.tensor_tensor(out=col[:, 8:9], in0=col[:, 4:5], in1=col[:, 3:4], op=ALU.add), reads=[ck(4), ck(3)], writes=[ck(8)]); yield
        S.op('dve', lambda e: e.tensor_scalar(out=col[:, 8:9], in0=col[:, 8:9], scalar1=-1.0, scalar2=80.0, op0=ALU.mult, op1=ALU.min), reads=[ck(8)], writes=[ck(8)]); yield
        S.op('act', lambda e: e.activation(out=col[:, 8:9], in_=col[:, 8:9], func=AF.Exp), reads=[ck(8)], writes=[ck(8)]); yield
        S.op('dve', lambda e: e.tensor_copy(out=col[:, 15:16], in_=ps[2][:, 256:257]), reads=[pk[2]], writes=[ck(15)]); yield
        S.op('dve', lambda e: e.scalar_tensor_tensor(out=col[:, 9:10], in0=col[:, 15:16], scalar=-1.0, in1=col[:, 15:16], op0=ALU.mult, op1=ALU.max), reads=[ck(15)], writes=[ck(9)]); yield
        S.op('dve', lambda e: e.tensor_tensor(out=col[:, 9:10], in0=col[:, 9:10], in1=col[:, 8:9], op=ALU.max), reads=[ck(9), ck(8)], writes=[ck(9)]); yield
        S.op('dve', lambda e: e.reciprocal(out=col[:, 9:10], in_=col[:, 9:10]), reads=[ck(9)], writes=[ck(9)]); yield
        S.op('dve', lambda e: e.tensor_scalar(out=ht[:], in0=ps[2][:, 0:256], scalar1=col[:, 9:10], scalar2=None, op0=ALU.mult), reads=[pk[2], ck(9)], writes=[ks('ht')]); yield
        S.op('act', lambda e: e.activation(out=jk[:], in_=ht[:], func=AF.Square, accum_out=col[:, 10:11]), reads=[ks('ht')], writes=[ks('jk'), ck(10)]); yield
        S.op('dve', lambda e: e.tensor_scalar(out=col[:, 10:11], in0=col[:, 10:11], scalar1=1.0 / 256, scalar2=EPS, op0=ALU.mult, op1=ALU.add), reads=[ck(10)], writes=[ck(10)]); yield
        S.op('act', lambda e: e.activation(out=col[:, 10:11], in_=col[:, 10:11], func=AF.Sqrt), reads=[ck(10)], writes=[ck(10)]); yield
        S.op('dve', lambda e: e.reciprocal(out=col[:, 10:11], in_=col[:, 10:11]), reads=[ck(10)], writes=[ck(10)]); yield
        S.op('dve', lambda e: e.scalar_tensor_tensor(out=yml[:], in0=ht[:], scalar=col[:, 10:11], in1=mlw[:, h * 256:(h + 1) * 256], op0=ALU.mult, op1=ALU.mult),
             reads=[ks('ht'), ck(10), 'm_mlw'], writes=[ks('yml')]); yield
        S.op('dve', lambda e: e.tensor_tensor(out=yml[:], in0=yml[:], in1=ogt[:], op=ALU.mult), reads=[ks('yml'), k('og')], writes=[ks('yml')]); yield
        S.dma('sp', lambda e: e.dma_start(out=g.yall[r0:r0 + 128, 1024 + h * 256:1024 + (h + 1) * 256], in_=yml[:]), reads=[ks('yml')], writes=['d_yall']); yield
        S.op('dve', lambda e: e.tensor_tensor(out=col[:, 11:12], in0=ftot, in1=mp, op=ALU.add), reads=[ks('fcs'), mpk], writes=[ck(11)]); yield
        S.op('dve', lambda e: e.tensor_tensor(out=col[:, 12:13], in0=col[:, 11:12], in1=col[:, 1:2], op=ALU.max), reads=[ck(11), ck(1)], writes=[ck(12)]); yield
        S.op('dve', lambda e: e.tensor_tensor(out=col[:, 13:14], in0=col[:, 11:12], in1=col[:, 12:13], op=ALU.subtract), reads=[ck(11), ck(12)], writes=[ck(13)]); yield
        S.op('dve', lambda e: e.tensor_tensor(out=col[:, 14:15], in0=col[:, 1:2], in1=col[:, 12:13], op=ALU.subtract), reads=[ck(1), ck(12)], writes=[ck(14)]); yield
        S.op('act', lambda e: e.activation(out=col[:, 13:15], in_=col[:, 13:15], func=AF.Exp), reads=[ck(13), ck(14)], writes=[ck(13), ck(14)]); yield
        S.op('dve', lambda e: e.tensor_scalar(out=Cst[h][:], in0=Cst[h][:], scalar1=col[:, 13:14], scalar2=None, op0=ALU.mult), reads=['m_C%d' % h, ck(13)], writes=['m_C%d' % h]); yield
        S.op('dve', lambda e: e.scalar_tensor_tensor(out=Cst[h][:], in0=ps[0][:, 0:257], scalar=col[:, 14:15], in1=Cst[h][:], op0=ALU.mult, op1=ALU.add),
             reads=[pk[0], ck(14), 'm_C%d' % h], writes=['m_C%d' % h]); yield
        S.op('act', lambda e: e.copy(out=Cstb[h][:], in_=Cst[h][:]), reads=['m_C%d' % h], writes=['m_Cb%d' % h]); yield
        S.op('dve', lambda e: e.tensor_copy(out=mp, in_=col[:, 12:13]), reads=[ck(12), mpk], writes=[mpk]); yield

    it = 0
    for c in range(NCH):
        for hp in range(2):
            p = it % 2
            it += 1
            gens = [body(c, 2 * hp + s, s, p) for s in range(2)]
            live = list(gens)
            while live:
                nxt = []
                for gen in live:
                    try:
                        next(gen)
                        nxt.append(gen)
                    except StopIteration:
                        pass
                live = nxt


def stage_attn(g):
    nc, S, T, NCH = g.nc, g.S, g.T, g.NCH
    cs = g.cs
    A = g.A
    ps = g.ps
    lt = [bc_load(g, 'a_l%d' % i, g.w[n], 64) for i, n in enumerate(('diff_lambda_q1', 'diff_lambda_k1', 'diff_lambda_q2', 'diff_lambda_k2'))]
    lj = A('a_lj', [128, 64], F32)
    lam = A('a_lam', [128, 4], F32)
    for i in range(2):
        S.op('dve', lambda e, i=i: e.scalar_tensor_tensor(out=lj[:], in0=lt[2 * i][:], scalar=1.0, in1=lt[2 * i + 1][:], op0=ALU.mult, op1=ALU.mult,
                                                          accum_out=lam[:, i:i + 1]), reads=['a_l%d' % (2 * i), 'a_l%d' % (2 * i + 1)], writes=['a_lj', 'a_lam%d' % i])
    S.op('act', lambda e: e.activation(out=lam[:, 0:2], in_=lam[:, 0:2], func=AF.Exp), reads=['a_lam0', 'a_lam1'], writes=['a_lam0', 'a_lam1'])
    S.op('dve', lambda e: e.tensor_tensor(out=lam[:, 2:3], in0=lam[:, 1:2], in1=lam[:, 0:1], op=ALU.subtract), reads=['a_lam0', 'a_lam1'], writes=['a_lam2'])
    S.op('dve', lambda e: e.tensor_scalar(out=lam[:, 2:3], in0=lam[:, 2:3], scalar1=-g.lam_init, scalar2=None, op0=ALU.add), reads=['a_lam2'], writes=['a_lam2'])
    dnw = A('a_dnw', [128, 1], F32)
    S.dma('sp', lambda e: e.dma_start(out=dnw[:], in_=g.w['diff_norm_w'].rearrange("(p o) -> p o", o=1)), writes=['a_dnw'])
    S.op('dve', lambda e: e.tensor_scalar(out=dnw[:], in0=dnw[:], scalar1=1.0 - g.lam_init, scalar2=None, op0=ALU.mult), reads=['a_dnw'], writes=['a_dnw'])
    kah = [A('a_ka%d' % m, [66, T], BF16) for m in range(2)]
    vh = A('a_vh', [128, NCH, 128], BF16)
    qat = [[A('a_qa%d_%d' % (i, m), [66, 512], BF16) for m in range(2)] for i in range(2)]
    pt = [A('a_pt%d' % i, [128, 512], BF16) for i in range(3)]
    rz = A('a_rz', [128, 512], F32)
    rm = [A('a_rm%d' % m, [128, 512], F32) for m in range(2)]
    sq = A('a_sq', [128, 512], F32)
    yb = A('a_yb', [128, 512], BF16)
    yo = A('a_yo', [128, 4, 128], F32)
    nqt = (NCH + 3) // 4
    pi = 0
    for h in range(8):
        for m in range(2):
            S.dma('sp', lambda e, m=m: e.dma_start(out=kah[m][:], in_=g.ka[h * 2 + m, :, :]), reads=['d_ka'], writes=['a_ka%d' % m])
        S.dma('sp', lambda e: e.dma_start(out=vh[:], in_=g.vda[:, h * 128:(h + 1) * 128].rearrange("(c p) d -> p c d", p=128)), reads=['d_vda'], writes=['a_vh'])
        for qt in range(nqt):
            q0 = qt * 512
            nq = min(512, T - q0)
            nkb = (q0 + nq) // 128
            qb = qt % 2
            for m in range(2):
                S.dma('sp', lambda e, m=m: e.dma_start(out=qat[qb][m][:, 0:nq], in_=g.qa[h * 2 + m, :, q0:q0 + nq]), reads=['d_qa'], writes=['a_qa%d_%d' % (qb, m)])
            for m in range(2):
                def emit_qk(kb, m=m):
                    sb = kb % 2
                    S.op('pe', lambda e: e.matmul(ps[sb][:, 0:nq], lhsT=kah[m][:, kb * 128:(kb + 1) * 128], rhs=qat[qb][m][:, 0:nq], start=True, stop=True),
                         reads=['a_ka%d' % m, 'a_qa%d_%d' % (qb, m)], writes=['ps%d' % sb])

                def emit_exp(kb, pti):
                    sb = kb % 2
                    S.op('act', lambda e: e.activation(out=pt[pti][:, 0:nq], in_=ps[sb][:, 0:nq], func=AF.Exp), reads=['ps%d' % sb], writes=['a_pt%d' % pti])
                    if kb * 128 + 127 > q0:
                        v = kb - qt * 4
                        S.op('pool', lambda e: e.tensor_tensor(out=pt[pti][:, 0:nq], in0=pt[pti][:, 0:nq], in1=cs['cmask'][:, v, 0:nq], op=ALU.mult),
                             reads=['a_pt%d' % pti, 'c_cmask'], writes=['a_pt%d' % pti])

                def emit_pv(kb, pti, m=m):
                    S.op('pe', lambda e: e.matmul(ps[2 + m][:, 0:nq], lhsT=vh[:, kb, :], rhs=pt[pti][:, 0:nq], start=(kb == 0), stop=(kb == nkb - 1)),
                         reads=['a_vh', 'a_pt%d' % pti], writes=['ps%d' % (2 + m)])
                    S.op('pe', lambda e: e.matmul(ps[4 + m][:, 0:nq], lhsT=cs['ones_bf'][:], rhs=pt[pti][:, 0:nq], start=(kb == 0), stop=(kb == nkb - 1)),
                         reads=['c_ones_bf', 'a_pt%d' % pti], writes=['ps%d' % (4 + m)])

                emit_qk(0)
                ptis = {}
                for kb in range(nkb):
                    ptis[kb] = pi % 3
                    pi += 1
                    emit_exp(kb, ptis[kb])
                    if kb + 1 < nkb:
                        emit_qk(kb + 1)
                    emit_pv(kb, ptis[kb])
                S.op('dve', lambda e, m=m: e.tensor_scalar(out=rz[:, 0:nq], in0=ps[4 + m][:, 0:nq], scalar1=1e-30, scalar2=None, op0=ALU.max), reads=['ps%d' % (4 + m)], writes=['a_rz'])
                S.op('dve', lambda e: e.reciprocal(out=rz[:, 0:nq], in_=rz[:, 0:nq]), reads=['a_rz'], writes=['a_rz'])
                S.op('dve', lambda e, m=m: e.tensor_tensor(out=rm[m][:, 0:nq], in0=ps[2 + m][:, 0:nq], in1=rz[:, 0:nq], op=ALU.mult), reads=['ps%d' % (2 + m), 'a_rz'], writes=['a_rm%d' % m])
            S.op('dve', lambda e: e.scalar_tensor_tensor(out=rm[0][:, 0:nq], in0=rm[1][:, 0:nq], scalar=lam[:, 2:3], in1=rm[0][:, 0:nq], op0=ALU.mult, op1=ALU.add),
                 reads=['a_rm0', 'a_rm1', 'a_lam2'], writes=['a_rm0'])
            S.op('act', lambda e: e.activation(out=sq[:, 0:nq], in_=rm[0][:, 0:nq], func=AF.Square), reads=['a_rm0'], writes=['a_sq'])
            S.op('pe', lambda e: e.matmul(ps[0][:, 0:nq], lhsT=cs['ones_f'][:], rhs=sq[:, 0:nq], start=True, stop=True), reads=['c_ones_f', 'a_sq'], writes=['ps0'])
            S.op('dve', lambda e: e.tensor_scalar(out=sq[:, 0:nq], in0=ps[0][:, 0:nq], scalar1=1.0 / 128, scalar2=EPS, op0=ALU.mult, op1=ALU.add), reads=['ps0', 'a_sq'], writes=['a_sq'])
            S.op('act', lambda e: e.activation(out=sq[:, 0:nq], in_=sq[:, 0:nq], func=AF.Sqrt), reads=['a_sq'], writes=['a_sq'])
            S.op('dve', lambda e: e.reciprocal(out=sq[:, 0:nq], in_=sq[:, 0:nq]), reads=['a_sq'], writes=['a_sq'])
            S.op('dve', lambda e: e.scalar_tensor_tensor(out=yb[:, 0:nq], in0=rm[0][:, 0:nq], scalar=dnw[:, 0:1], in1=sq[:, 0:nq], op0=ALU.mult, op1=ALU.mult),
                 reads=['a_rm0', 'a_dnw', 'a_sq'], writes=['a_yb'])
            nsub = nq // 128
            for i in range(nsub):
                S.op('pe', lambda e, i=i: e.transpose(out=g.pb[0][:, i * 128:(i + 1) * 128], in_=yb[:, i * 128:(i + 1) * 128], identity=cs['ident_bf'][:]),
                     reads=['a_yb', 'c_ident_bf'], writes=['pb0'])
            S.op('act', lambda e: e.copy(out=yo[:, 0:nsub, :], in_=g.pb[0][:, 0:nsub * 128].rearrange("p (a b) -> p a b", b=128)), reads=['pb0'], writes=['a_yo'])
            S.dma('sp', lambda e: e.dma_start(out=g.yall[q0:q0 + nq, 2048 + h * 128:2048 + (h + 1) * 128].rearrange("(a p) d -> p a d", p=128), in_=yo[:, 0:nsub, :]),
                  reads=['a_yo'], writes=['d_yall'])


def phase_c(g):
    nc, S, NCC = g.nc, g.S, g.NCC
    cs = g.cs
    A = g.A
    ps = g.ps
    w_in = 'w_in'
    TB = 256
    rows = A('c_rows', [128, NCC], I32)
    S.dma('sp', lambda e: e.dma_start(out=rows[:], in_=g.rows), writes=['c_rows'])
    rmask = A('c_rmask', [128, NCC], F32)
    S.dma('sp', lambda e: e.dma_start(out=rmask[:], in_=g.rowmask), writes=['c_rmask'])
    ssdw = bc_load(g, 'c_ssdw', g.w['ssd_norm_w'], 1024)
    nwb = A('c_nwb', [128, D], F32)
    xres = A('c_xres', [128, 2, D], F32)
    h2b = A('c_h2b', [128, 2, D], BF16)
    uT = A('c_uT', [128, KC, TB], BF16)
    ub = A('c_ub', [128, D], BF16)
    junk = A('c_junk', [128, D], BF16)
    st = A('c_st', [128, 2], F32)
    yrow = A('c_yrow', [128, 3072], F32)
    ybf = A('c_ybf', [128, 3072], BF16)
    yT = A('c_yT', [128, 24, TB], BF16)
    mT = A('c_mT', [128, KC, TB], BF16)
    acc4 = A('c_acc4', [128, 4, TB], F32)
    sg = A('c_sg', [128, TB], F32)
    tmpm = A('c_tmpm', [128, TB], F32)
    wt = [A('c_wt%d' % i, [128, KC, 512], BF16) for i in range(2)]
    skT = A('c_skT', [128, 16, 128], BF16)
    skl = A('c_skl', [128, 128], F32)
    qTs = A('c_qTs', [128, TB], BF16)
    sc = A('c_sc', [128, 128], F32)
    sc2 = A('c_sc2', [128, 128], F32)
    top = A('c_top', [128, 2, 16, 16], F32)
    topi = A('c_topi', [128, 2, 16, 16], U32)
    topf = A('c_topf', [128, 16, 16], F32)
    cand = A('c_cand', [128, 16, 16], F32)
    cand2 = A('c_cand2', [128, 16, 16], F32)
    best = A('c_best', [128, 8, 16], F32)
    bpos = A('c_bpos', [128, 8, 16], U32)
    bq = A('c_bq', [128, 8, 16], U32)
    af = A('c_af', [128, 8, 16], F32)
    oh = A('c_oh', [128, 8, 16, 16], F32)
    isel = A('c_isel', [128, 2, 8, 16], F32)
    idxf = A('c_idxf', [128, 128], F32)
    idx = A('c_idx', [128, 128], I32)
    gate = A('c_gate', [128, 8, 16], F32)
    gsum = A('c_gsum', [128, 8], F32)
    pre = A('c_pre', [128, 128], F32)
    wgt = A('c_wgt', [128, 128], F32)
    ug = [A('c_ug%d' % i, [128, 2 * D], BF16) for i in range(5)]
    ugk = ['c_ug%d' % i for i in range(5)]
    ug += [yT[:, 0:16, :].rearrange("p a b -> p (a b)"), mT[:].rearrange("p a b -> p (a b)"), uT[:].rearrange("p a b -> p (a b)")]
    ugk += ['c_yT', 'c_mT', 'c_uT']
    NG = len(ug)
    dg = [A('c_dg%d' % i, [128, 128], BF16) for i in range(2)]
    g1 = A('c_g1', [128, 128], F32)

    for m in range(2):
        for h in range(8):
            S.dma('sp', lambda e, m=m, h=h: e.dma_start(out=skl[:], in_=g.w['peer_sub_keys'][m, h, :, :]), writes=['c_skl'])
            S.op('pe', lambda e: e.transpose(out=ps[0][:, 0:128], in_=skl[:], identity=cs['ident_f'][:]), reads=['c_skl', 'c_ident_f'], writes=['ps0'])
            S.op('act', lambda e, m=m, h=h: e.copy(out=skT[:, h * 2 + m, :], in_=ps[0][:, 0:128]), reads=['ps0'], writes=['c_skT'])

    wi = [0]

    def next_w(src, nk, c0_, ncols):
        i = wi[0] % 2
        wi[0] += 1
        load_w(g, wt[i], 'c_wt%d' % i, src, 0, nk, c0_, ncols)
        return wt[i], 'c_wt%d' % i

    pidx = [0]

    def next_ps():
        i = pidx[0] % 4
        pidx[0] += 1
        return ps[i], 'ps%d' % i

    nblk = (NCC + 1) // 2
    gi = 0
    for bi in range(nblk):
        c0 = bi * 2
        ncb = min(2, NCC - c0)
        tb = ncb * 128
        S.dma('sp', lambda e: e.dma_start(out=nwb[:], in_=g.w['mix_norm_w'].partition_broadcast(128)), writes=['c_nwb'])
        for j in range(ncb):
            S.dma('pool', lambda e, j=j: e.indirect_dma_start(out=xres[:, j, :], out_offset=None, in_=g.x,
                                                              in_offset=bass.IndirectOffsetOnAxis(ap=rows[:, c0 + j:c0 + j + 1], axis=0)),
                  reads=['c_rows', g.xkey], writes=['c_xres%d' % j])
            rmsnorm_T(g, 'c_', xres[:, j, :], 'c_xres%d' % j, nwb, 'c_nwb', ub, junk, st, uT, 'c_uT', j * 128)
        for j in range(ncb):
            S.dma('pool', lambda e, j=j: e.indirect_dma_start(out=yrow[:], out_offset=None, in_=g.yall,
                                                              in_offset=bass.IndirectOffsetOnAxis(ap=rows[:, c0 + j:c0 + j + 1], axis=0)),
                  reads=['c_rows', 'd_yall'], writes=['c_yrow'])
            S.op('act', lambda e: e.activation(out=junk[:, 0:1024], in_=yrow[:, 0:1024], func=AF.Square, accum_out=st[:, 0:1]),
                 reads=['c_yrow'], writes=['c_junk', 'c_ss'])
            S.op('dve', lambda e: e.tensor_scalar(out=st[:, 1:2], in0=st[:, 0:1], scalar1=1.0 / 1024, scalar2=EPS, op0=ALU.mult, op1=ALU.add),
                 reads=['c_ss'], writes=['c_rs'])
            S.op('act', lambda e: e.activation(out=st[:, 1:2], in_=st[:, 1:2], func=AF.Sqrt), reads=['c_rs'], writes=['c_rs'])
            S.op('dve', lambda e: e.reciprocal(out=st[:, 1:2], in_=st[:, 1:2]), reads=['c_rs'], writes=['c_rs'])
            S.op('dve', lambda e: e.scalar_tensor_tensor(out=ybf[:, 0:1024], in0=yrow[:, 0:1024], scalar=st[:, 1:2], in1=ssdw[:], op0=ALU.mult, op1=ALU.mult),
                 reads=['c_yrow', 'c_rs', 'c_ssdw'], writes=['c_ybf'])
            S.op('act', lambda e: e.copy(out=ybf[:, 1024:3072], in_=yrow[:, 1024:3072]), reads=['c_yrow', 'c_ybf'], writes=['c_ybf'])
            transpose_to(g, ybf, 'c_ybf', 24, yT, 'c_yT', 0, j * 128)
        for jg in range(4):
            for br, wbn in enumerate(('w_branch_ssd', 'w_branch_mlstm', 'w_branch_diff')):
                wg, wgk = next_w(w_in, KC, OG + br * 2048 + jg * 512, 512)
                wb, wbk = next_w(wbn, 8, jg * 512, 512)
                for ti in range(4):
                    pa, pak = next_ps()
                    for kc in range(KC):
                        S.op('pe', lambda e, kc=kc, pa=pa, wg=wg, ti=ti: e.matmul(pa[:, 0:tb], lhsT=wg[:, kc, ti * 128:(ti + 1) * 128], rhs=uT[:, kc, 0:tb],
                                                                                 start=(kc == 0), stop=(kc == KC - 1)), reads=[wgk, 'c_uT'], writes=[pak])
                    pb_, pbk = next_ps()
                    for kc in range(8):
                        S.op('pe', lambda e, kc=kc, pb_=pb_, wb=wb, ti=ti, br=br: e.matmul(pb_[:, 0:tb], lhsT=wb[:, kc, ti * 128:(ti + 1) * 128], rhs=yT[:, br * 8 + kc, 0:tb],
                                                                                         start=(kc == 0), stop=(kc == 7)), reads=[wbk, 'c_yT'], writes=[pbk])
                    S.op('act', lambda e, pa=pa: e.activation(out=sg[:, 0:tb], in_=pa[:, 0:tb], func=AF.Sigmoid), reads=[pak], writes=['c_sg'])
                    if br == 0:
                        S.op('dve', lambda e, pb_=pb_, ti=ti: e.tensor_tensor(out=acc4[:, ti, 0:tb], in0=sg[:, 0:tb], in1=pb_[:, 0:tb], op=ALU.mult),
                             reads=['c_sg', pbk], writes=['c_acc%d' % ti])
                    else:
                        S.op('dve', lambda e, pb_=pb_: e.tensor_tensor(out=tmpm[:, 0:tb], in0=sg[:, 0:tb], in1=pb_[:, 0:tb], op=ALU.mult),
                             reads=['c_sg', pbk], writes=['c_tmpm'])
                        S.op('dve', lambda e, ti=ti: e.tensor_tensor(out=acc4[:, ti, 0:tb], in0=acc4[:, ti, 0:tb], in1=tmpm[:, 0:tb], op=ALU.add),
                             reads=['c_tmpm', 'c_acc%d' % ti], writes=['c_acc%d' % ti])
            for ti in range(4):
                S.op('act', lambda e, ti=ti, jg=jg: e.copy(out=mT[:, jg * 4 + ti, 0:tb], in_=acc4[:, ti, 0:tb]), reads=['c_acc%d' % ti], writes=['c_mT'])
        for cg in range(4):
            wo, wok = next_w('w_out', KC, cg * 512, 512)
            for j in range(ncb):
                pa, pak = next_ps()
                for kc in range(KC):
                    S.op('pe', lambda e, kc=kc, pa=pa, wo=wo, j=j: e.matmul(pa[:, 0:512], lhsT=mT[:, kc, j * 128:(j + 1) * 128], rhs=wo[:, kc, :],
                                                                           start=(kc == 0), stop=(kc == KC - 1)), reads=[wok, 'c_mT'], writes=[pak])
                S.op('dve', lambda e, pa=pa, j=j, cg=cg: e.tensor_tensor(out=xres[:, j, cg * 512:(cg + 1) * 512], in0=xres[:, j, cg * 512:(cg + 1) * 512],
                                                                        in1=pa[:, 0:512], op=ALU.add), reads=[pak, 'c_xres%d' % j], writes=['c_xres%d' % j])
        if g.dbg:
            for j in range(ncb):
                S.dma('sp', lambda e, j=j: e.dma_start(out=g.dbg_hmix[(c0 + j) * 128:(c0 + j + 1) * 128, :], in_=xres[:, j, :]), reads=['c_xres%d' % j], writes=['d_hmix'])
        S.dma('sp', lambda e: e.dma_start(out=nwb[:], in_=g.w['ffn_norm_w'].partition_broadcast(128)), reads=['c_nwb'], writes=['c_nwb'])
        for j in range(ncb):
            rmsnorm_T(g, 'c_h%d' % j, xres[:, j, :], 'c_xres%d' % j, nwb, 'c_nwb', h2b[:, j, :], junk, st, uT, 'c_uT', j * 128, jkey='c_junk', stkey='c_')
        for cg in range(4):
            wq, wqk = next_w('peer_w_q', KC, cg * 512, 512)
            for ti in range(4):
                hm = cg * 4 + ti
                pa, pak = next_ps()
                for kc in range(KC):
                    S.op('pe', lambda e, kc=kc, pa=pa, wq=wq, ti=ti: e.matmul(pa[:, 0:tb], lhsT=wq[:, kc, ti * 128:(ti + 1) * 128], rhs=uT[:, kc, 0:tb],
                                                                             start=(kc == 0), stop=(kc == KC - 1)), reads=[wqk, 'c_uT'], writes=[pak])
                S.op('act', lambda e, pa=pa: e.copy(out=qTs[:, 0:tb], in_=pa[:, 0:tb]), reads=[pak], writes=['c_qTs'])
                for j in range(ncb):
                    p2, p2k = next_ps()
                    S.op('pe', lambda e, p2=p2, j=j, hm=hm: e.matmul(p2[:, 0:128], lhsT=qTs[:, j * 128:(j + 1) * 128], rhs=skT[:, hm, :], start=True, stop=True),
                         reads=['c_qTs', 'c_skT'], writes=[p2k])
                    S.op('act', lambda e, p2=p2: e.copy(out=sc[:], in_=p2[:, 0:128]), reads=[p2k], writes=['c_sc'])
                    tk = 'c_top%d' % j
                    S.op('dve', lambda e, j=j, hm=hm: e.max(out=top[:, j, hm, 0:8], in_=sc[:]), reads=['c_sc'], writes=[tk])
                    S.op('dve', lambda e, j=j, hm=hm: e.max_index(out=topi[:, j, hm, 0:8], in_max=top[:, j, hm, 0:8], in_values=sc[:]), reads=['c_sc', tk], writes=[tk + 'i'])
                    S.op('dve', lambda e, j=j, hm=hm: e.match_replace(out=sc2[:], in_to_replace=top[:, j, hm, 0:8], in_values=sc[:], imm_value=-1e30),
                         reads=['c_sc', tk], writes=['c_sc2'])
                    S.op('dve', lambda e, j=j, hm=hm: e.max(out=top[:, j, hm, 8:16], in_=sc2[:]), reads=['c_sc2', tk], writes=[tk])
                    S.op('dve', lambda e, j=j, hm=hm: e.max_index(out=topi[:, j, hm, 8:16], in_max=top[:, j, hm, 8:16], in_values=sc2[:]), reads=['c_sc2', tk, tk + 'i'], writes=[tk + 'i'])
        for j in range(ncb):
            tk = 'c_top%d' % j
            S.op('dve', lambda e, j=j: e.tensor_copy(out=topf[:], in_=topi[:, j, :, :]), reads=[tk + 'i'], writes=['c_topf'])
            tv = top[:, j, :, :].rearrange("p (h m) k -> p h m k", m=2)
            for h in range(8):
                S.op('dve', lambda e, h=h, tv=tv: e.tensor_tensor(out=cand[:], in0=tv[:, h, 0, :].unsqueeze(2).to_broadcast([128, 16, 16]),
                                                                 in1=tv[:, h, 1, :].unsqueeze(1).to_broadcast([128, 16, 16]), op=ALU.add), reads=[tk], writes=['c_cand'])
                cf = cand[:].rearrange("p a b -> p (a b)")
                cf2 = cand2[:].rearrange("p a b -> p (a b)")
                S.op('dve', lambda e, h=h: e.max(out=best[:, h, 0:8], in_=cf), reads=['c_cand'], writes=['c_best'])
                S.op('dve', lambda e, h=h: e.max_index(out=bpos[:, h, 0:8], in_max=best[:, h, 0:8], in_values=cf), reads=['c_cand', 'c_best'], writes=['c_bpos'])
                S.op('dve', lambda e, h=h: e.match_replace(out=cf2, in_to_replace=best[:, h, 0:8], in_values=cf, imm_value=-1e30), reads=['c_cand', 'c_best'], writes=['c_cand2'])
                S.op('dve', lambda e, h=h: e.max(out=best[:, h, 8:16], in_=cf2), reads=['c_cand2', 'c_best'], writes=['c_best'])
                S.op('dve', lambda e, h=h: e.max_index(out=bpos[:, h, 8:16], in_max=best[:, h, 8:16], in_values=cf2), reads=['c_cand2', 'c_best', 'c_bpos'], writes=['c_bpos'])
            tf = topf[:].rearrange("p (h m) k -> p h m k", m=2)
            for m_, (opn, sval) in enumerate(((ALU.logical_shift_right, 4), (ALU.bitwise_and, 15))):
                S.op('dve', lambda e, opn=opn, sval=sval: e.tensor_single_scalar(out=bq[:], in_=bpos[:], scalar=sval, op=opn), reads=['c_bpos'], writes=['c_bq'])
                S.op('dve', lambda e: e.tensor_copy(out=af[:], in_=bq[:]), reads=['c_bq'], writes=['c_af'])
                S.op('dve', lambda e: e.tensor_tensor(out=oh[:], in0=af[:].unsqueeze(3).to_broadcast([128, 8, 16, 16]),
                                                       in1=cs['iota16'][:].unsqueeze(1).unsqueeze(1).to_broadcast([128, 8, 16, 16]), op=ALU.is_equal),
                     reads=['c_af', 'c_iota16'], writes=['c_oh'])
                S.op('dve', lambda e, m_=m_: e.tensor_tensor(out=oh[:], in0=oh[:], in1=tf[:, :, m_, :].unsqueeze(2).to_broadcast([128, 8, 16, 16]), op=ALU.mult),
                     reads=['c_oh', 'c_topf'], writes=['c_oh'])
                S.op('dve', lambda e, m_=m_: e.tensor_reduce(out=isel[:, m_, :, :], in_=oh[:], axis=AX.X, op=ALU.add), reads=['c_oh'], writes=['c_isel%d' % m_])
            S.op('dve', lambda e: e.scalar_tensor_tensor(out=idxf[:], in0=isel[:, 0, :, :].rearrange("p a b -> p (a b)"), scalar=128.0,
                                                          in1=isel[:, 1, :, :].rearrange("p a b -> p (a b)"), op0=ALU.mult, op1=ALU.add),
                 reads=['c_isel0', 'c_isel1'], writes=['c_idxf'])
            S.op('dve', lambda e: e.tensor_copy(out=idx[:], in_=idxf[:]), reads=['c_idxf'], writes=['c_idx'])
            S.op('dve', lambda e: e.tensor_tensor(out=gate[:], in0=best[:], in1=best[:, :, 0:1].to_broadcast([128, 8, 16]), op=ALU.subtract), reads=['c_best'], writes=['c_gate'])
            S.op('act', lambda e: e.activation(out=gate[:], in_=gate[:], func=AF.Exp), reads=['c_gate'], writes=['c_gate'])
            S.op('dve', lambda e: e.tensor_reduce(out=gsum[:], in_=gate[:], axis=AX.X, op=ALU.add), reads=['c_gate'], writes=['c_gsum'])
            S.op('dve', lambda e: e.reciprocal(out=gsum[:], in_=gsum[:]), reads=['c_gsum'], writes=['c_gsum'])
            S.op('dve', lambda e: e.tensor_tensor(out=gate[:], in0=gate[:], in1=gsum[:].unsqueeze(2).to_broadcast([128, 8, 16]), op=ALU.mult), reads=['c_gate', 'c_gsum'], writes=['c_gate'])
            gk = 'c_gate'

            def emit_v(sl, b_):
                d_ = sl % 2
                S.op('dve', lambda e: e.tensor_scalar(out=dg[d_][:], in0=cs['ident_bf'][:], scalar1=g1[:, sl:sl + 1], scalar2=gate[:].rearrange("p a b -> p (a b)")[:, sl:sl + 1],
                                                       op0=ALU.mult, op1=ALU.mult), reads=['c_ident_bf', 'c_g1_%d' % (sl % 4), gk], writes=['c_dg%d' % d_])
                for q in range(4):
                    S.op('pe', lambda e, q=q: e.matmul(ps[q][:, 0:512], lhsT=dg[d_][:], rhs=ug[b_][:, D + q * 512:D + (q + 1) * 512], start=(sl == 0), stop=(sl == 127)),
                         reads=['c_dg%d' % d_, ugk[b_]], writes=['ps%d' % q])

            prev = None
            for sl in range(128):
                b_ = gi % NG
                gi += 1
                S.dma('pool', lambda e, b_=b_, sl=sl: e.indirect_dma_start(out=ug[b_][:, :], out_offset=None, in_=g.uvb,
                                                                           in_offset=bass.IndirectOffsetOnAxis(ap=idx[:, sl:sl + 1], axis=0)),
                      reads=['c_idx', 'd_uvb_%d' % g.lid], writes=[ugk[b_]])
                S.op('dve', lambda e, b_=b_, sl=sl, j=j: e.scalar_tensor_tensor(out=junk[:], in0=ug[b_][:, 0:D], scalar=1.0, in1=h2b[:, j, :], op0=ALU.mult, op1=ALU.mult,
                                                                                accum_out=pre[:, sl:sl + 1]),
                     reads=[ugk[b_], 'c_h%dub' % j], writes=['c_junk', 'c_pre_%d' % (sl % 4)])
                S.op('act', lambda e, sl=sl: e.activation(out=g1[:, sl:sl + 1], in_=pre[:, sl:sl + 1], func=AF.Gelu), reads=['c_pre_%d' % (sl % 4)], writes=['c_g1_%d' % (sl % 4)])
                if prev is not None:
                    emit_v(*prev)
                prev = (sl, b_)
            emit_v(*prev)
            for q in range(4):
                S.op('dve', lambda e, q=q, j=j: e.tensor_tensor(out=xres[:, j, q * 512:(q + 1) * 512], in0=xres[:, j, q * 512:(q + 1) * 512], in1=ps[q][:, 0:512], op=ALU.add),
                     reads=['ps%d' % q, 'c_xres%d' % j], writes=['c_xres%d' % j])
            if g.final:
                S.dma('sp', lambda e: e.dma_start(out=nwb[:], in_=g.fnw.partition_broadcast(128)), reads=['c_nwb'], writes=['c_nwb'])
                S.op('act', lambda e, j=j: e.activation(out=junk[:], in_=xres[:, j, :], func=AF.Square, accum_out=st[:, 0:1]), reads=['c_xres%d' % j], writes=['c_junk', 'c_ss'])
                S.op('dve', lambda e: e.tensor_scalar(out=st[:, 1:2], in0=st[:, 0:1], scalar1=1.0 / D, scalar2=EPS, op0=ALU.mult, op1=ALU.add), reads=['c_ss'], writes=['c_rs'])
                S.op('act', lambda e: e.activation(out=st[:, 1:2], in_=st[:, 1:2], func=AF.Sqrt), reads=['c_rs'], writes=['c_rs'])
                S.op('dve', lambda e: e.reciprocal(out=st[:, 1:2], in_=st[:, 1:2]), reads=['c_rs'], writes=['c_rs'])
                S.op('dve', lambda e, j=j: e.scalar_tensor_tensor(out=xres[:, j, :], in0=xres[:, j, :], scalar=st[:, 1:2], in1=nwb[:], op0=ALU.mult, op1=ALU.mult),
                     reads=['c_xres%d' % j, 'c_rs', 'c_nwb'], writes=['c_xres%d' % j])
            else:
                S.op('dve', lambda e, j=j: e.tensor_scalar(out=xres[:, j, :], in0=xres[:, j, :], scalar1=rmask[:, c0 + j:c0 + j + 1], scalar2=None, op0=ALU.mult),
                     reads=['c_xres%d' % j, 'c_rmask'], writes=['c_xres%d' % j])
            S.dma('sp', lambda e, j=j: e.dma_start(out=g.out[(c0 + j) * 128:(c0 + j + 1) * 128, :], in_=xres[:, j, :]), reads=['c_xres%d' % j], writes=[g.outkey])


def make_in_maps(inp, layers, NCH, NCC, names=None):
    T = NCH * 128
    consts = host_consts()
    maps = []
    for core in range(8):
        b, s = core // 2, core % 2
        m = {}
        xp = np.zeros((T, D), np.float32)
        xp[PADN:PADN + META] = inp['meta_tokens']
        xp[128:] = inp['x'][b, :T - 128]
        m['xpad'] = xp
        pp = np.zeros((T,), np.int32)
        pp[PADN:PADN + META] = np.arange(META, dtype=np.int32) - META
        pp[128:] = inp['positions'][b, :T - 128]
        m['pos'] = np.ascontiguousarray(pp.reshape(NCH, 128).T)
        c0 = 0 if s == 0 else NCH - NCC
        rows = ((c0 + np.arange(NCC))[None, :] * 128 + np.arange(128)[:, None]).astype(np.int32)
        m['rows'] = np.ascontiguousarray(rows)
        m['rowmask'] = np.ascontiguousarray((rows >= PADN).astype(np.float32))
        rows_f = (np.arange(NCH)[None, :] * 128 + np.arange(128)[:, None]).astype(np.int32)
        m['rows_full'] = np.ascontiguousarray(rows_f)
        m['rowmask_full'] = np.ascontiguousarray((rows_f >= PADN).astype(np.float32))
        for l in layers:
            for n in W_SPECS:
                if names is None or (n, l) in names:
                    m['%s_l%d' % (n, l)] = np.ascontiguousarray(inp[n][l])
        m['final_norm_w'] = np.ascontiguousarray(inp['final_norm_w'])
        m.update(consts)
        maps.append(m)
    return maps


def kernel(**inputs):
    inp = {k: np.asarray(v) for k, v in inputs.items()}
    B, SEQ = inp['x'].shape[0], inp['x'].shape[1]
    NCH = (SEQ + 128) // 128
    NCC = (NCH + 1) // 2
    T = NCH * 128
    layers = [0, 1]
    nc, g = build(layers, NCH, NCC)
    in_maps = make_in_maps(inp, layers, NCH, NCC, names=g.in_names)
    res = run_bass_kernel_spmd(nc, in_maps, core_ids=list(range(8)))
    full = np.zeros((B, T, D), np.float32)
    for core in range(8):
        b, s = core // 2, core % 2
        c0 = 0 if s == 0 else NCH - NCC
        full[b, c0 * 128:(c0 + NCC) * 128] = np.asarray(res.results[core]['xout'])
    return np.ascontiguousarray(full[:, 128:, :])
```
